# Optimizing a Trainium2 kernel written in Bass

```python
import math
import jax, jax.numpy as jnp
from jax import lax
import numpy as np

D_MODEL = 2048
BATCH = 4
SEQ = 4096
DEPTH = 2

CTX_LEN = 256
GRID_W = 64
MIX_WIDTH = D_MODEL
F_GROUPS = 4
F_WIDTH = MIX_WIDTH // 4
F_GDIM = F_WIDTH // F_GROUPS
POOL_WINDOWS = (2, 4, 8, 16)
P_GROUPS = len(POOL_WINDOWS)
P_WIDTH = MIX_WIDTH // 4
P_GDIM = P_WIDTH // P_GROUPS
RET_WIDTH = MIX_WIDTH // 2
RET_HEADS = 8
RET_DK = RET_WIDTH // RET_HEADS
RET_CHUNK = 128
N_BRANCH = 3
ROPE_BASE = 10000.0
EPS = 1e-6

F_X_OFF = 0
F_G_OFF = F_X_OFF + F_WIDTH
P_X_OFF = F_G_OFF + F_WIDTH
P_G_OFF = P_X_OFF + P_WIDTH
R_Q_OFF = P_G_OFF + P_WIDTH
R_K_OFF = R_Q_OFF + RET_WIDTH
R_V_OFF = R_K_OFF + RET_WIDTH
R_G_OFF = R_V_OFF + RET_WIDTH
MG_OFF = R_G_OFF + RET_WIDTH
IN_WIDTH = MG_OFF + N_BRANCH * D_MODEL

kernel_name = "hybrid_fourier_pool_retention_dit"


def rms_norm(x, g):
    xf = x.astype(jnp.float32)
    y = xf * lax.rsqrt(jnp.mean(xf * xf, axis=-1, keepdims=True) + EPS)
    return (y * g.astype(jnp.float32)).astype(x.dtype)


def head_rms(o):
    return o * lax.rsqrt(jnp.mean(o * o, axis=-1, keepdims=True) + EPS)


def to_heads(u):
    B, N, _ = u.shape
    return u.reshape(B, N, RET_HEADS, RET_DK).transpose(0, 2, 1, 3).astype(jnp.float32)


def rope_axis(x, pos):
    nf = x.shape[-1] // 2
    inv = ROPE_BASE ** (-jnp.arange(nf, dtype=jnp.float32) / nf)
    ang = pos.astype(jnp.float32)[:, None] * inv[None, :]
    cos, sin = jnp.cos(ang), jnp.sin(ang)
    x1, x2 = x[..., :nf], x[..., nf:]
    return jnp.concatenate([x1 * cos - x2 * sin, x1 * sin + x2 * cos], axis=-1)


def rope_2d(x, rows, cols):
    h = x.shape[-1] // 2
    return jnp.concatenate([rope_axis(x[..., :h], rows), rope_axis(x[..., h:], cols)], axis=-1)


def fourier_mix(u, w_fourier):
    B, N, _ = u.shape
    ug = u.astype(jnp.float32).reshape(B, N, F_GROUPS, F_GDIM)
    mixed = jnp.fft.fft2(ug, axes=(1, 3), norm="ortho").real
    y = jnp.einsum("bngc,gcd->bngd", mixed, w_fourier.astype(jnp.float32))
    return y.reshape(B, N, F_WIDTH).astype(u.dtype)


def pool_mix(u, w_pool, pool_scale):
    B, N, _ = u.shape
    uf = u.astype(jnp.float32).reshape(B, N, P_GROUPS, P_GDIM)
    cs = jnp.concatenate([jnp.zeros((B, 1, P_GROUPS, P_GDIM), jnp.float32), jnp.cumsum(uf, axis=1)], axis=1)
    t = jnp.arange(N)
    outs = []
    for g, w in enumerate(POOL_WINDOWS):
        lo = jnp.clip(t - w // 2, 0, N)
        hi = jnp.clip(t + w // 2, 0, N)
        cs_g = cs[:, :, g]
        cnt = (hi - lo).astype(jnp.float32)[None, :, None]
        outs.append((cs_g[:, hi] - cs_g[:, lo]) / cnt - uf[:, :, g])
    pooled = jnp.stack(outs, axis=2)
    y = jnp.einsum("bngc,gcd->bngd", pooled, w_pool.astype(jnp.float32)).reshape(B, N, P_WIDTH)
    return (y * pool_scale.astype(jnp.float32)).astype(u.dtype)


def retention_scan(q, k, v, log_gamma, s0):
    B, H, N, d = q.shape
    C = RET_CHUNK
    nc = N // C
    qc = q.reshape(B, H, nc, C, d)
    kc = k.reshape(B, H, nc, C, d)
    vc = v.reshape(B, H, nc, C, d)
    i = jnp.arange(C, dtype=jnp.float32)
    lg = log_gamma[:, None]
    diff = i[:, None] - i[None, :]
    mask = jnp.where(diff >= 0, jnp.exp(lg[:, :, None] * jnp.maximum(diff, 0.0)), 0.0)
    scores = jnp.einsum("bhnid,bhnjd->bhnij", qc, kc) * mask[None, :, None]
    o_intra = jnp.einsum("bhnij,bhnje->bhnie", scores, vc)
    k_dec = kc * jnp.exp(lg * (C - 1 - i))[None, :, None, :, None]
    kv = jnp.einsum("bhnjd,bhnje->nbhde", k_dec, vc)
    chunk_decay = jnp.exp(log_gamma * C)[None, :, None, None]

    def step(s, kv_n):
        return chunk_decay * s + kv_n, s

    s_final, s_prev = lax.scan(step, s0, kv)
    q_dec = qc * jnp.exp(lg * (i + 1))[None, :, None, :, None]
    o_cross = jnp.einsum("bhnid,nbhde->bhnie", q_dec, s_prev)
    return (o_intra + o_cross).reshape(B, H, N, d), s_final


def retention_final_state(k, v, log_gamma):
    N = k.shape[2]
    w = jnp.exp(log_gamma[:, None] * (N - 1 - jnp.arange(N, dtype=jnp.float32))[None, :])
    return jnp.einsum("bhnd,hn,bhne->bhde", k, w, v)


def hybrid_mixer(h, w_in, w_fourier, w_pool, pool_scale, log_gamma,
                 w_up_fourier, w_up_pool, w_up_ret, w_out, pos, s0_fwd, s0_bwd):
    B, N, _ = h.shape
    proj = h @ w_in
    f_x, f_g = proj[..., F_X_OFF:F_G_OFF], proj[..., F_G_OFF:P_X_OFF]
    p_x, p_g = proj[..., P_X_OFF:P_G_OFF], proj[..., P_G_OFF:R_Q_OFF]
    q = to_heads(proj[..., R_Q_OFF:R_K_OFF]) * (RET_DK ** -0.5)
    k = to_heads(proj[..., R_K_OFF:R_V_OFF])
    v = to_heads(proj[..., R_V_OFF:R_G_OFF])
    r_g = proj[..., R_G_OFF:MG_OFF]
    gate_logits = proj[..., MG_OFF:].reshape(B, N, N_BRANCH, D_MODEL)
    if pos is not None:
        q = rope_2d(q, pos[0], pos[1])
        k = rope_2d(k, pos[0], pos[1])
    y_f = (fourier_mix(f_x, w_fourier) * jax.nn.silu(f_g)) @ w_up_fourier
    y_p = (pool_mix(p_x, w_pool, pool_scale) * jax.nn.silu(p_g)) @ w_up_pool
    o_f, s_f = retention_scan(q, k, v, log_gamma[0], s0_fwd)
    o_b, s_b = retention_scan(jnp.flip(q, 2), jnp.flip(k, 2), jnp.flip(v, 2), log_gamma[1], s0_bwd)
    o = head_rms(o_f + jnp.flip(o_b, 2)).transpose(0, 2, 1, 3).reshape(B, N, RET_WIDTH).astype(h.dtype)
    y_r = (o * jax.nn.silu(r_g)) @ w_up_ret
    g = jax.nn.sigmoid(gate_logits.astype(jnp.float32)).astype(h.dtype)
    merged = g[:, :, 0] * y_f + g[:, :, 1] * y_p + g[:, :, 2] * y_r
    return (merged @ w_out).astype(h.dtype), s_f, s_b


def setup_inputs(seed: int = 0) -> dict:
    key = jax.random.key(seed)
    ks = jax.random.split(key, 20)
    f32 = jnp.float32
    D = D_MODEL

    def nrm(k, shape, scale):
        return jax.random.normal(k, shape, f32) * scale

    base_decay = (5.0 + jnp.arange(RET_HEADS, dtype=f32)) * math.log(2.0)
    return {
        "x": nrm(ks[0], (BATCH, SEQ, D), 1.0),
        "c": nrm(ks[1], (BATCH, D), 1.0),
        "ctx": nrm(ks[2], (BATCH, CTX_LEN, D), 1.0),
        "c_ctx": nrm(ks[3], (D,), 1.0),
        "w_ada": nrm(ks[4], (DEPTH, D, 3 * D), D ** -0.5),
        "b_ada": nrm(ks[5], (DEPTH, 3 * D), 0.01),
        "norm_g": 1.0 + nrm(ks[6], (DEPTH, D), 0.02),
        "w_in": nrm(ks[7], (DEPTH, D, IN_WIDTH), D ** -0.5),
        "w_fourier": nrm(ks[8], (DEPTH, F_GROUPS, F_GDIM, F_GDIM), F_GDIM ** -0.5),
        "w_pool": nrm(ks[9], (DEPTH, P_GROUPS, P_GDIM, P_GDIM), P_GDIM ** -0.5),
        "pool_scale": 1.0 + nrm(ks[10], (DEPTH, P_WIDTH), 0.02),
        "ret_decay_logit": base_decay[None, None, :] + nrm(ks[11], (DEPTH, 2, RET_HEADS), 0.1),
        "w_up_fourier": nrm(ks[12], (DEPTH, F_WIDTH, D), F_WIDTH ** -0.5),
        "w_up_pool": nrm(ks[13], (DEPTH, P_WIDTH, D), P_WIDTH ** -0.5),
        "w_up_ret": nrm(ks[14], (DEPTH, RET_WIDTH, D), RET_WIDTH ** -0.5),
        "w_out": nrm(ks[15], (DEPTH, D, D), D ** -0.5),
        "final_norm_g": 1.0 + nrm(ks[16], (D,), 0.02),
    }


def reference(x, c, ctx, c_ctx, w_ada, b_ada, norm_g, w_in, w_fourier, w_pool, pool_scale,
              ret_decay_logit, w_up_fourier, w_up_pool, w_up_ret, w_out, final_norm_g):
    B, N, _ = x.shape
    ROWS = N // GRID_W
    grid_r, grid_c = jnp.meshgrid(jnp.arange(ROWS), jnp.arange(GRID_W), indexing="ij")
    pos = (grid_r.reshape(-1), grid_c.reshape(-1))
    silu_c = jax.nn.silu(c)
    silu_cc = jax.nn.silu(c_ctx)
    s_zero = jnp.zeros((ctx.shape[0], RET_HEADS, RET_DK, RET_DK), jnp.float32)
    for l in range(DEPTH):
        last = l == DEPTH - 1
        mod = silu_c @ w_ada[l] + b_ada[l]
        shift, scale, gate = jnp.split(mod[:, None, :], 3, axis=-1)
        mod_c = silu_cc @ w_ada[l] + b_ada[l]
        shift_c, scale_c, gate_c = jnp.split(mod_c, 3, axis=-1)
        log_gamma = jax.nn.log_sigmoid(ret_decay_logit[l].astype(jnp.float32))
        h = rms_norm(x, norm_g[l]) * (1.0 + scale) + shift
        hc = rms_norm(ctx, norm_g[l]) * (1.0 + scale_c) + shift_c
        if last:
            kv_c = hc @ w_in[l][:, R_K_OFF:R_G_OFF]
            k_c = to_heads(kv_c[..., :RET_WIDTH])
            v_c = to_heads(kv_c[..., RET_WIDTH:])
            s_f = retention_final_state(k_c, v_c, log_gamma[0])
            s_b = retention_final_state(jnp.flip(k_c, 2), jnp.flip(v_c, 2), log_gamma[1])
        else:
            y_c, s_f, s_b = hybrid_mixer(hc, w_in[l], w_fourier[l], w_pool[l], pool_scale[l], log_gamma,
                                         w_up_fourier[l], w_up_pool[l], w_up_ret[l], w_out[l],
                                         None, s_zero, s_zero)
        y, _, _ = hybrid_mixer(h, w_in[l], w_fourier[l], w_pool[l], pool_scale[l], log_gamma,
                               w_up_fourier[l], w_up_pool[l], w_up_ret[l], w_out[l],
                               pos, s_f, s_b)
        x = x + gate * y
        if not last:
            ctx = ctx + gate_c * y_c
    return rms_norm(x, final_norm_g)
```

```python
import math
from contextlib import ExitStack

import ml_dtypes
import numpy as np

import concourse.bass as bass
import concourse.mybir as mybir
from concourse.bass_utils import run_bass_kernel_spmd

F32 = mybir.dt.float32
BF16 = mybir.dt.bfloat16
AF = mybir.ActivationFunctionType
ALU = mybir.AluOpType

EPOCH = 30000
RING = 8

D = 2048
NX = 4096
NH = 2048
NCX = 256
DEPTH = 2
INW = 12288
F_X, F_G, P_X, P_G, R_Q, R_K, R_V, R_G, MG = 0, 512, 1024, 1536, 2048, 3072, 4096, 5120, 6144
EPS = 1e-6
QS = 128.0 ** -0.5
NCORES = 8
PAIRS = [[0, 1], [2, 3], [4, 5], [6, 7]]

DEBUG = {}
PHASE_LIMIT = 1000
R_STAGE = 99
R_HEADS = 8


class Prog:
    ENGS = ['pe', 'act', 'dve', 'pool', 'sp']

    def __init__(self, nc):
        self.nc = nc
        self.gstack = ExitStack()
        self.pstack = ExitStack()
        self.sem_cache = {}
        self.tick = {e: 0 for e in self.ENGS}
        self.dcount = {e: 0 for e in self.ENGS}
        self.ring_last = {}
        self.last_tick = {}
        self._reset()
        self.nuniq = 0
        self.ncc = 0
        self._caps = []

    def _reset(self):
        self.ops = []
        self.last_w = {}
        self.readers = {}

    def sem(self, name):
        if name not in self.sem_cache:
            self.sem_cache[name] = self.gstack.enter_context(self.nc.semaphore(name))
        return self.sem_cache[name]

    def sbp(self, name, shape, dt):
        return self.gstack.enter_context(self.nc.sbuf_tensor("sb_" + name, list(shape), dt))

    def sb(self, name, shape, dt):
        self.nuniq += 1
        return self.pstack.enter_context(self.nc.sbuf_tensor("%s_%d" % (name, self.nuniq), list(shape), dt))

    def ps(self, name, shape, dt=F32):
        self.nuniq += 1
        return self.pstack.enter_context(self.nc.psum_tensor("%s_%d" % (name, self.nuniq), list(shape), dt))

    def dram(self, name, shape, dt):
        return self.nc.dram_tensor(name, list(shape), dt, kind="Internal").ap()

    def cc(self, kind, ins, outs, groups, r=(), w=()):
        name = 'cc_%d' % self.ncc
        self.ncc += 1
        self.op('pool', lambda e: e.collective_compute(kind, ALU.bypass, replica_groups=groups, ins=ins, outs=outs), r, w, False, (), name)
        return name

    def capture(self):
        self._caps.append([])

    def end_capture(self):
        return self._caps.pop()

    def replay(self, *lists):
        pos = [0] * len(lists)
        total = sum(len(l) for l in lists)
        for _ in range(total):
            best, bf = None, None
            for k, l in enumerate(lists):
                if pos[k] < len(l):
                    f = (pos[k] + 0.5) / len(l)
                    if bf is None or f < bf:
                        best, bf = k, f
            a = lists[best][pos[best]]
            pos[best] += 1
            self.op(*a)

    def op(self, eng, fn, r=(), w=(), dma=False, extra=(), cc=None):
        if self._caps:
            self._caps[-1].append((eng, fn, tuple(r), tuple(w), dma, tuple(extra), cc))
            return None
        i = len(self.ops)
        deps = set()
        for k in r:
            lw = self.last_w.get(k)
            if lw is not None:
                deps.add(lw)
        for k in w:
            lw = self.last_w.get(k)
            if lw is not None:
                deps.add(lw)
            for j in self.readers.get(k, ()):
                deps.add(j)
        deps.discard(i)
        self.ops.append(dict(eng=eng, fn=fn, dma=dma, deps=deps, r=tuple(r), w=tuple(w), extra=tuple(extra)))
        if cc is not None:
            self.ops[-1]['cc'] = cc
        for k in w:
            self.last_w[k] = i
            self.readers[k] = []
        for k in r:
            self.readers.setdefault(k, []).append(i)
        return i

    def pe(self, fn, r=(), w=()):
        return self.op('pe', fn, r, w)

    def act(self, fn, r=(), w=()):
        return self.op('act', fn, r, w)

    def dve(self, fn, r=(), w=()):
        return self.op('dve', fn, r, w)

    def pool(self, fn, r=(), w=()):
        return self.op('pool', fn, r, w)

    def dma(self, out, in_, r=(), w=(), q='sp', extra=()):
        return self.op(q, lambda e: e.dma_start(out=out, in_=in_), r, w, dma=True, extra=extra)

    def flush(self):
        nc = self.nc
        ops = self.ops
        n = len(ops)
        for i, o in enumerate(ops):
            keep = set()
            for j in o['deps']:
                p = ops[j]
                same = (p['eng'] == o['eng']) and (not p['dma']) and (not o['dma'])
                if same:
                    if o['eng'] == 'pe':
                        continue
                    if not (set(p['w']) & (set(o['r']) | set(o['w']))):
                        continue
                keep.add(j)
            o['deps'] = keep
        needed = [False] * n
        for o in ops:
            for j in o['deps']:
                needed[j] = True
        lastc = {}
        for i, o in enumerate(ops):
            if not o['dma'] and 'cc' not in o:
                lastc[o['eng']] = i
        for e, i in lastc.items():
            needed[i] = True
        ring_prev = {}
        for i, o in enumerate(ops):
            e = o['eng']
            if 'cc' in o:
                o['sig'] = (o['cc'], 1)
            elif o['dma']:
                k = self.dcount[e]
                self.dcount[e] += 1
                slot = k % RING
                o['sig'] = ('d_%s_%d' % (e, slot), 16 * (k // RING + 1))
                o['ring_wait'] = self.ring_last.get((e, slot))
                self.ring_last[(e, slot)] = o['sig']
            elif needed[i]:
                t = self.tick[e]
                self.tick[e] += 1
                o['sig'] = ('c_%s_%d' % (e, t // EPOCH), t % EPOCH + 1)
                self.last_tick[e] = o['sig']
            else:
                o['sig'] = None
        for o in ops:
            if o['sig'] is not None:
                self.sem(o['sig'][0])
        barrier = list(self.last_tick.values()) + list(self.ring_last.values())

        def run_engine(e, eh):
            waited = {}

            def wait(sn, v):
                if waited.get(sn, 0) >= v:
                    return
                eh.wait_ge(self.sem(sn), v)
                waited[sn] = v

            for i, o in enumerate(ops):
                if o['eng'] != e:
                    continue
                need = {}
                for j in o['deps']:
                    s = ops[j]['sig']
                    if need.get(s[0], 0) < s[1]:
                        need[s[0]] = s[1]
                if o['dma'] and o['ring_wait'] is not None:
                    s = o['ring_wait']
                    if need.get(s[0], 0) < s[1]:
                        need[s[0]] = s[1]
                for s in o['extra']:
                    if need.get(s[0], 0) < s[1]:
                        need[s[0]] = s[1]
                for sn, v in need.items():
                    wait(sn, v)
                ins = o['fn'](eh)
                if 'cc' in o:
                    ins.then_inc(self.sem(o['sig'][0]))
                elif o['sig'] is not None:
                    ins.then_inc(self.sem(o['sig'][0]), 16 if o['dma'] else 1)
            for sn, v in barrier:
                wait(sn, v)

        with nc.Block() as block:
            @block.sync
            def _(eh):
                run_engine('sp', eh)

            @block.tensor
            def _(eh):
                run_engine('pe', eh)

            @block.vector
            def _(eh):
                run_engine('dve', eh)

            @block.scalar
            def _(eh):
                run_engine('act', eh)

            @block.gpsimd
            def _(eh):
                run_engine('pool', eh)
        self.pstack.close()
        self.pstack = ExitStack()
        self._reset()

    def finish(self):
        self.gstack.close()


def bc(ap1d, n=128):
    return ap1d.partition_broadcast(n)


_CONST = {}


def _consts():
    if _CONST:
        return _CONST
    bf = ml_dtypes.bfloat16
    j = np.arange(128)
    diff = j[None, :] - j[:, None]
    cf = np.zeros((128, 704), np.float32)
    cf[:, 0:128] = np.maximum(diff, 0)
    cf[:, 128:256] = np.maximum(-diff, 0)
    cf[:, 256:384] = np.eye(128) * QS
    cf[:, 384:512] = (j + 1)[None, :]
    cf[:, 512:640] = (128 - j)[None, :]
    jf = np.zeros((128, 32), np.float32)
    for l in range(2):
        jf[:, l * 16:l * 16 + 8] = (127 - j)[:, None]
        jf[:, l * 16 + 8:l * 16 + 16] = j[:, None]
    cf[:, 640:672] = jf
    cchunk = np.arange(16)
    cf[:, 672:688] = (NH - 1) - (128 * cchunk[None, :] + j[:, None])
    cf[:, 688:704] = 128 * cchunk[None, :] + j[:, None]
    _CONST['cf'] = cf
    cb = np.zeros((128, 512), np.float32)
    cb[:, 0:128] = np.eye(128)
    perm = np.zeros((128, 128), np.float32)
    for m in range(128):
        partner = m + 32 if (m % 64) < 32 else m - 32
        perm[partner, m] = 1.0
    cb[:, 128:256] = perm
    ang = 2 * np.pi * np.outer(j, j) / 128.0
    cb[:, 256:384] = np.cos(ang)
    cb[:, 384:512] = np.sin(ang)
    _CONST['cb'] = cb.astype(bf)
    n = np.arange(NX)
    rows, cols = n // 64, n % 64
    inv = 10000.0 ** (-np.arange(32, dtype=np.float64) / 32)
    rc = np.zeros((128, NX), np.float64)
    rs = np.zeros((128, NX), np.float64)
    for d in range(128):
        pos = rows if d < 64 else cols
        a = pos * inv[d % 32]
        rc[d] = np.cos(a)
        rs[d] = np.sin(a) * (-1.0 if (d % 64) < 32 else 1.0)
    _CONST['ropec'] = rc.astype(np.float32)
    _CONST['ropes'] = rs.astype(np.float32)
    for nm, N in (('x', NX), ('c', NCX)):
        t = np.arange(N)
        ic = np.zeros((4, N), np.float32)
        for g, w in enumerate((2, 4, 8, 16)):
            lo = np.clip(t - w // 2, 0, N)
            hi = np.clip(t + w // 2, 0, N)
            ic[g] = 1.0 / (hi - lo)
        _CONST['invc_' + nm] = ic
        k = np.arange(N, dtype=np.int64)
        m = (np.outer(k, k) % N).astype(np.float64)
        a = 2 * np.pi * m / N
        sc = 1.0 / math.sqrt(N * 128.0)
        _CONST['dftc_' + nm] = (np.cos(a) * sc).astype(bf)
        _CONST['dfts_' + nm] = (-np.sin(a) * sc).astype(bf)
    return _CONST


class Seq:
    def __init__(self, name, N, who, rope):
        self.name, self.N, self.who, self.rope = name, N, who, rope


def build():
    nc = bass.Bass("TRN2", target_bir_lowering=False)
    P = Prog(nc)

    def inp(name, shape, dt=F32):
        return nc.dram_tensor(name, list(shape), dt, kind="ExternalInput").ap()

    x_in = inp("x", [NH, D])
    ctx_in = inp("ctx", [NCX, D])
    cT_in = inp("cT", [128, 16, 2])
    wada_in = inp("w_ada", [DEPTH, 6, 128, 16, 512])
    bada_in = inp("b_ada", [DEPTH * 3072])
    ng_in = inp("norm_g", [DEPTH, D])
    win_in = inp("w_in", [DEPTH, 24, 128, 16, 512])
    wf_in = inp("w_fourier", [DEPTH, 128, 4, 128])
    wp_in = inp("w_pool", [DEPTH, 128, 4, 128])
    psc_in = inp("pool_scale", [DEPTH, 128, 4])
    rdl_in = inp("ret_decay_logit", [32])
    wup_in = inp("w_up", [DEPTH, 16, 128, 16, 128])
    wout_in = inp("w_out", [DEPTH, 4, 128, 16, 512])
    fng_in = inp("final_norm_g", [D])
    cf_in = inp("cf", [128, 704])
    cb_in = inp("cb", [128, 512], BF16)
    ropec_in = inp("ropec", [128, NH])
    ropes_in = inp("ropes", [128, NH])
    role_in = inp("role", [128, 4])
    invc_in = {'x': inp("invc_x", [4, NH]), 'c': inp("invc_c", [4, NCX])}
    dftc_in = {'x': inp("dftc_x", [NX, NH], BF16), 'c': inp("dftc_c", [NCX, NCX], BF16)}
    dfts_in = {'x': inp("dfts_x", [NX, NH], BF16), 'c': inp("dfts_c", [NCX, NCX], BF16)}
    out = nc.dram_tensor("out", [NH, D], F32, kind="ExternalOutput").ap()

    modrows = P.dram("modrows", [DEPTH, 3, 2, D], F32)
    projT = {'x': P.dram("projT_x", [INW, NH], BF16), 'c': P.dram("projT_c", [INW, NCX], BF16)}
    zT = {'x': P.dram("zT_x", [D, NH], BF16), 'c': P.dram("zT_c", [D, NCX], BF16)}
    xres = {'x': P.dram("xres_x", [NH, D], F32), 'c': P.dram("xres_c", [NCX, D], F32)}
    wub = P.dram("wub", [DEPTH, 16, 128, 16, 128], BF16)
    wob = P.dram("wob", [DEPTH, 4, 128, 16, 512], BF16)
    masks_d = P.dram("masks_d", [128, 16, 128], F32)
    ab_src = [P.dram("ab_src%d" % i, [512, 1024], BF16) for i in range(4)]
    ab_all = [P.dram("ab_all%d" % i, [1024, 1024], BF16) for i in range(4)]
    ed_src = P.dram("ed_src", [8, 1024], BF16)
    ed_all = P.dram("ed_all", [16, 1024], BF16)
    md_src = P.dram("md_src", [4, 3072], F32)
    md_all = P.dram("md_all", [8, 3072], F32)
    st_src = P.dram("st_src", [16 * 128, 128], F32)
    st_all = P.dram("st_all", [2 * 16 * 128, 128], F32)
    rsc = P.dram("rsc", [8, 4, 128, NH], BF16)
    qdec_d = P.dram("qdec_d", [128, 32, 128], F32)
    dbg = {}
    for k, shp in DEBUG.items():
        dbg[k] = nc.dram_tensor("dbg_" + k, list(shp[0]), shp[1], kind="ExternalOutput").ap()

    cf = P.sbp("cf", [128, 704], F32)
    cb = P.sbp("cb", [128, 512], BF16)
    kdec = P.sbp("kdec", [128, 32], F32)
    dch = P.sbp("dch", [128, 32], F32)
    s0 = P.sbp("s0", [128, 16, 128], F32)
    role = P.sbp("role", [128, 4], F32)
    coef = P.sbp("coef", [128, 32], F32)
    tabk = P.sbp("tabk", [128, 32, 16], F32)
    Dpos, Dneg, Iq, ipl1, imn, jfac = cf[:, 0:128], cf[:, 128:256], cf[:, 256:384], cf[:, 384:512], cf[:, 512:640], cf[:, 640:672]
    ident, perm, cs128 = cb[:, 0:128], cb[:, 128:256], cb[:, 256:512]

    seq_c = Seq('c', NCX, 1, False)
    seq_x = Seq('x', NH, 0, True)

    def cast_upout(l):
        for dc in range(0, 16, 4):
            P.dma(wub[l, dc:dc + 4], wup_in[l, dc:dc + 4], w=[('wub', l, dc)], q='pool')
        for cg in range(4):
            P.dma(wob[l, cg], wout_in[l, cg], w=[('wob', l, cg)], q='pool')

    def prologue():
        P.dma(cf[:], cf_in, w=['cf'])
        P.dma(cb[:], cb_in, w=['cb'])
        cT = P.sb("cT", [128, 16, 2], F32)
        scb = P.sb("scb", [128, 16, 2], BF16)
        HW_ = 3072
        badah = P.sb("badah", [2, DEPTH * HW_], F32)
        ngb = P.sb("ngb", [2, D], F32)
        modh = P.sb("modh", [2, DEPTH, HW_], F32)
        mod = P.sb("mod", [2, 6144], F32)
        wab = [P.sb("wab%d" % i, [128, 16, 512], BF16) for i in range(2)]
        pa = P.ps("pa", [128, 512])
        P.dma(cT[:], cT_in, w=['cT'])
        P.act(lambda e: e.activation(out=scb[:], in_=cT[:], func=AF.Silu), r=['cT'], w=['scb'])
        P.dma(badah[:], bc(bada_in, 2), w=['badah'])
        k = 0
        for l in range(DEPTH):
            for cg in range(6):
                wa = wab[k % 2]
                wr = 'wab%d' % (k % 2)
                k += 1
                P.dma(wa[:], wada_in[l, cg], w=[wr], q='pool')
                for kc in range(16):
                    P.pe(lambda e, wa=wa, kc=kc: e.matmul(pa[0:2, :], lhsT=scb[:, kc, :], rhs=wa[:, kc, :], start=(kc == 0), stop=(kc == 15)),
                         r=['scb', wr], w=['pa'])
                P.dve(lambda e, cg=cg, l=l: e.tensor_tensor(out=modh[:, l, cg * 512:(cg + 1) * 512], in0=pa[0:2, :], in1=badah[:, l * HW_ + cg * 512:l * HW_ + (cg + 1) * 512], op=ALU.add),
                      r=['pa', 'badah'], w=['modh'])
        P.dma(md_src.rearrange("(l w) c -> w l c", w=2), modh[:], r=['modh'], w=['md_src'])
        mdn = P.cc("AllGather", [md_src], [md_all], PAIRS, r=['md_src'], w=['md_all'])
        for l in range(DEPTH):
            P.dma(ngb[:], bc(ng_in[l], 2), w=['ngb'])
            for r_ in range(2):
                P.dma(mod[:, r_ * HW_:(r_ + 1) * HW_], md_all[r_ * 4 + l * 2:r_ * 4 + l * 2 + 2, :], r=['md_all'], w=['mod'], extra=[(mdn, 1)])
            P.dve(lambda e: e.scalar_tensor_tensor(out=mod[:, 2048:4096], in0=mod[:, 2048:4096], scalar=1.0, in1=ngb[:], op0=ALU.add, op1=ALU.mult),
                  r=['mod', 'ngb'], w=['mod'])
            for kind in range(3):
                P.dma(modrows[l, kind], mod[:, kind * 2048:(kind + 1) * 2048], r=['mod'], w=[('modrows', l, kind)])
        lgt = P.sb("lgt", [128, 32], F32)
        lg = P.sb("lg", [128, 32], F32)
        tmp32 = P.sb("tmp32", [128, 32], F32)
        masks = P.sb("masks", [128, 16, 128], F32)
        qdec = P.sb("qdec", [128, 32, 128], F32)
        tm = [P.sb("tm%d" % i, [128, 128], F32) for i in range(2)]
        P.dma(lgt[:], bc(rdl_in), w=['lgt'])
        P.act(lambda e: e.activation(out=tmp32[:], in_=lgt[:], func=AF.Exp, scale=-1.0), r=['lgt'], w=['tmp32'])
        P.act(lambda e: e.activation(out=lgt[:], in_=tmp32[:], func=AF.Ln, bias=1.0), r=['tmp32'], w=['lgt'])
        P.dve(lambda e: e.tensor_scalar(out=lg[:], in0=lgt[:], scalar1=-1.0, scalar2=None, op0=ALU.mult), r=['lgt'], w=['lg'])
        P.dve(lambda e: e.tensor_tensor(out=tmp32[:], in0=lg[:], in1=jfac, op=ALU.mult), r=['lg', 'cf', 'tmp32'], w=['tmp32'])
        P.act(lambda e: e.activation(out=kdec[:], in_=tmp32[:], func=AF.Exp), r=['tmp32'], w=['kdec'])
        P.act(lambda e: e.activation(out=dch[:], in_=lg[:], func=AF.Exp, scale=128.0), r=['lg'], w=['dch'])
        for idx_ in range(32):
            esrc = cf[:, 672:688] if (idx_ % 16) < 8 else cf[:, 688:704]
            P.act(lambda e, idx_=idx_, esrc=esrc: e.activation(out=tabk[:, idx_, :], in_=esrc, func=AF.Exp, scale=lg[:, idx_:idx_ + 1]), r=['lg', 'cf'], w=['tabk'])
        P.dma(role[:], role_in, w=['role'])
        P.act(lambda e: e.activation(out=tmp32[:], in_=lg[:], func=AF.Exp, scale=float(NH)), r=['lg', 'kdec'], w=['tmp32'])
        for l in range(DEPTH):
            P.dve(lambda e, l=l: e.tensor_scalar(out=coef[:, l * 16:l * 16 + 8], in0=tmp32[:, l * 16:l * 16 + 8], scalar1=role[:, 1:2], scalar2=role[:, 0:1], op0=ALU.mult, op1=ALU.add), r=['tmp32', 'role'], w=['coef'])
            P.dve(lambda e, l=l: e.tensor_scalar(out=coef[:, l * 16 + 8:l * 16 + 16], in0=tmp32[:, l * 16 + 8:l * 16 + 16], scalar1=role[:, 0:1], scalar2=role[:, 1:2], op0=ALU.mult, op1=ALU.add), r=['tmp32', 'role'], w=['coef'])
        for l in range(DEPTH):
            for h in range(8):
                t = tm[h % 2]
                tr = 'tm%d' % (h % 2)
                i0, i1 = l * 16 + h, l * 16 + 8 + h
                P.dve(lambda e, t=t, i0=i0: e.tensor_scalar(out=t[:], in0=Dpos, scalar1=lg[:, i0:i0 + 1], scalar2=None, op0=ALU.mult), r=['lg', 'cf'], w=[tr])
                P.dve(lambda e, t=t, i1=i1: e.scalar_tensor_tensor(out=t[:], in0=Dneg, scalar=lg[:, i1:i1 + 1], in1=t[:], op0=ALU.mult, op1=ALU.add), r=['lg', 'cf', tr], w=[tr])
                P.act(lambda e, t=t: e.activation(out=t[:], in_=t[:], func=AF.Exp), r=[tr], w=[tr])
                P.dve(lambda e, t=t, l=l, h=h: e.scalar_tensor_tensor(out=masks[:, l * 8 + h, :], in0=t[:], scalar=QS, in1=Iq, op0=ALU.mult, op1=ALU.add), r=[tr, 'cf'], w=['masks'])
                P.act(lambda e, i0=i0: e.activation(out=qdec[:, i0, :], in_=ipl1, func=AF.Exp, scale=lg[:, i0:i0 + 1]), r=['lg', 'cf'], w=['qdec'])
                P.act(lambda e, i1=i1: e.activation(out=qdec[:, i1, :], in_=imn, func=AF.Exp, scale=lg[:, i1:i1 + 1]), r=['lg', 'cf'], w=['qdec'])
        P.dve(lambda e: e.tensor_scalar(out=qdec[:], in0=qdec[:], scalar1=QS, scalar2=None, op0=ALU.mult), r=['qdec'], w=['qdec'])
        P.dma(masks_d, masks[:], r=['masks'], w=['masks_d'])
        P.dma(qdec_d, qdec[:], r=['qdec'], w=['qdec_d'])
        P.flush()

    def phaseP(l, sq, xsrc, chunks, precast=None, pre=None):
        segs = ([(pre[0], pre[1])] if pre is not None else []) + [(sq, xsrc)]
        N = sum(sg_[0].N for sg_ in segs)
        TS = N
        whos = sorted(set(sg_[0].who for sg_ in segs))
        geffs = {w_: P.sb("geff%d" % w_, [128, D], F32) for w_ in whos}
        shifts = {w_: P.sb("shift%d" % w_, [128, D], F32) for w_ in whos}
        hT = P.sb("hT", [128, 16, TS], BF16)
        xb = [P.sb("xb%d" % i, [128, D], F32) for i in range(2)]
        t1s = [P.sb("t1%d" % i, [128, D], F32) for i in range(2)]
        hbs = [P.sb("hb%d" % i, [128, D], BF16) for i in range(2)]
        junk = P.sb("junk", [128, D], BF16)
        sts = [P.sb("st%d" % i, [128, 4], F32) for i in range(2)]
        wbuf = [P.sb("wbuf%d" % i, [128, 16, 512], BF16) for i in range(2)]
        stg = [P.sb("stg%d" % i, [128, TS], BF16) for i in range(2)]
        pT = P.ps("pT", [128, 2048], BF16)
        pb = [P.ps("pb%d" % i, [128, 512]) for i in range(4)]
        tile_src = []
        for (sq_, src_) in segs:
            for t_ in range(sq_.N // 128):
                tile_src.append((src_, t_ * 128, sq_.who))
        for w_ in sorted(whos, key=lambda w_: min(i_ for i_, ts_ in enumerate(tile_src) if ts_[2] == w_)):
            P.dma(geffs[w_][:], bc(modrows[l, 1, w_]), w=['geff%d' % w_])
            P.dma(shifts[w_][:], bc(modrows[l, 0, w_]), w=['shift%d' % w_])
        groups = sorted(set(c // 4 for c in chunks))
        kx = kw = ks = kp = 0
        pieces = []
        if precast is not None:
            for dc_ in range(16):
                pieces.append(lambda dc_=dc_: P.dma(wub[precast, dc_], wup_in[precast, dc_], w=[('wub', precast, dc_)], q='pool'))
                cg_, q_ = dc_ // 4, dc_ % 4
                pieces.append(lambda cg_=cg_, q_=q_: P.dma(wob[precast, cg_][q_ * 32:(q_ + 1) * 32], wout_in[precast, cg_][q_ * 32:(q_ + 1) * 32], w=[('wob', precast, cg_, q_)], q='pool'))
        for sb0 in range(0, N, TS):
            stageB_prev = None
            for tt in range(TS // 128):
                src_, t0, who_ = tile_src[tt]
                geff, shift = geffs[who_], shifts[who_]
                gn, sn = 'geff%d' % who_, 'shift%d' % who_
                xt = xb[kx % 2]
                xr = 'xb%d' % (kx % 2)
                q2 = kx % 2
                t1, hb, st = t1s[q2], hbs[q2], sts[q2]
                t1n, hbn, stn = 't1%d' % q2, 'hb%d' % q2, 'st%d' % q2
                kx += 1
                P.capture()
                P.dma(xt[:], src_[t0:t0 + 128, :], w=[xr])
                P.act(lambda e, xt=xt, st=st: e.activation(out=junk[:], in_=xt[:], func=AF.Square, accum_out=st[:, 0:1]), r=[xr], w=['junk', stn + 'a'])
                P.act(lambda e, st=st: e.activation(out=st[:, 1:2], in_=st[:, 0:1], func=AF.Ln, scale=1.0 / D, bias=EPS), r=[stn + 'a'], w=[stn + 'b'])
                P.act(lambda e, st=st: e.activation(out=st[:, 2:3], in_=st[:, 1:2], func=AF.Exp, scale=-0.5), r=[stn + 'b'], w=[stn + 'c'])
                P.dve(lambda e, xt=xt, st=st, t1=t1, geff=geff: e.scalar_tensor_tensor(out=t1[:], in0=xt[:], scalar=st[:, 2:3], in1=geff[:], op0=ALU.mult, op1=ALU.mult),
                      r=[xr, stn + 'c', gn], w=[t1n])
                XS = 640
                P.dve(lambda e, t1=t1, hb=hb, shift=shift: e.tensor_tensor(out=hb[:, 0:XS], in0=t1[:, 0:XS], in1=shift[:, 0:XS], op=ALU.add), r=[t1n, sn], w=[hbn + 'a'])
                P.pool(lambda e, t1=t1, hb=hb, shift=shift: e.tensor_tensor(out=hb[:, XS:D], in0=t1[:, XS:D], in1=shift[:, XS:D], op=ALU.add), r=[t1n, sn], w=[hbn + 'b'])
                stageA = P.end_capture()
                if stageB_prev is None:
                    P.replay(stageA)
                else:
                    P.replay(stageA, stageB_prev)
                P.capture()
                for kc in range(16):
                    P.pe(lambda e, kc=kc, hb=hb: e.transpose(out=pT[:, kc * 128:(kc + 1) * 128], in_=hb[:, kc * 128:(kc + 1) * 128], identity=ident),
                         r=[hbn + 'a', hbn + 'b', 'cb'], w=['pT'])
                P.act(lambda e, tt=tt: e.activation(out=hT[:, 0:8, tt * 128:(tt + 1) * 128], in_=pT[:, 0:1024].rearrange("p (c n) -> p c n", c=8), func=AF.Copy),
                      r=['pT'], w=[('hT', tt)])
                P.dve(lambda e, tt=tt: e.tensor_copy(out=hT[:, 8:16, tt * 128:(tt + 1) * 128], in_=pT[:, 1024:2048].rearrange("p (c n) -> p c n", c=8)),
                      r=['pT'], w=[('hT', tt)])
                stageB_prev = P.end_capture()
            P.replay(stageB_prev)
            hTr = [('hT', tt) for tt in range(TS // 128)]
            for cg in groups:
                wb = wbuf[kw % 2]
                wr = 'wbuf%d' % (kw % 2)
                kw += 1
                P.dma(wb[:], win_in[l, cg], w=[wr], q='pool')
                if precast is not None:
                    for _ in range(2 if kw <= 8 else 1):
                        if pieces:
                            pieces.pop(0)()
                for cc in range(4):
                    chunk = cg * 4 + cc
                    if chunk not in chunks:
                        continue
                    c0 = chunk * 128
                    sg = stg[ks % 2]
                    sr = 'stg%d' % (ks % 2)
                    ks += 1
                    if (F_G <= c0 < P_X) or (P_G <= c0 < R_Q) or (R_G <= c0 < MG):
                        fn = AF.Silu
                    elif c0 >= MG:
                        fn = AF.Sigmoid
                    else:
                        fn = None
                    for tb in range(0, TS, 512):
                        tw = min(512, TS - tb)
                        pp = pb[kp % 4]
                        pr = 'pb%d' % (kp % 4)
                        kp += 1
                        for kc in range(16):
                            P.pe(lambda e, pp=pp, wb=wb, kc=kc, cc=cc, tb=tb, tw=tw: e.matmul(pp[:, 0:tw], lhsT=wb[:, kc, cc * 128:(cc + 1) * 128], rhs=hT[:, kc, tb:tb + tw], start=(kc == 0), stop=(kc == 15)),
                                 r=[wr] + hTr[tb // 128:(tb + tw) // 128], w=[pr])
                        if fn is None:
                            P.dve(lambda e, pp=pp, sg=sg, tb=tb, tw=tw: e.tensor_copy(out=sg[:, tb:tb + tw], in_=pp[:, 0:tw]), r=[pr], w=[sr])
                        else:
                            P.act(lambda e, pp=pp, sg=sg, tb=tb, tw=tw, fn=fn: e.activation(out=sg[:, tb:tb + tw], in_=pp[:, 0:tw], func=fn), r=[pr], w=[sr])
                    o_ = 0
                    for (sq_, src_) in segs:
                        P.dma(projT[sq_.name][c0:c0 + 128, 0:sq_.N], sg[:, o_:o_ + sq_.N], r=[sr], w=[('projT', sq_.name, chunk)])
                        o_ += sq_.N
        P.flush()

    def phaseF(l, sq, mode='all', ccw=None, noflush=False):
        N = sq.N
        nloc = N // 128
        nch = nloc if mode != 'f2' else 2 * nloc
        KB = min(N, 512)
        pj = projT[sq.name]
        zt = zT[sq.name]
        AB = P.sb("AB", [128, nch, 4, 256], BF16)
        if mode == 'f2':
            ex = [(nm_, 1) for nm_ in ccw]
            for r_ in range(2):
                for k_ in range(4):
                    c0_ = r_ * nloc + k_ * 4
                    P.dma(AB[:, c0_:c0_ + 4].rearrange("p c g f -> p c (g f)"), ab_all[k_][r_ * 512:(r_ + 1) * 512].rearrange("(c p) f -> p c f", p=128),
                          w=[('ABl', r_, k_)], extra=ex)
        fxb = [P.sb("fxb%d" % i, [128, N], BF16) for i in range(2)]
        wF = P.sb("wF", [128, 4, 128], BF16)
        HC = min(nch, 16)
        NDB = 3 if mode == 'f2' else (1 if mode == 'f1' else 4)
        dbuf = [P.sb("dbuf%d" % i, [128, HC if mode != 'f1' else 1, KB], BF16) for i in range(NDB)]
        mx = [P.sb("mx%d" % i, [128, KB], BF16) for i in range(4)]
        fg = [P.sb("fg%d" % i, [128, KB], BF16) for i in range(2)]
        zf = [P.sb("zf%d" % i, [128, KB], BF16) for i in range(2)]
        pf = [P.ps("pf%d" % i, [128, 512]) for i in range(2)] if mode != 'f2' else None
        pm = [P.ps("pm%d" % i, [128, 512]) for i in range(4)] if mode != 'f1' else None
        py = [P.ps("py%d" % i, [128, 512]) for i in range(2)] if mode != 'f1' else None
        if mode != 'f1':
            P.dma(wF[:], wf_in[l], w=['wF'], q='pool')
        ke = 0
        for g in (range(4) if mode != 'f2' else ()):
            fx = fxb[g % 2]
            fr = 'fxb%d' % (g % 2)
            P.dma(fx[:], pj[F_X + g * 128:F_X + (g + 1) * 128, :], w=[fr])
            for ch in range(0, nch, 2):
                pp = pf[ke % 2]
                pr = 'pf%d' % (ke % 2)
                for s in range(2):
                    P.pe(lambda e, pp=pp, fx=fx, ch=ch, s=s: e.matmul(pp[:, s * 256:(s + 1) * 256], lhsT=fx[:, (ch + s) * 128:(ch + s + 1) * 128], rhs=cs128, start=True, stop=True),
                         r=[fr, 'cb'], w=[pr])
                if ke % 2 == 0:
                    P.act(lambda e, pp=pp, ch=ch, g=g: e.activation(out=AB[:, ch:ch + 2, g, :], in_=pp[:].rearrange("p (s n) -> p s n", s=2), func=AF.Copy), r=[pr], w=[('AB', g)])
                else:
                    P.dve(lambda e, pp=pp, ch=ch, g=g: e.tensor_copy(out=AB[:, ch:ch + 2, g, :], in_=pp[:].rearrange("p (s n) -> p s n", s=2)), r=[pr], w=[('AB', g)])
                ke += 1
        ABr = [('AB', g) for g in range(4)]
        if mode == 'f2':
            ABr = [('ABl', r_, k_) for r_ in range(2) for k_ in range(4)]
        if mode == 'f1':
            names = []
            ev = ed_src.rearrange("r (c j) -> (r c) j", j=16)
            P.dma(ev[:, 0:8], pj[P_X:P_X + 512, 0:8], w=['ed_src0'])
            P.dma(ev[:, 8:16], pj[P_X:P_X + 512, N - 8:N], w=['ed_src1'])
            names.append(P.cc("AllGather", [ed_src], [ed_all], PAIRS, r=['ed_src0', 'ed_src1'], w=['ed_all']))
            for k_ in range(4):
                P.dma(ab_src[k_].rearrange("(c p) f -> p c f", p=128), AB[:, k_ * 4:(k_ + 1) * 4].rearrange("p c g f -> p c (g f)"), r=ABr, w=[('ab_src', k_)])
                names.append(P.cc("AllGather", [ab_src[k_]], [ab_all[k_]], PAIRS, r=[('ab_src', k_)], w=[('ab_all', k_)]))
            if not noflush:
                P.flush()
            return names
        kd = 0
        for kb in range(N // KB):
            k0 = kb * KB
            first = True
            nsteps = 2 * nch
            step = 0
            for tbl in range(2):
                src = (dftc_in if tbl == 0 else dfts_in)[sq.name]
                for hf in range(nch // HC):
                    db = dbuf[kd % NDB]
                    dr = 'dbuf%d' % (kd % NDB)
                    kd += 1
                    P.dma(db[:], src[hf * HC * 128:(hf + 1) * HC * 128, k0:k0 + KB].rearrange("(c p) k -> p c k", p=128), w=[dr])
                    for c in range(HC):
                        ch = hf * HC + c
                        for g in range(4):
                            P.pe(lambda e, g=g, ch=ch, tbl=tbl, db=db, c=c, step=step: e.matmul(pm[g][:, 0:KB], lhsT=AB[:, ch, g, tbl * 128:(tbl + 1) * 128], rhs=db[:, c, :], start=(step == 0), stop=(step == nsteps - 1)),
                                 r=[dr] + ABr, w=['pm%d' % g])
                        step += 1
            for g in range(4):
                if g % 2 == 0 or mode == 'f2':
                    P.act(lambda e, g=g: e.activation(out=mx[g][:], in_=pm[g][:, 0:KB], func=AF.Copy), r=['pm%d' % g], w=['mx%d' % g])
                else:
                    P.dve(lambda e, g=g: e.tensor_copy(out=mx[g][:], in_=pm[g][:, 0:KB]), r=['pm%d' % g], w=['mx%d' % g])
            for g in range(4):
                i = (kb * 4 + g) % 2
                P.dma(fg[i][:], pj[F_G + g * 128:F_G + (g + 1) * 128, k0:k0 + KB], w=['fg%d' % i])
                P.pe(lambda e, g=g, i=i: e.matmul(py[i][:, 0:KB], lhsT=wF[:, g, :], rhs=mx[g][:], start=True, stop=True), r=['wF', 'mx%d' % g], w=['py%d' % i])
                P.dve(lambda e, i=i: e.tensor_tensor(out=zf[i][:], in0=py[i][:, 0:KB], in1=fg[i][:], op=ALU.mult), r=['py%d' % i, 'fg%d' % i], w=['zf%d' % i])
                P.dma(zt[g * 128:(g + 1) * 128, k0:k0 + KB], zf[i][:], r=['zf%d' % i], w=[('zT', g, kb)], q='act')
        if not noflush:
            P.flush()

    def phaseQ(l, sq, noflush=False, ccw=None):
        N = sq.N
        exq = [(ccw, 1)] if ccw is not None else []
        KB = min(N, 512)
        pj = projT[sq.name]
        zt = zT[sq.name]
        pxb = P.sb("pxb", [128, N + 16], BF16)
        X = P.sb("X", [128, N + 16], F32)
        A = P.sb("A", [128, N + 16], F32)
        B = P.sb("B", [128, N + 16], F32)
        invc = P.sb("invc", [128, N], F32)
        pooled = P.sb("pooled", [128, N], BF16)
        pg = P.sb("pg", [128, N], BF16)
        zp = P.sb("zp", [128, N], BF16)
        wP = P.sb("wP", [128, 4, 128], BF16)
        pscale = P.sb("pscale", [128, 4], F32)
        hl = P.sb("hl", [128, 8], BF16)
        hr = P.sb("hr", [128, 8], BF16)
        py = [P.ps("py%d" % i, [128, 512]) for i in range(2)]
        P.dma(wP[:], wp_in[l], w=['wP'], q='pool')
        P.dma(pscale[:], psc_in[l], w=['pscale'])
        P.pool(lambda e: e.memset(pxb[:], 0.0), w=['pxb'])
        k = 0
        for g in range(4):
            P.dma(pxb[:, 8:N + 8], pj[P_X + g * 128:P_X + (g + 1) * 128, :], r=[], w=['pxb'])
            P.dma(pg[:], pj[P_G + g * 128:P_G + (g + 1) * 128, :], w=['pg'])
            P.dma(invc[:], bc(invc_in[sq.name][g]), w=['invc'])
            P.dve(lambda e: e.tensor_copy(out=X[:], in_=pxb[:]), r=['pxb'], w=['X'])
            if sq.name == 'x':
                G0 = ed_all[0:8].rearrange("r (c j) -> (r c) j", j=16)
                G1 = ed_all[8:16].rearrange("r (c j) -> (r c) j", j=16)
                P.dma(hl[:], G0[g * 128:(g + 1) * 128, 8:16], w=['hl'], extra=exq)
                P.dma(hr[:], G1[g * 128:(g + 1) * 128, 0:8], w=['hr'], extra=exq)
                P.dve(lambda e: e.tensor_scalar(out=X[:, 0:8], in0=hl[:], scalar1=role[:, 1:2], scalar2=None, op0=ALU.mult), r=['hl', 'role'], w=['X'])
                P.dve(lambda e: e.tensor_scalar(out=X[:, N + 8:N + 16], in0=hr[:], scalar1=role[:, 0:1], scalar2=None, op0=ALU.mult), r=['hr', 'role'], w=['X'])
            P.dve(lambda e: e.tensor_tensor(out=A[:, 1:N + 16], in0=X[:, 0:N + 15], in1=X[:, 1:N + 16], op=ALU.add), r=['X'], w=['A'])
            cur, curr, oth, othr = A, 'A', B, 'B'
            lo, hi = 1, N + 16
            for s in range(g):
                sh = 1 << s
                P.dve(lambda e, cur=cur, oth=oth, lo=lo, hi=hi, sh=sh: e.tensor_tensor(out=oth[:, lo + sh:hi - sh], in0=cur[:, lo:hi - 2 * sh], in1=cur[:, lo + 2 * sh:hi], op=ALU.add),
                      r=[curr], w=[othr])
                lo, hi = lo + sh, hi - sh
                cur, curr, oth, othr = oth, othr, cur, curr
            P.dve(lambda e, cur=cur, oth=oth: e.tensor_tensor(out=oth[:, 8:N + 8], in0=cur[:, 8:N + 8], in1=invc[:], op=ALU.mult), r=[curr, 'invc'], w=[othr])
            P.dve(lambda e, oth=oth: e.tensor_tensor(out=pooled[:], in0=oth[:, 8:N + 8], in1=X[:, 8:N + 8], op=ALU.subtract), r=[othr, 'X'], w=['pooled'])
            for tb in range(0, N, KB):
                i = k % 2
                k += 1
                P.pe(lambda e, g=g, i=i, tb=tb: e.matmul(py[i][:, 0:KB], lhsT=wP[:, g, :], rhs=pooled[:, tb:tb + KB], start=True, stop=True), r=['wP', 'pooled'], w=['qpy%d' % i])
                P.dve(lambda e, g=g, i=i, tb=tb: e.scalar_tensor_tensor(out=zp[:, tb:tb + KB], in0=py[i][:, 0:KB], scalar=pscale[:, g:g + 1], in1=pg[:, tb:tb + KB], op0=ALU.mult, op1=ALU.mult),
                      r=['qpy%d' % i, 'pscale', 'pg'], w=['zp'])
            P.dma(zt[512 + g * 128:512 + (g + 1) * 128, :], zp[:], r=['zp'], w=[('zT', 4 + g)], q='act')
        if not noflush:
            P.flush()

    def phaseR(l, sq, states_only):
        N = sq.N
        nch = N // 128
        G4 = min(nch, 4)
        pj = projT[sq.name]
        zt = zT[sq.name]
        is_ctx = sq.who == 1
        qT = P.sb("qT", [128, N], BF16)
        kT = P.sb("kT", [128, N], BF16)
        vT = P.sb("vT", [128, N], BF16)
        vtok = P.sb("vtok", [128, nch, 128], BF16)
        ktok = P.sb("ktok", [128, nch, 128], BF16)
        kdf = P.sb("kdf", [128, nch, 128], BF16)
        kdb = P.sb("kdb", [128, nch, 128], BF16)
        Sf = P.sb("Sf", [128, nch, 128], BF16)
        Sb = P.sb("Sb", [128, nch, 128], BF16)
        Sx = [P.sb("Sx%d" % i, [128, 128], F32) for i in range(2)]
        ptr = P.ps("ptr", [128, 1024], BF16)
        pkv = [P.ps("pkv%d" % i, [128, 512]) for i in range(2)]
        if sq.rope:
            ropec = P.sb("ropec", [128, N], F32)
            ropes = P.sb("ropes", [128, N], F32)
            qr = P.sb("qr", [128, N], BF16)
            kr = P.sb("kr", [128, N], BF16)
            rt1 = P.sb("rt1", [128, 512], F32)
            rt2 = P.sb("rt2", [128, 512], F32)
            prot = P.ps("prot", [128, 512])
            P.dma(ropec[:], ropec_in, w=['ropec'])
            P.dma(ropes[:], ropes_in, w=['ropes'])
        else:
            qr, kr = qT, kT
        if not states_only:
            masks = P.sb("masks", [128, 8, 128], F32)
            qdec = P.sb("qdec", [128, 16, 128], F32)
            rg = P.sb("rg", [128, N], BF16)
            qf = P.sb("qf", [128, N], BF16)
            qb = P.sb("qb", [128, N], BF16)
            zr = P.sb("zr", [128, N], BF16)
            PT = P.sb("PT", [128, G4, 128], BF16)
            on = P.sb("on", [128, G4, 128], BF16)
            jk = P.sb("jk", [128, 128], BF16)
            rs = P.sb("rs", [128, 3, G4], F32)
            psc = P.ps("psc", [128, 512])
            po = P.ps("po", [128, 512])
            pT2 = P.ps("pT2", [128, 1024], BF16)
            P.dma(masks[:], masks_d[:, l * 8:(l + 1) * 8, :], w=['masks'])
            P.dma(qdec[:], qdec_d[:, l * 16:(l + 1) * 16, :], w=['qdec'])
        kk = 0
        for h in range(R_HEADS):
            P.dma(kT[:], pj[R_K + h * 128:R_K + (h + 1) * 128, :], w=['kT'])
            P.dma(vT[:], pj[R_V + h * 128:R_V + (h + 1) * 128, :], w=['vT'])
            if not states_only:
                P.dma(qT[:], pj[R_Q + h * 128:R_Q + (h + 1) * 128, :], w=['qT'])
                P.dma(rg[:], pj[R_G + h * 128:R_G + (h + 1) * 128, :], w=['rg'])
            qrr, krr = ('qr', 'kr') if sq.rope else ('qT', 'kT')
            if sq.rope:
                todo = [(kT, 'kT', kr, 'kr')]
                if not states_only:
                    todo.append((qT, 'qT', qr, 'qr'))
                for (src, srn, dst, dsn) in todo:
                    for tb in range(0, N, 512):
                        P.pe(lambda e, src=src, tb=tb: e.matmul(prot[:], lhsT=perm, rhs=src[:, tb:tb + 512], start=True, stop=True), r=[srn, 'cb'], w=['prot'])
                        P.dve(lambda e, tb=tb: e.tensor_tensor(out=rt1[:], in0=prot[:], in1=ropes[:, tb:tb + 512], op=ALU.mult), r=['prot', 'ropes'], w=['rt1'])
                        P.pool(lambda e, src=src, tb=tb: e.tensor_tensor(out=rt2[:], in0=src[:, tb:tb + 512], in1=ropec[:, tb:tb + 512], op=ALU.mult), r=[srn, 'ropec'], w=['rt2'])
                        P.pool(lambda e, dst=dst, tb=tb: e.tensor_tensor(out=dst[:, tb:tb + 512], in0=rt1[:], in1=rt2[:], op=ALU.add), r=['rt1', 'rt2'], w=[dsn])
            for (src, srn, dst, dsn) in ((vT, 'vT', vtok, 'vtok'), (kr, krr, ktok, 'ktok')):
                for c8 in range(0, nch, 8):
                    n8 = min(8, nch - c8)
                    for c in range(n8):
                        P.pe(lambda e, src=src, c=c, c8=c8: e.transpose(out=ptr[:, c * 128:(c + 1) * 128], in_=src[:, (c8 + c) * 128:(c8 + c + 1) * 128], identity=ident), r=[srn, 'cb'], w=['ptr'])
                    if kk % 2 == 0:
                        P.act(lambda e, dst=dst, c8=c8, n8=n8: e.activation(out=dst[:, c8:c8 + n8, :], in_=ptr[:, 0:n8 * 128].rearrange("p (c n) -> p c n", c=n8), func=AF.Copy), r=['ptr'], w=[dsn])
                    else:
                        P.dve(lambda e, dst=dst, c8=c8, n8=n8: e.tensor_copy(out=dst[:, c8:c8 + n8, :], in_=ptr[:, 0:n8 * 128].rearrange("p (c n) -> p c n", c=n8)), r=['ptr'], w=[dsn])
                    kk += 1
            if0, ib0 = l * 16 + h, l * 16 + 8 + h
            P.act(lambda e, i=if0: e.activation(out=kdf[:], in_=ktok[:], func=AF.Copy, scale=kdec[:, i:i + 1]), r=['ktok', 'kdec'], w=['kdf'])
            P.act(lambda e, i=ib0: e.activation(out=kdb[:], in_=ktok[:], func=AF.Copy, scale=kdec[:, i:i + 1]), r=['ktok', 'kdec'], w=['kdb'])
            if R_STAGE < 2:
                continue
            for d in range(2):
                kd_, kdn = (kdf, 'kdf') if d == 0 else (kdb, 'kdb')
                Sall, Sn = (Sf, 'Sf') if d == 0 else (Sb, 'Sb')
                idx = l * 16 + d * 8 + h
                order = list(range(nch)) if d == 0 else list(range(nch - 1, -1, -1))
                cur = 0
                if is_ctx:
                    P.dve(lambda e: e.memset(Sx[0][:], 0.0), w=['Sx0'])
                else:
                    P.dve(lambda e, d=d, h=h: e.tensor_copy(out=Sx[0][:], in_=s0[:, d * 8 + h, :]), r=['s0'], w=['Sx0'])
                P.act(lambda e, Sall=Sall, c=order[0]: e.activation(out=Sall[:, c, :], in_=Sx[0][:], func=AF.Copy), r=['Sx0'], w=[Sn])
                for oi, c in enumerate(order):
                    pk = pkv[oi % 2]
                    P.pe(lambda e, kd_=kd_, c=c, pk=pk: e.matmul(pk[:, 0:128], lhsT=kd_[:, c, :], rhs=vtok[:, c, :], start=True, stop=True), r=[kdn, 'vtok'], w=[('pkv', oi % 2)])
                    nxt = 1 - cur
                    last = oi == nch - 1
                    if last and is_ctx:
                        P.dve(lambda e, cur=cur, pk=pk, idx=idx, d=d, h=h: e.scalar_tensor_tensor(out=s0[:, d * 8 + h, :], in0=Sx[cur][:], scalar=dch[:, idx:idx + 1], in1=pk[:, 0:128], op0=ALU.mult, op1=ALU.add),
                              r=['Sx%d' % cur, 'dch', ('pkv', oi % 2)], w=['s0'])
                    elif not last:
                        P.dve(lambda e, cur=cur, nxt=nxt, pk=pk, idx=idx: e.scalar_tensor_tensor(out=Sx[nxt][:], in0=Sx[cur][:], scalar=dch[:, idx:idx + 1], in1=pk[:, 0:128], op0=ALU.mult, op1=ALU.add),
                              r=['Sx%d' % cur, 'dch', ('pkv', oi % 2)], w=['Sx%d' % nxt])
                        P.act(lambda e, Sall=Sall, c2=order[oi + 1], nxt=nxt: e.activation(out=Sall[:, c2, :], in_=Sx[nxt][:], func=AF.Copy), r=['Sx%d' % nxt], w=[Sn])
                        cur = nxt
            if states_only or R_STAGE < 3:
                continue
            P.dve(lambda e, i=h: e.tensor_tensor(out=qf[:].rearrange("p (c i) -> p c i", i=128), in0=qr[:].rearrange("p (c i) -> p c i", i=128),
                                                  in1=qdec[:, i, :].unsqueeze(1).broadcast_to([128, nch, 128]), op=ALU.mult), r=[qrr, 'qdec'], w=['qf'])
            P.pool(lambda e, i=8 + h: e.tensor_tensor(out=qb[:].rearrange("p (c i) -> p c i", i=128), in0=qr[:].rearrange("p (c i) -> p c i", i=128),
                                                      in1=qdec[:, i, :].unsqueeze(1).broadcast_to([128, nch, 128]), op=ALU.mult), r=[qrr, 'qdec'], w=['qb'])
            if R_STAGE < 4:
                continue
            for c4 in range(0, nch, G4):
                for cc in range(G4):
                    c = c4 + cc
                    P.pe(lambda e, c=c, cc=cc: e.matmul(psc[:, cc * 128:(cc + 1) * 128], lhsT=kr[:, c * 128:(c + 1) * 128], rhs=qr[:, c * 128:(c + 1) * 128], start=True, stop=True), r=[krr, qrr], w=['psc'])
                P.dve(lambda e, h=h: e.tensor_tensor(out=PT[:], in0=psc[:, 0:G4 * 128].rearrange("p (c i) -> p c i", i=128), in1=masks[:, h, :].unsqueeze(1).broadcast_to([128, G4, 128]), op=ALU.mult),
                      r=['psc', 'masks'], w=['PT'])
                for cc in range(G4):
                    c = c4 + cc
                    P.pe(lambda e, c=c, cc=cc: e.matmul(po[:, cc * 128:(cc + 1) * 128], lhsT=PT[:, cc, :], rhs=vtok[:, c, :], start=True, stop=False), r=['PT', 'vtok'], w=['po'])
                    P.pe(lambda e, c=c, cc=cc: e.matmul(po[:, cc * 128:(cc + 1) * 128], lhsT=qf[:, c * 128:(c + 1) * 128], rhs=Sf[:, c, :], start=False, stop=False), r=['qf', 'Sf'], w=['po'])
                    P.pe(lambda e, c=c, cc=cc: e.matmul(po[:, cc * 128:(cc + 1) * 128], lhsT=qb[:, c * 128:(c + 1) * 128], rhs=Sb[:, c, :], start=False, stop=True), r=['qb', 'Sb'], w=['po'])
                for cc in range(G4):
                    P.act(lambda e, cc=cc: e.activation(out=jk[:], in_=po[:, cc * 128:(cc + 1) * 128], func=AF.Square, accum_out=rs[:, 0, cc:cc + 1]), r=['po'], w=['jk', 'rs0'])
                P.act(lambda e: e.activation(out=rs[:, 1, :], in_=rs[:, 0, :], func=AF.Ln, scale=1.0 / 128, bias=EPS), r=['rs0'], w=['rs1'])
                P.act(lambda e: e.activation(out=rs[:, 2, :], in_=rs[:, 1, :], func=AF.Exp, scale=-0.5), r=['rs1'], w=['rs2'])
                P.dve(lambda e: e.tensor_tensor(out=on[:], in0=po[:, 0:G4 * 128].rearrange("p (c i) -> p c i", i=128), in1=rs[:, 2, :].unsqueeze(2).broadcast_to([128, G4, 128]), op=ALU.mult),
                      r=['po', 'rs2'], w=['on'])
                for cc in range(G4):
                    P.pe(lambda e, cc=cc: e.transpose(out=pT2[:, cc * 128:(cc + 1) * 128], in_=on[:, cc, :], identity=ident), r=['on', 'cb'], w=['pT2'])
                P.dve(lambda e, c4=c4: e.tensor_tensor(out=zr[:, c4 * 128:(c4 + G4) * 128], in0=pT2[:, 0:G4 * 128], in1=rg[:, c4 * 128:(c4 + G4) * 128], op=ALU.mult), r=['pT2', 'rg'], w=['zr'])
            P.dma(zt[1024 + h * 128:1024 + (h + 1) * 128, :], zr[:], r=['zr'], w=[('zT', 8 + h)])
        P.flush()

    def rope_emit(src, srn, dst, dsn, N, ropec, ropes, prot, rt1, rt2, cnt):
        for tb in range(0, N, 512):
            i = cnt[0] % 2
            cnt[0] += 1
            P.pe(lambda e, src=src, tb=tb, i=i: e.matmul(prot[i][:], lhsT=perm, rhs=src[:, tb:tb + 512], start=True, stop=True), r=[srn, 'cb'], w=['prot%d' % i])
            P.dve(lambda e, tb=tb, i=i: e.tensor_tensor(out=rt1[i][:], in0=prot[i][:], in1=ropes[:, tb:tb + 512], op=ALU.mult), r=['prot%d' % i, 'ropes'], w=['rt1%d' % i])
            P.pool(lambda e, src=src, tb=tb, i=i: e.tensor_tensor(out=rt2[i][:], in0=src[:, tb:tb + 512], in1=ropec[:, tb:tb + 512], op=ALU.mult), r=[srn, 'ropec'], w=['rt2%d' % i])
            P.pool(lambda e, dst=dst, tb=tb, i=i: e.tensor_tensor(out=dst[:, tb:tb + 512], in0=rt1[i][:], in1=rt2[i][:], op=ALU.add), r=['rt1%d' % i, 'rt2%d' % i], w=[dsn])

    def phaseR0x(l):
        N = NH
        nch = N // 128
        pj = projT['x']
        ropec = P.sb("ropec", [128, N], F32)
        ropes = P.sb("ropes", [128, N], F32)
        kT = [P.sb("kT%d" % i, [128, N], BF16) for i in range(2)]
        vT = [P.sb("vT%d" % i, [128, N], BF16) for i in range(2)]
        kr = [P.sb("kr%d" % i, [128, N], BF16) for i in range(2)]
        vtok = [P.sb("vtok%d" % i, [128, nch, 128], BF16) for i in range(2)]
        ktok = [P.sb("ktok%d" % i, [128, nch, 128], BF16) for i in range(2)]
        kdf = [P.sb("kdf%d" % i, [128, nch, 128], BF16) for i in range(2)]
        kdb = [P.sb("kdb%d" % i, [128, nch, 128], BF16) for i in range(2)]
        kfin = [[P.sb("kfin%d_%d" % (d_, i), [128, nch, 128], BF16) for i in range(2)] for d_ in range(2)]
        Lst = P.sb("Lst", [128, 16, 128], F32)
        rt1 = [P.sb("rt1%d" % i, [128, 512], F32) for i in range(2)]
        rt2 = [P.sb("rt2%d" % i, [128, 512], F32) for i in range(2)]
        prot = [P.ps("prot%d" % i, [128, 512]) for i in range(2)]
        ptr = [P.ps("ptr%d" % i, [128, 1024], BF16) for i in range(2)]
        pkv = [P.ps("pkv%d" % i, [128, 512]) for i in range(2)]
        P.dma(ropec[:], ropec_in, w=['ropec'])
        P.dma(ropes[:], ropes_in, w=['ropes'])
        cnt = [0]
        kk = 0
        kq = 0
        def stA1(h):
            p = h % 2
            P.capture()
            P.dma(kT[p][:], pj[R_K + h * 128:R_K + (h + 1) * 128, :], w=['kT%d' % p])
            P.dma(vT[p][:], pj[R_V + h * 128:R_V + (h + 1) * 128, :], w=['vT%d' % p])
            rope_emit(kT[p], 'kT%d' % p, kr[p], 'kr%d' % p, N, ropec, ropes, prot, rt1, rt2, cnt)
            return P.end_capture()

        def stA2(h):
            p = h % 2
            P.capture()
            for (src, srn, dst, dsn) in ((vT[p], 'vT%d' % p, vtok[p], 'vtok%d' % p), (kr[p], 'kr%d' % p, ktok[p], 'ktok%d' % p)):
                for c8 in range(0, nch, 8):
                    pt = ptr[kkc[0] % 2]
                    ptn = 'ptr%d' % (kkc[0] % 2)
                    for c in range(8):
                        P.pe(lambda e, src=src, c=c, c8=c8, pt=pt: e.transpose(out=pt[:, c * 128:(c + 1) * 128], in_=src[:, (c8 + c) * 128:(c8 + c + 1) * 128], identity=ident), r=[srn, 'cb'], w=[ptn])
                    if kkc[0] % 2 == 0:
                        P.act(lambda e, dst=dst, c8=c8, pt=pt: e.activation(out=dst[:, c8:c8 + 8, :], in_=pt[:].rearrange("p (c n) -> p c n", c=8), func=AF.Copy), r=[ptn], w=[dsn])
                    else:
                        P.dve(lambda e, dst=dst, c8=c8, pt=pt: e.tensor_copy(out=dst[:, c8:c8 + 8, :], in_=pt[:].rearrange("p (c n) -> p c n", c=8)), r=[ptn], w=[dsn])
                    kkc[0] += 1
            if0, ib0 = l * 16 + h, l * 16 + 8 + h
            P.act(lambda e, i=if0, p=p: e.activation(out=kdf[p][:], in_=ktok[p][:], func=AF.Copy, scale=kdec[:, i:i + 1]), r=['ktok%d' % p, 'kdec'], w=['kdf%d' % p])
            P.act(lambda e, i=ib0, p=p: e.activation(out=kdb[p][:], in_=ktok[p][:], func=AF.Copy, scale=kdec[:, i:i + 1]), r=['ktok%d' % p, 'kdec'], w=['kdb%d' % p])
            P.dma(rsc[h, 0], kr[p][:], r=['kr%d' % p], w=[('rsc', h, 0)], q='act')
            P.dma(rsc[h, 1], vtok[p][:].rearrange("p c n -> p (c n)"), r=['vtok%d' % p], w=[('rsc', h, 1)], q='act')
            P.dma(rsc[h, 2], kdf[p][:].rearrange("p c n -> p (c n)"), r=['kdf%d' % p], w=[('rsc', h, 2)], q='act')
            P.dma(rsc[h, 3], kdb[p][:].rearrange("p c n -> p (c n)"), r=['kdb%d' % p], w=[('rsc', h, 3)], q='act')
            for d in range(2):
                idx = l * 16 + d * 8 + h
                P.dve(lambda e, d=d, p=p, idx=idx: e.tensor_tensor(out=kfin[d][p][:], in0=ktok[p][:], in1=tabk[:, idx, :].unsqueeze(2).broadcast_to([128, nch, 128]), op=ALU.mult),
                      r=['ktok%d' % p, 'tabk'], w=['kfin%d_%d' % (d, p)])
            return P.end_capture()

        def stB(h):
            p = h % 2
            P.capture()
            for d in range(2):
                pk = pkv[kqc[0] % 2]
                pkn = 'pkv%d' % (kqc[0] % 2)
                kqc[0] += 1
                for c in range(nch):
                    P.pe(lambda e, d=d, c=c, pk=pk, p=p: e.matmul(pk[:, 0:128], lhsT=kfin[d][p][:, c, :], rhs=vtok[p][:, c, :], start=(c == 0), stop=(c == nch - 1)), r=['kfin%d_%d' % (d, p), 'vtok%d' % p], w=[pkn])
                if d == 0:
                    P.act(lambda e, pk=pk, d=d, h=h: e.activation(out=Lst[:, d * 8 + h, :], in_=pk[:, 0:128], func=AF.Copy), r=[pkn], w=['Lst'])
                else:
                    P.dve(lambda e, pk=pk, d=d, h=h: e.tensor_copy(out=Lst[:, d * 8 + h, :], in_=pk[:, 0:128]), r=[pkn], w=['Lst'])
            return P.end_capture()

        kkc = [0]
        kqc = [0]
        for sst in range(8 + 2):
            lists = []
            if sst < 8:
                lists.append(stA1(sst))
            if 0 <= sst - 1 < 8:
                lists.append(stA2(sst - 1))
            if 0 <= sst - 2 < 8:
                lists.append(stB(sst - 2))
            P.replay(*lists)
        P.dma(st_src.rearrange("(k p) e -> p k e", p=128), Lst[:], r=['Lst'], w=['st_src'])
        name = P.cc("AllGather", [st_src], [st_all], PAIRS, r=['st_src'], w=['st_all'])
        P.flush()
        return name

    def phaseRx(l, ccw, precast=None):
        N = NH
        nch = N // 128
        G4 = 4
        pj = projT['x']
        zt = zT['x']
        ropec = P.sb("ropec", [128, N], F32)
        ropes = P.sb("ropes", [128, N], F32)
        masks = P.sb("masks", [128, 8, 128], F32)
        qdec = P.sb("qdec", [128, 16, 128], F32)
        names = ['qT', 'rg', 'kr', 'vtok', 'kdf', 'kdb', 'qr', 'qf', 'qb', 'Sf', 'Sb', 'zr']
        NS = 3
        T = {nm: [P.sb("%s%d" % (nm, i), [128, N], BF16) for i in range(NS)] for nm in names}
        Ii = [P.sb("Ii%d" % i, [128, 128], F32) for i in range(2)]
        tis = [P.sb("ti%d" % i, [128, 128], F32) for i in range(2)]
        Sxd = [[P.sb("Sx%d_%d" % (d_, i), [128, 128], F32) for i in range(2)] for d_ in range(2)]
        PT = [P.sb("PT%d" % i, [128, G4, 128], BF16) for i in range(2)]
        on = [P.sb("on%d" % i, [128, G4, 128], BF16) for i in range(2)]
        jk = P.sb("jk", [128, 128], BF16)
        rs = [P.sb("rs%d" % i, [128, 3, G4], F32) for i in range(2)]
        rt1 = [P.sb("rt1%d" % i, [128, 512], F32) for i in range(2)]
        rt2 = [P.sb("rt2%d" % i, [128, 512], F32) for i in range(2)]
        prot = [P.ps("prot%d" % i, [128, 512]) for i in range(1)]
        prot = [prot[0], prot[0]]
        pkv = [P.ps("pkv%d" % i, [128, 512]) for i in range(2)]
        psc = [P.ps("psc%d" % i, [128, 512]) for i in range(2)]
        po = [P.ps("po%d" % i, [128, 512]) for i in range(2)]
        pT2 = P.ps("pT2", [128, 1024], BF16)
        P.dma(ropec[:], ropec_in, w=['ropec'])
        P.dma(ropes[:], ropes_in, w=['ropes'])
        P.dma(masks[:], masks_d[:, l * 8:(l + 1) * 8, :], w=['masks'])
        P.dma(qdec[:], qdec_d[:, l * 16:(l + 1) * 16, :], w=['qdec'])
        stv = st_all.rearrange("(r k p) e -> r k p e", r=2, k=16)
        if precast is not None:
            cast_upout(precast)
        cnt = [0]
        kq = 0
        kg = 0
        c3 = lambda t: t[:].rearrange("p (c i) -> p c i", i=128)
        stage_lists = {}
        for h in range(8):
            p = h % NS
            R = lambda nm, p=p: '%s%d' % (nm, p)
            qT, rg, kr, vtok, kdf, kdb, qr, qf, qb, Sf, Sb, zr = [T[nm][p] for nm in names]
            P.capture()
            P.dma(qT[:], pj[R_Q + h * 128:R_Q + (h + 1) * 128, :], w=[R('qT')])
            P.dma(rg[:], pj[R_G + h * 128:R_G + (h + 1) * 128, :], w=[R('rg')])
            P.dma(kr[:], rsc[h, 0], w=[R('kr')])
            P.dma(vtok[:], rsc[h, 1], w=[R('vtok')])
            P.dma(kdf[:], rsc[h, 2], w=[R('kdf')])
            P.dma(kdb[:], rsc[h, 3], w=[R('kdb')])
            for tb in range(0, N, 512):
                i = cnt[0] % 2
                cnt[0] += 1
                P.pe(lambda e, qT=qT, tb=tb: e.matmul(prot[0][:], lhsT=perm, rhs=qT[:, tb:tb + 512], start=True, stop=True), r=[R('qT'), 'cb'], w=['prot'])
                P.dve(lambda e, tb=tb, i=i: e.tensor_tensor(out=rt1[i][:], in0=prot[0][:], in1=ropes[:, tb:tb + 512], op=ALU.mult), r=['prot', 'ropes'], w=['rt1%d' % i])
                P.pool(lambda e, qT=qT, tb=tb, i=i: e.tensor_tensor(out=rt2[i][:], in0=qT[:, tb:tb + 512], in1=ropec[:, tb:tb + 512], op=ALU.mult), r=[R('qT'), 'ropec'], w=['rt2%d' % i])
                P.pool(lambda e, qr=qr, tb=tb, i=i: e.tensor_tensor(out=qr[:, tb:tb + 512], in0=rt1[i][:], in1=rt2[i][:], op=ALU.add), r=['rt1%d' % i, 'rt2%d' % i], w=[R('qr')])
            P.pool(lambda e, i=h, qf=qf, qr=qr: e.tensor_tensor(out=c3(qf), in0=c3(qr), in1=qdec[:, i, :].unsqueeze(1).broadcast_to([128, nch, 128]), op=ALU.mult), r=[R('qr'), 'qdec'], w=[R('qf')])
            P.pool(lambda e, i=8 + h, qb=qb, qr=qr: e.tensor_tensor(out=c3(qb), in0=c3(qr), in1=qdec[:, i, :].unsqueeze(1).broadcast_to([128, nch, 128]), op=ALU.mult), r=[R('qr'), 'qdec'], w=[R('qb')])
            stage_lists[('A1', h)] = P.end_capture()
            P.capture()
            dlists = []
            for d in range(2):
                P.capture()
                kd_, kdn = (kdf, R('kdf')) if d == 0 else (kdb, R('kdb'))
                Sall, Sn = (Sf, R('Sf')) if d == 0 else (Sb, R('Sb'))
                idx = l * 16 + d * 8 + h
                order = list(range(nch)) if d == 0 else list(range(nch - 1, -1, -1))
                ii = Ii[d]
                tid = tis[d]
                SxD = Sxd[d]
                pk = pkv[d]
                pkn = 'pkv%d' % d
                src = stv[0, h] if d == 0 else stv[1, 8 + h]
                P.dma(ii[:], src, w=['Ii%d' % d], extra=[(ccw, 1)])
                P.dve(lambda e, d=d, h=h, idx=idx, tid=tid: e.tensor_scalar(out=tid[:], in0=s0[:, d * 8 + h, :], scalar1=coef[:, idx:idx + 1], scalar2=None, op0=ALU.mult), r=['s0', 'coef'], w=['ti%d' % d])
                P.dve(lambda e, d=d, ii=ii, tid=tid, SxD=SxD: e.scalar_tensor_tensor(out=SxD[0][:], in0=ii[:], scalar=role[:, (1 - d):(2 - d)], in1=tid[:], op0=ALU.mult, op1=ALU.add), r=['Ii%d' % d, 'role', 'ti%d' % d], w=['Sx%d_0' % d])
                P.act(lambda e, Sall=Sall, c=order[0], SxD=SxD: e.activation(out=Sall[:, c * 128:(c + 1) * 128], in_=SxD[0][:], func=AF.Copy), r=['Sx%d_0' % d], w=[Sn])
                cur = 0
                for oi, c in enumerate(order[:-1]):
                    P.pe(lambda e, kd_=kd_, c=c, pk=pk, vtok=vtok: e.matmul(pk[:, 0:128], lhsT=kd_[:, c * 128:(c + 1) * 128], rhs=vtok[:, c * 128:(c + 1) * 128], start=True, stop=True), r=[kdn, R('vtok')], w=[pkn])
                    nxt = 1 - cur
                    P.dve(lambda e, cur=cur, nxt=nxt, pk=pk, idx=idx, SxD=SxD: e.scalar_tensor_tensor(out=SxD[nxt][:], in0=SxD[cur][:], scalar=dch[:, idx:idx + 1], in1=pk[:, 0:128], op0=ALU.mult, op1=ALU.add),
                          r=['Sx%d_%d' % (d, cur), 'dch', pkn], w=['Sx%d_%d' % (d, nxt)])
                    P.act(lambda e, Sall=Sall, c2=order[oi + 1], nxt=nxt, SxD=SxD: e.activation(out=Sall[:, c2 * 128:(c2 + 1) * 128], in_=SxD[nxt][:], func=AF.Copy), r=['Sx%d_%d' % (d, nxt)], w=[Sn])
                    cur = nxt
                dlists.append(P.end_capture())
            P.replay(dlists[0], dlists[1])
            stage_lists[('A2', h)] = P.end_capture()
            P.capture()
            NG = nch // G4

            def stX(g, h=h, kr=kr, qr=qr):
                g2 = g % 2
                c4 = g * G4
                ps_, PT_ = psc[g2], PT[g2]
                pscn, PTn = 'psc%d' % g2, 'PT%d' % g2
                for cc in range(G4):
                    c = c4 + cc
                    P.pe(lambda e, c=c, cc=cc, ps_=ps_: e.matmul(ps_[:, cc * 128:(cc + 1) * 128], lhsT=kr[:, c * 128:(c + 1) * 128], rhs=qr[:, c * 128:(c + 1) * 128], start=True, stop=True), r=[R('kr'), R('qr')], w=[pscn])
                P.dve(lambda e, ps_=ps_, PT_=PT_: e.tensor_tensor(out=PT_[:], in0=ps_[:].rearrange("p (c i) -> p c i", i=128), in1=masks[:, h, :].unsqueeze(1).broadcast_to([128, G4, 128]), op=ALU.mult),
                      r=[pscn, 'masks'], w=[PTn])

            def stY(g, vtok=vtok, qf=qf, qb=qb, Sf=Sf, Sb=Sb):
                g2 = g % 2
                c4 = g * G4
                po_, PT_, on_, rs_ = po[g2], PT[g2], on[g2], rs[g2]
                pon, PTn, onn, rsn = 'po%d' % g2, 'PT%d' % g2, 'on%d' % g2, 'rs%d' % g2
                for cc in range(G4):
                    c = c4 + cc
                    sl = slice(c * 128, (c + 1) * 128)
                    P.pe(lambda e, cc=cc, sl=sl: e.matmul(po_[:, cc * 128:(cc + 1) * 128], lhsT=PT_[:, cc, :], rhs=vtok[:, sl], start=True, stop=False), r=[PTn, R('vtok')], w=[pon])
                    P.pe(lambda e, cc=cc, sl=sl: e.matmul(po_[:, cc * 128:(cc + 1) * 128], lhsT=qf[:, sl], rhs=Sf[:, sl], start=False, stop=False), r=[R('qf'), R('Sf')], w=[pon])
                    P.pe(lambda e, cc=cc, sl=sl: e.matmul(po_[:, cc * 128:(cc + 1) * 128], lhsT=qb[:, sl], rhs=Sb[:, sl], start=False, stop=True), r=[R('qb'), R('Sb')], w=[pon])
                for cc in range(G4):
                    P.act(lambda e, cc=cc: e.activation(out=jk[:], in_=po_[:, cc * 128:(cc + 1) * 128], func=AF.Square, accum_out=rs_[:, 0, cc:cc + 1]), r=[pon], w=['jk', rsn + 'a'])
                P.act(lambda e: e.activation(out=rs_[:, 1, :], in_=rs_[:, 0, :], func=AF.Ln, scale=1.0 / 128, bias=EPS), r=[rsn + 'a'], w=[rsn + 'b'])
                P.act(lambda e: e.activation(out=rs_[:, 2, :], in_=rs_[:, 1, :], func=AF.Exp, scale=-0.5), r=[rsn + 'b'], w=[rsn + 'c'])
                for cc in range(G4):
                    P.act(lambda e, cc=cc: e.activation(out=on_[:, cc, :], in_=po_[:, cc * 128:(cc + 1) * 128], func=AF.Copy, scale=rs_[:, 2, cc:cc + 1]), r=[pon, rsn + 'c'], w=[onn])

            def stZ(g, zr=zr, rg=rg):
                g2 = g % 2
                c4 = g * G4
                on_ = on[g2]
                onn = 'on%d' % g2
                for cc in range(G4):
                    P.pe(lambda e, cc=cc: e.transpose(out=pT2[:, cc * 128:(cc + 1) * 128], in_=on_[:, cc, :], identity=ident), r=[onn, 'cb'], w=['pT2'])
                P.dve(lambda e: e.tensor_tensor(out=zr[:, c4 * 128:(c4 + G4) * 128], in0=pT2[:, 0:G4 * 128], in1=rg[:, c4 * 128:(c4 + G4) * 128], op=ALU.mult), r=['pT2', R('rg')], w=[R('zr')])

            for sst in range(NG + 2):
                if sst < NG:
                    stX(sst)
                if 0 <= sst - 1 < NG:
                    stY(sst - 1)
                if 0 <= sst - 2 < NG:
                    stZ(sst - 2)
            P.dma(zt[1024 + h * 128:1024 + (h + 1) * 128, :], zr[:], r=[R('zr')], w=[('zT', 8 + h)], q='act')
            stage_lists[('B', h)] = P.end_capture()
        for sst in range(8 + 2):
            lists = []
            if sst < 8:
                lists.append(stage_lists[('A1', sst)])
            if 0 <= sst - 1 < 8:
                lists.append(stage_lists[('A2', sst - 1)])
            if 0 <= sst - 2 < 8:
                lists.append(stage_lists[('B', sst - 2)])
            P.replay(*lists)
        P.flush()

    def phaseM(l, sq, xsrc, xdst, final, precast=None):
        N = sq.N
        KB = min(N, 512)
        NT = KB // 128
        pj = projT[sq.name]
        zt = zT[sq.name]
        gate = P.sb("gate", [128, D], F32)
        zbs = [P.sb("zb%d" % i, [128, 16, KB], BF16) for i in range(2)]
        mergeds = [P.sb("merged%d" % i, [128, 16, KB], BF16) for i in range(2)]
        wu = [P.sb("wu%d" % i, [128, 16, 128], BF16) for i in range(3)]
        wo = [P.sb("wo%d" % i, [128, 16, 512], BF16) for i in range(2)]
        gts = [P.sb("gts%d" % i, [128, 3, KB], BF16) for i in range(3)]
        tqs = [[P.sb("tq%d_%d" % (i, j), [128, KB], F32) for i in range(3)] for j in range(2)]
        xin = [P.sb("xin%d" % i, [128, 512], F32) for i in range(2)]
        tys = [P.sb("ty%d" % i, [128, 512], F32) for i in range(2)]
        if final:
            fng = P.sb("fng", [128, D], F32)
            xfull = [P.sb("xfull%d" % i, [128, D], F32) for i in range(NT)]
            ss = P.sb("ss", [128, NT, 8], F32)
            jk2 = P.sb("jk2", [128, 512], BF16)
            P.dma(fng[:], bc(fng_in), w=['fng'])
        else:
            xo = [P.sb("xo%d" % i, [128, 512], F32) for i in range(2)]
        pybs = [[P.ps("pyb%d_%d" % (i, j), [128, 512]) for i in range(3)] for j in range(2)]
        pout = [P.ps("pout%d" % i, [128, 512]) for i in range(2)]
        P.dma(gate[:], bc(modrows[l, 2, sq.who]), w=['gate'])
        gsrc = pj[MG:INW].rearrange("(b c p) t -> p b c t", b=3, c=16)
        cn = dict(ku=0, ko=0, kx=0, kp=0)
        NB = N // KB

        def emitU(bi, dcs):
            tb = bi * KB
            zbi = bi % 2
            zb = zbs[zbi]
            zbn = 'zb%d' % zbi
            mg = mergeds[bi % 2]
            for dc in dcs:
                ku = cn['ku']
                cn['ku'] += 1
                w_ = wu[ku % 3]
                wr = 'wu%d' % (ku % 3)
                gt = gts[ku % 3]
                gr = 'gts%d' % (ku % 3)
                pyb = pybs[ku % 2]
                tq = tqs[ku % 2]
                pj_ = ku % 2
                while cn.get('kl', 0) <= min(ku + 2, NB * 16 - 1):
                    kl = cn.get('kl', 0)
                    cn['kl'] = kl + 1
                    lb, ldc = kl // 16, kl % 16
                    P.dma(wu[kl % 3][:], wub[l, ldc], w=['wu%d' % (kl % 3)])
                    P.dma(gts[kl % 3][:], gsrc[:, :, ldc, lb * KB:(lb + 1) * KB], w=['gts%d' % (kl % 3)])
                for br, (k0, k1) in enumerate(((0, 4), (4, 8), (8, 16))):
                    for kc in range(k0, k1):
                        P.pe(lambda e, br=br, kc=kc, k0=k0, k1=k1, w_=w_, pyb=pyb, zb=zb: e.matmul(pyb[br][:, 0:KB], lhsT=w_[:, kc, :], rhs=zb[:, kc, :], start=(kc == k0), stop=(kc == k1 - 1)),
                             r=[wr, zbn], w=['pyb%d_%d' % (br, pj_)])
                for br in range(3):
                    P.dve(lambda e, br=br, gt=gt, pyb=pyb, tq=tq: e.tensor_tensor(out=tq[br][:], in0=pyb[br][:, 0:KB], in1=gt[:, br, :], op=ALU.mult), r=['pyb%d_%d' % (br, pj_), gr], w=['tq%d_%d' % (br, pj_)])
                P.pool(lambda e, tq=tq: e.tensor_tensor(out=tq[0][:], in0=tq[0][:], in1=tq[1][:], op=ALU.add), r=['tq0_%d' % pj_, 'tq1_%d' % pj_], w=['tq0_%d' % pj_])
                P.pool(lambda e, dc=dc, tq=tq, mg=mg: e.tensor_tensor(out=mg[:, dc, :], in0=tq[0][:], in1=tq[2][:], op=ALU.add), r=['tq0_%d' % pj_, 'tq2_%d' % pj_], w=[('merged', bi % 2, dc)])

        def emitO(bi):
            tb = bi * KB
            mg = mergeds[bi % 2]
            mgr = [('merged', bi % 2, d_) for d_ in range(16)]
            for cg in range(4):
                ko = cn['ko']
                cn['ko'] += 1
                wo_ = wo[ko % 2]
                wor = 'wo%d' % (ko % 2)
                if cg < 3:
                    P.dma(wo[(ko + 1) % 2][:], wob[l, cg + 1], w=['wo%d' % ((ko + 1) % 2)])
                for ti in range(NT):
                    t0 = tb + ti * 128
                    kp, kx = cn['kp'], cn['kx']
                    cn['kp'] += 1
                    cn['kx'] += 1
                    pp = pout[kp % 2]
                    pr = 'pout%d' % (kp % 2)
                    xi = xin[kx % 2]
                    xr = 'xin%d' % (kx % 2)
                    P.dma(xi[:], xsrc[t0:t0 + 128, cg * 512:(cg + 1) * 512], w=[xr])
                    for kc in range(16):
                        P.pe(lambda e, pp=pp, kc=kc, ti=ti, wo_=wo_, mg=mg: e.matmul(pp[:], lhsT=mg[:, kc, ti * 128:(ti + 1) * 128], rhs=wo_[:, kc, :], start=(kc == 0), stop=(kc == 15)),
                             r=mgr + [wor], w=[pr])
                    ty = tys[kx % 2]
                    tyn = 'ty%d' % (kx % 2)
                    P.dve(lambda e, pp=pp, cg=cg, ty=ty: e.tensor_tensor(out=ty[:], in0=pp[:], in1=gate[:, cg * 512:(cg + 1) * 512], op=ALU.mult), r=[pr, 'gate'], w=[tyn])
                    if final:
                        xf = xfull[ti]
                        P.pool(lambda e, xf=xf, xi=xi, cg=cg, ty=ty: e.tensor_tensor(out=xf[:, cg * 512:(cg + 1) * 512], in0=ty[:], in1=xi[:], op=ALU.add), r=[tyn, xr], w=[('xfull', ti)])
                        P.act(lambda e, xf=xf, cg=cg, ti=ti: e.activation(out=jk2[:], in_=xf[:, cg * 512:(cg + 1) * 512], func=AF.Square, accum_out=ss[:, ti, cg:cg + 1]), r=[('xfull', ti)], w=['jk2', ('ss', ti)])
                    else:
                        xo_ = xo[kx % 2]
                        xor_ = 'xo%d' % (kx % 2)
                        P.pool(lambda e, xo_=xo_, xi=xi, ty=ty: e.tensor_tensor(out=xo_[:], in0=ty[:], in1=xi[:], op=ALU.add), r=[tyn, xr], w=[xor_])
                        P.dma(xdst[t0:t0 + 128, cg * 512:(cg + 1) * 512], xo_[:], r=[xor_], w=[('xdst', t0, cg)], q='act')
            if final:
                for ti in range(NT):
                    t0 = tb + ti * 128
                    xf = xfull[ti]
                    P.dve(lambda e, ti=ti: e.tensor_tensor(out=ss[:, ti, 4:5], in0=ss[:, ti, 0:1], in1=ss[:, ti, 1:2], op=ALU.add), r=[('ss', ti)], w=[('ss', ti)])
                    P.dve(lambda e, ti=ti: e.tensor_tensor(out=ss[:, ti, 5:6], in0=ss[:, ti, 2:3], in1=ss[:, ti, 3:4], op=ALU.add), r=[('ss', ti)], w=[('ss', ti)])
                    P.dve(lambda e, ti=ti: e.tensor_tensor(out=ss[:, ti, 6:7], in0=ss[:, ti, 4:5], in1=ss[:, ti, 5:6], op=ALU.add), r=[('ss', ti)], w=[('ss', ti)])
                    P.act(lambda e, ti=ti: e.activation(out=ss[:, ti, 7:8], in_=ss[:, ti, 6:7], func=AF.Ln, scale=1.0 / D, bias=EPS), r=[('ss', ti)], w=[('ss', ti)])
                    P.act(lambda e, ti=ti: e.activation(out=ss[:, ti, 6:7], in_=ss[:, ti, 7:8], func=AF.Exp, scale=-0.5), r=[('ss', ti)], w=[('ss', ti)])
                    P.dve(lambda e, ti=ti, xf=xf: e.scalar_tensor_tensor(out=xf[:], in0=xf[:], scalar=ss[:, ti, 6:7], in1=fng[:], op0=ALU.mult, op1=ALU.mult), r=[('xfull', ti), ('ss', ti), 'fng'], w=[('xfull', ti)])
                    P.dma(xdst[t0:t0 + 128, :], xf[:], r=[('xfull', ti)], w=[('xdst', t0)], q='act')

        def loadz(bi):
            P.dma(zbs[bi % 2][:], zt[:, bi * KB:(bi + 1) * KB].rearrange("(c p) t -> p c t", p=128), w=['zb%d' % (bi % 2)])

        HEAD = 2
        loadz(0)
        emitU(0, range(16))
        for bi in range(NB):
            P.dma(wo[cn['ko'] % 2][:], wob[l, 0], w=['wo%d' % (cn['ko'] % 2)])
            if bi + 1 < NB:
                loadz(bi + 1)
                emitU(bi + 1, range(HEAD))
            emitO(bi)
            if bi + 1 < NB:
                emitU(bi + 1, range(HEAD, 16))
        P.flush()

    allc = list(range(96))
    kvc = list(range(R_K // 128, R_G // 128))
    plan = [prologue]
    ccn = {}
    for l in range(DEPTH):
        last = l == DEPTH - 1
        csrc = ctx_in if l == 0 else xres['c']
        xsrc = x_in if l == 0 else xres['x']
        if not last:
            plan.append(lambda l=l, csrc=csrc, xsrc=xsrc: phaseP(l, seq_x, xsrc, allc, precast=l, pre=(seq_c, csrc)))
            plan.append(lambda l=l: phaseF(l, seq_c))
            def qr_c(l=l):
                phaseQ(l, seq_c, noflush=True)
                phaseR(l, seq_c, False)
            plan.append(qr_c)
            plan.append(lambda l=l, csrc=csrc: phaseM(l, seq_c, csrc, xres['c'], False))
        else:
            plan.append(lambda l=l, csrc=csrc: phaseP(l, seq_c, csrc, kvc))
            plan.append(lambda l=l: phaseR(l, seq_c, True))
        if last:
            plan.append(lambda l=l, xsrc=xsrc: phaseP(l, seq_x, xsrc, allc, precast=l))
        def f1r0(l=l):
            ccn['ab'] = phaseF(l, seq_x, 'f1', noflush=True)
            ccn['st'] = phaseR0x(l)
        plan.append(f1r0)
        def f2q(l=l):
            P.capture()
            phaseF(l, seq_x, 'f2', ccn['ab'], noflush=True)
            Lf = P.end_capture()
            P.capture()
            phaseQ(l, seq_x, noflush=True, ccw=ccn['ab'][0])
            Lq = P.end_capture()
            P.replay(Lf, Lq)
            P.flush()
        plan.append(f2q)
        plan.append(lambda l=l: phaseRx(l, ccn['st']))
        plan.append(lambda l=l, xsrc=xsrc, last=last: phaseM(l, seq_x, xsrc, out if last else xres['x'], last))
    for ph in plan[:PHASE_LIMIT]:
        ph()
    if dbg:
        srcs = dict(projT_c=projT['c'], zT_c=zT['c'], xres_c=xres['c'], modrows=modrows, masks_d=masks_d, qdec_d=qdec_d,
                    projT_x=projT['x'], zT_x=zT['x'], xres_x=xres['x'])
        for k, ap in dbg.items():
            P.dma(ap, srcs[k], w=[('dbg', k)])
        P.flush()
    P.finish()
    return nc


def _blk(w, ncols):
    K, C = w.shape
    return np.ascontiguousarray(w.reshape(16, 128, C // ncols, ncols).transpose(2, 1, 0, 3))


def make_in_maps(x, c, ctx, c_ctx, w_ada, b_ada, norm_g, w_in, w_fourier, w_pool, pool_scale,
                 ret_decay_logit, w_up_fourier, w_up_pool, w_up_ret, w_out, final_norm_g):
    f = np.float32
    x = np.asarray(x, f); c = np.asarray(c, f); ctx = np.asarray(ctx, f); c_ctx = np.asarray(c_ctx, f)
    C = _consts()
    shared = {}
    wada_blk = np.stack([_blk(np.asarray(w_ada[l], f), 512) for l in range(DEPTH)])
    b_ada = np.asarray(b_ada, f)
    shared["norm_g"] = np.ascontiguousarray(np.asarray(norm_g, f))
    shared["w_in"] = np.stack([_blk(np.asarray(w_in[l], f), 512) for l in range(DEPTH)])
    shared["w_fourier"] = np.ascontiguousarray(np.asarray(w_fourier, f).transpose(0, 2, 1, 3))
    shared["w_pool"] = np.ascontiguousarray(np.asarray(w_pool, f).transpose(0, 2, 1, 3))
    shared["pool_scale"] = np.ascontiguousarray(np.asarray(pool_scale, f).reshape(DEPTH, 4, 128).transpose(0, 2, 1))
    shared["ret_decay_logit"] = np.ascontiguousarray(np.asarray(ret_decay_logit, f).reshape(32))
    wup = [np.concatenate([np.asarray(w_up_fourier[l], f), np.asarray(w_up_pool[l], f), np.asarray(w_up_ret[l], f)], axis=0) for l in range(DEPTH)]
    shared["w_up"] = np.stack([_blk(w, 128) for w in wup])
    shared["w_out"] = np.stack([_blk(np.asarray(w_out[l], f), 512) for l in range(DEPTH)])
    shared["final_norm_g"] = np.ascontiguousarray(np.asarray(final_norm_g, f))
    shared["cf"] = C['cf']
    shared["cb"] = C['cb']
    shared["invc_c"] = C['invc_c']
    shared["dftc_c"] = C['dftc_c']
    shared["dfts_c"] = C['dfts_c']
    half = []
    for r in range(2):
        sl = slice(r * NH, (r + 1) * NH)
        role = np.zeros((128, 4), np.float32)
        role[:, r] = 1.0
        half.append(dict(w_ada=np.ascontiguousarray(wada_blk[:, r * 6:(r + 1) * 6]),
                         b_ada=np.ascontiguousarray(b_ada[:, r * 3072:(r + 1) * 3072].reshape(-1)),
                         ropec=np.ascontiguousarray(C['ropec'][:, sl]), ropes=np.ascontiguousarray(C['ropes'][:, sl]),
                         invc_x=np.ascontiguousarray(C['invc_x'][:, sl]), dftc_x=np.ascontiguousarray(C['dftc_x'][:, sl]),
                         dfts_x=np.ascontiguousarray(C['dfts_x'][:, sl]), role=role))
    maps = []
    for core in range(NCORES):
        b, r = core // 2, core % 2
        m = dict(shared)
        m.update(half[r])
        m["x"] = np.ascontiguousarray(x[b, r * NH:(r + 1) * NH])
        m["ctx"] = np.ascontiguousarray(ctx[b])
        cc = np.stack([c[b], c_ctx], axis=-1)
        m["cT"] = np.ascontiguousarray(cc.reshape(16, 128, 2).transpose(1, 0, 2))
        maps.append(m)
    return maps


_NC = None


def kernel(**inputs):
    global _NC
    maps = make_in_maps(**inputs)
    if _NC is None:
        _NC = build()
    res = run_bass_kernel_spmd(_NC, maps, core_ids=list(range(NCORES)))
    outs = [np.asarray(r["out"], np.float32) for r in res.results]
    return np.stack([np.concatenate([outs[2 * b], outs[2 * b + 1]], axis=0) for b in range(NCORES // 2)], axis=0)
```

```python
import math
from contextlib import ExitStack

import ml_dtypes
import numpy as np

import concourse.bass as bass
import concourse.mybir as mybir
from concourse.bass_utils import run_bass_kernel_spmd

F32 = mybir.dt.float32
BF16 = mybir.dt.bfloat16
AF = mybir.ActivationFunctionType
ALU = mybir.AluOpType

EPOCH = 30000
RING = 8

D = 2048
NX = 4096
NH = 2048
NCX = 256
DEPTH = 2
INW = 12288
F_X, F_G, P_X, P_G, R_Q, R_K, R_V, R_G, MG = 0, 512, 1024, 1536, 2048, 3072, 4096, 5120, 6144
EPS = 1e-6
QS = 128.0 ** -0.5
NCORES = 8
PAIRS = [[0, 1], [2, 3], [4, 5], [6, 7]]

DEBUG = {}
PHASE_LIMIT = 1000
R_STAGE = 99
R_HEADS = 8


class Prog:
    ENGS = ['pe', 'act', 'dve', 'pool', 'sp']

    def __init__(self, nc):
        self.nc = nc
        self.gstack = ExitStack()
        self.pstack = ExitStack()
        self.sem_cache = {}
        self.tick = {e: 0 for e in self.ENGS}
        self.dcount = {e: 0 for e in self.ENGS}
        self.ring_last = {}
        self.last_tick = {}
        self._reset()
        self.nuniq = 0
        self.ncc = 0
        self._caps = []

    def _reset(self):
        self.ops = []
        self.last_w = {}
        self.readers = {}

    def sem(self, name):
        if name not in self.sem_cache:
            self.sem_cache[name] = self.gstack.enter_context(self.nc.semaphore(name))
        return self.sem_cache[name]

    def sbp(self, name, shape, dt):
        return self.gstack.enter_context(self.nc.sbuf_tensor("sb_" + name, list(shape), dt))

    def sb(self, name, shape, dt):
        self.nuniq += 1
        return self.pstack.enter_context(self.nc.sbuf_tensor("%s_%d" % (name, self.nuniq), list(shape), dt))

    def ps(self, name, shape, dt=F32):
        self.nuniq += 1
        return self.pstack.enter_context(self.nc.psum_tensor("%s_%d" % (name, self.nuniq), list(shape), dt))

    def dram(self, name, shape, dt):
        return self.nc.dram_tensor(name, list(shape), dt, kind="Internal").ap()

    def cc(self, kind, ins, outs, groups, r=(), w=()):
        name = 'cc_%d' % self.ncc
        self.ncc += 1
        self.op('pool', lambda e: e.collective_compute(kind, ALU.bypass, replica_groups=groups, ins=ins, outs=outs), r, w, False, (), name)
        return name

    def capture(self):
        self._caps.append([])

    def end_capture(self):
        return self._caps.pop()

    def replay(self, *lists):
        pos = [0] * len(lists)
        total = sum(len(l) for l in lists)
        for _ in range(total):
            best, bf = None, None
            for k, l in enumerate(lists):
                if pos[k] < len(l):
                    f = (pos[k] + 0.5) / len(l)
                    if bf is None or f < bf:
                        best, bf = k, f
            a = lists[best][pos[best]]
            pos[best] += 1
            self.op(*a)

    def op(self, eng, fn, r=(), w=(), dma=False, extra=(), cc=None):
        if self._caps:
            self._caps[-1].append((eng, fn, tuple(r), tuple(w), dma, tuple(extra), cc))
            return None
        i = len(self.ops)
        deps = set()
        for k in r:
            lw = self.last_w.get(k)
            if lw is not None:
                deps.add(lw)
        for k in w:
            lw = self.last_w.get(k)
            if lw is not None:
                deps.add(lw)
            for j in self.readers.get(k, ()):
                deps.add(j)
        deps.discard(i)
        self.ops.append(dict(eng=eng, fn=fn, dma=dma, deps=deps, r=tuple(r), w=tuple(w), extra=tuple(extra)))
        if cc is not None:
            self.ops[-1]['cc'] = cc
        for k in w:
            self.last_w[k] = i
            self.readers[k] = []
        for k in r:
            self.readers.setdefault(k, []).append(i)
        return i

    def pe(self, fn, r=(), w=()):
        return self.op('pe', fn, r, w)

    def act(self, fn, r=(), w=()):
        return self.op('act', fn, r, w)

    def dve(self, fn, r=(), w=()):
        return self.op('dve', fn, r, w)

    def pool(self, fn, r=(), w=()):
        return self.op('pool', fn, r, w)

    def dma(self, out, in_, r=(), w=(), q='sp', extra=()):
        return self.op(q, lambda e: e.dma_start(out=out, in_=in_), r, w, dma=True, extra=extra)

    def flush(self):
        nc = self.nc
        ops = self.ops
        n = len(ops)
        for i, o in enumerate(ops):
            keep = set()
            for j in o['deps']:
                p = ops[j]
                same = (p['eng'] == o['eng']) and (not p['dma']) and (not o['dma'])
                if same:
                    if o['eng'] == 'pe':
                        continue
                    if not (set(p['w']) & (set(o['r']) | set(o['w']))):
                        continue
                keep.add(j)
            o['deps'] = keep
        needed = [False] * n
        for o in ops:
            for j in o['deps']:
                needed[j] = True
        lastc = {}
        for i, o in enumerate(ops):
            if not o['dma'] and 'cc' not in o:
                lastc[o['eng']] = i
        for e, i in lastc.items():
            needed[i] = True
        ring_prev = {}
        for i, o in enumerate(ops):
            e = o['eng']
            if 'cc' in o:
                o['sig'] = (o['cc'], 1)
            elif o['dma']:
                k = self.dcount[e]
                self.dcount[e] += 1
                slot = k % RING
                o['sig'] = ('d_%s_%d' % (e, slot), 16 * (k // RING + 1))
                o['ring_wait'] = self.ring_last.get((e, slot))
                self.ring_last[(e, slot)] = o['sig']
            elif needed[i]:
                t = self.tick[e]
                self.tick[e] += 1
                o['sig'] = ('c_%s_%d' % (e, t // EPOCH), t % EPOCH + 1)
                self.last_tick[e] = o['sig']
            else:
                o['sig'] = None
        for o in ops:
            if o['sig'] is not None:
                self.sem(o['sig'][0])
        barrier = list(self.last_tick.values()) + list(self.ring_last.values())

        def run_engine(e, eh):
            waited = {}

            def wait(sn, v):
                if waited.get(sn, 0) >= v:
                    return
                eh.wait_ge(self.sem(sn), v)
                waited[sn] = v

            for i, o in enumerate(ops):
                if o['eng'] != e:
                    continue
                need = {}
                for j in o['deps']:
                    s = ops[j]['sig']
                    if need.get(s[0], 0) < s[1]:
                        need[s[0]] = s[1]
                if o['dma'] and o['ring_wait'] is not None:
                    s = o['ring_wait']
                    if need.get(s[0], 0) < s[1]:
                        need[s[0]] = s[1]
                for s in o['extra']:
                    if need.get(s[0], 0) < s[1]:
                        need[s[0]] = s[1]
                for sn, v in need.items():
                    wait(sn, v)
                ins = o['fn'](eh)
                if 'cc' in o:
                    ins.then_inc(self.sem(o['sig'][0]))
                elif o['sig'] is not None:
                    ins.then_inc(self.sem(o['sig'][0]), 16 if o['dma'] else 1)
            for sn, v in barrier:
                wait(sn, v)

        with nc.Block() as block:
            @block.sync
            def _(eh):
                run_engine('sp', eh)

            @block.tensor
            def _(eh):
                run_engine('pe', eh)

            @block.vector
            def _(eh):
                run_engine('dve', eh)

            @block.scalar
            def _(eh):
                run_engine('act', eh)

            @block.gpsimd
            def _(eh):
                run_engine('pool', eh)
        self.pstack.close()
        self.pstack = ExitStack()
        self._reset()

    def finish(self):
        self.gstack.close()


def bc(ap1d, n=128):
    return ap1d.partition_broadcast(n)


_CONST = {}


def _consts():
    if _CONST:
        return _CONST
    bf = ml_dtypes.bfloat16
    j = np.arange(128)
    diff = j[None, :] - j[:, None]
    cf = np.zeros((128, 704), np.float32)
    cf[:, 0:128] = np.maximum(diff, 0)
    cf[:, 128:256] = np.maximum(-diff, 0)
    cf[:, 256:384] = np.eye(128) * QS
    cf[:, 384:512] = (j + 1)[None, :]
    cf[:, 512:640] = (128 - j)[None, :]
    jf = np.zeros((128, 32), np.float32)
    for l in range(2):
        jf[:, l * 16:l * 16 + 8] = (127 - j)[:, None]
        jf[:, l * 16 + 8:l * 16 + 16] = j[:, None]
    cf[:, 640:672] = jf
    cchunk = np.arange(16)
    cf[:, 672:688] = (NH - 1) - (128 * cchunk[None, :] + j[:, None])
    cf[:, 688:704] = 128 * cchunk[None, :] + j[:, None]
    _CONST['cf'] = cf
    cb = np.zeros((128, 512), np.float32)
    cb[:, 0:128] = np.eye(128)
    perm = np.zeros((128, 128), np.float32)
    for m in range(128):
        partner = m + 32 if (m % 64) < 32 else m - 32
        perm[partner, m] = 1.0
    cb[:, 128:256] = perm
    ang = 2 * np.pi * np.outer(j, j) / 128.0
    cb[:, 256:384] = np.cos(ang)
    cb[:, 384:512] = np.sin(ang)
    _CONST['cb'] = cb.astype(bf)
    n = np.arange(NX)
    rows, cols = n // 64, n % 64
    inv = 10000.0 ** (-np.arange(32, dtype=np.float64) / 32)
    rc = np.zeros((128, NX), np.float64)
    rs = np.zeros((128, NX), np.float64)
    for d in range(128):
        pos = rows if d < 64 else cols
        a = pos * inv[d % 32]
        rc[d] = np.cos(a)
        rs[d] = np.sin(a) * (-1.0 if (d % 64) < 32 else 1.0)
    _CONST['ropec'] = rc.astype(np.float32)
    _CONST['ropes'] = rs.astype(np.float32)
    for nm, N in (('x', NX), ('c', NCX)):
        t = np.arange(N)
        ic = np.zeros((4, N), np.float32)
        for g, w in enumerate((2, 4, 8, 16)):
            lo = np.clip(t - w // 2, 0, N)
            hi = np.clip(t + w // 2, 0, N)
            ic[g] = 1.0 / (hi - lo)
        _CONST['invc_' + nm] = ic
        k = np.arange(N, dtype=np.int64)
        m = (np.outer(k, k) % N).astype(np.float64)
        a = 2 * np.pi * m / N
        sc = 1.0 / math.sqrt(N * 128.0)
        _CONST['dftc_' + nm] = (np.cos(a) * sc).astype(bf)
        _CONST['dfts_' + nm] = (-np.sin(a) * sc).astype(bf)
    return _CONST


class Seq:
    def __init__(self, name, N, who, rope):
        self.name, self.N, self.who, self.rope = name, N, who, rope


def build():
    nc = bass.Bass("TRN2", target_bir_lowering=False)
    P = Prog(nc)

    def inp(name, shape, dt=F32):
        return nc.dram_tensor(name, list(shape), dt, kind="ExternalInput").ap()

    x_in = inp("x", [NH, D])
    ctx_in = inp("ctx", [NCX, D])
    cT_in = inp("cT", [128, 16, 2])
    wada_in = inp("w_ada", [DEPTH, 6, 128, 16, 512])
    bada_in = inp("b_ada", [DEPTH * 3072])
    ng_in = inp("norm_g", [DEPTH, D])
    win_in = inp("w_in", [DEPTH, 24, 128, 16, 512])
    wf_in = inp("w_fourier", [DEPTH, 128, 4, 128])
    wp_in = inp("w_pool", [DEPTH, 128, 4, 128])
    psc_in = inp("pool_scale", [DEPTH, 128, 4])
    rdl_in = inp("ret_decay_logit", [32])
    wup_in = inp("w_up", [DEPTH, 16, 128, 16, 128])
    wout_in = inp("w_out", [DEPTH, 4, 128, 16, 512])
    fng_in = inp("final_norm_g", [D])
    cf_in = inp("cf", [128, 704])
    cb_in = inp("cb", [128, 512], BF16)
    ropec_in = inp("ropec", [128, NH])
    ropes_in = inp("ropes", [128, NH])
    role_in = inp("role", [128, 4])
    invc_in = {'x': inp("invc_x", [4, NH]), 'c': inp("invc_c", [4, NCX])}
    dftc_in = {'x': inp("dftc_x", [NX, NH], BF16), 'c': inp("dftc_c", [NCX, NCX], BF16)}
    dfts_in = {'x': inp("dfts_x", [NX, NH], BF16), 'c': inp("dfts_c", [NCX, NCX], BF16)}
    out = nc.dram_tensor("out", [NH, D], F32, kind="ExternalOutput").ap()

    modrows = P.dram("modrows", [DEPTH, 3, 2, D], F32)
    projT = {'x': P.dram("projT_x", [INW, NH], BF16), 'c': P.dram("projT_c", [INW, NCX], BF16)}
    zT = {'x': P.dram("zT_x", [D, NH], BF16), 'c': P.dram("zT_c", [D, NCX], BF16)}
    xres = {'x': P.dram("xres_x", [NH, D], F32), 'c': P.dram("xres_c", [NCX, D], F32)}
    wub = P.dram("wub", [DEPTH, 16, 128, 16, 128], BF16)
    wob = P.dram("wob", [DEPTH, 4, 128, 16, 512], BF16)
    masks_d = P.dram("masks_d", [128, 16, 128], F32)
    ab_src = [P.dram("ab_src%d" % i, [512, 1024], BF16) for i in range(4)]
    ab_all = [P.dram("ab_all%d" % i, [1024, 1024], BF16) for i in range(4)]
    ed_src = P.dram("ed_src", [8, 1024], BF16)
    ed_all = P.dram("ed_all", [16, 1024], BF16)
    md_src = P.dram("md_src", [4, 3072], F32)
    md_all = P.dram("md_all", [8, 3072], F32)
    st_src = P.dram("st_src", [16 * 128, 128], F32)
    st_all = P.dram("st_all", [2 * 16 * 128, 128], F32)
    rsc = P.dram("rsc", [8, 4, 128, NH], BF16)
    qdec_d = P.dram("qdec_d", [128, 32, 128], F32)
    dbg = {}
    for k, shp in DEBUG.items():
        dbg[k] = nc.dram_tensor("dbg_" + k, list(shp[0]), shp[1], kind="ExternalOutput").ap()

    cf = P.sbp("cf", [128, 704], F32)
    cb = P.sbp("cb", [128, 512], BF16)
    kdec = P.sbp("kdec", [128, 32], F32)
    dch = P.sbp("dch", [128, 32], F32)
    s0 = P.sbp("s0", [128, 16, 128], F32)
    role = P.sbp("role", [128, 4], F32)
    coef = P.sbp("coef", [128, 32], F32)
    tabk = P.sbp("tabk", [128, 32, 16], F32)
    Dpos, Dneg, Iq, ipl1, imn, jfac = cf[:, 0:128], cf[:, 128:256], cf[:, 256:384], cf[:, 384:512], cf[:, 512:640], cf[:, 640:672]
    ident, perm, cs128 = cb[:, 0:128], cb[:, 128:256], cb[:, 256:512]

    seq_c = Seq('c', NCX, 1, False)
    seq_x = Seq('x', NH, 0, True)

    def cast_upout(l):
        for dc in range(0, 16, 4):
            P.dma(wub[l, dc:dc + 4], wup_in[l, dc:dc + 4], w=[('wub', l, dc)], q='pool')
        for cg in range(4):
            P.dma(wob[l, cg], wout_in[l, cg], w=[('wob', l, cg)], q='pool')

    def prologue():
        P.dma(cf[:], cf_in, w=['cf'])
        P.dma(cb[:], cb_in, w=['cb'])
        cT = P.sb("cT", [128, 16, 2], F32)
        scb = P.sb("scb", [128, 16, 2], BF16)
        HW_ = 3072
        badah = P.sb("badah", [2, DEPTH * HW_], F32)
        ngb = P.sb("ngb", [2, D], F32)
        modh = P.sb("modh", [2, DEPTH, HW_], F32)
        mod = P.sb("mod", [2, 6144], F32)
        wab = [P.sb("wab%d" % i, [128, 16, 512], BF16) for i in range(2)]
        pa = P.ps("pa", [128, 512])
        P.dma(cT[:], cT_in, w=['cT'])
        P.act(lambda e: e.activation(out=scb[:], in_=cT[:], func=AF.Silu), r=['cT'], w=['scb'])
        P.dma(badah[:], bc(bada_in, 2), w=['badah'])
        k = 0
        for l in range(DEPTH):
            for cg in range(6):
                wa = wab[k % 2]
                wr = 'wab%d' % (k % 2)
                k += 1
                P.dma(wa[:], wada_in[l, cg], w=[wr], q='pool')
                for kc in range(16):
                    P.pe(lambda e, wa=wa, kc=kc: e.matmul(pa[0:2, :], lhsT=scb[:, kc, :], rhs=wa[:, kc, :], start=(kc == 0), stop=(kc == 15)),
                         r=['scb', wr], w=['pa'])
                P.dve(lambda e, cg=cg, l=l: e.tensor_tensor(out=modh[:, l, cg * 512:(cg + 1) * 512], in0=pa[0:2, :], in1=badah[:, l * HW_ + cg * 512:l * HW_ + (cg + 1) * 512], op=ALU.add),
                      r=['pa', 'badah'], w=['modh'])
        P.dma(md_src.rearrange("(l w) c -> w l c", w=2), modh[:], r=['modh'], w=['md_src'])
        mdn = P.cc("AllGather", [md_src], [md_all], PAIRS, r=['md_src'], w=['md_all'])
        for l in range(DEPTH):
            P.dma(ngb[:], bc(ng_in[l], 2), w=['ngb'])
            for r_ in range(2):
                P.dma(mod[:, r_ * HW_:(r_ + 1) * HW_], md_all[r_ * 4 + l * 2:r_ * 4 + l * 2 + 2, :], r=['md_all'], w=['mod'], extra=[(mdn, 1)])
            P.dve(lambda e: e.scalar_tensor_tensor(out=mod[:, 2048:4096], in0=mod[:, 2048:4096], scalar=1.0, in1=ngb[:], op0=ALU.add, op1=ALU.mult),
                  r=['mod', 'ngb'], w=['mod'])
            for kind in range(3):
                P.dma(modrows[l, kind], mod[:, kind * 2048:(kind + 1) * 2048], r=['mod'], w=[('modrows', l, kind)])
        lgt = P.sb("lgt", [128, 32], F32)
        lg = P.sb("lg", [128, 32], F32)
        tmp32 = P.sb("tmp32", [128, 32], F32)
        masks = P.sb("masks", [128, 16, 128], F32)
        qdec = P.sb("qdec", [128, 32, 128], F32)
        tm = [P.sb("tm%d" % i, [128, 128], F32) for i in range(2)]
        P.dma(lgt[:], bc(rdl_in), w=['lgt'])
        P.act(lambda e: e.activation(out=tmp32[:], in_=lgt[:], func=AF.Exp, scale=-1.0), r=['lgt'], w=['tmp32'])
        P.act(lambda e: e.activation(out=lgt[:], in_=tmp32[:], func=AF.Ln, bias=1.0), r=['tmp32'], w=['lgt'])
        P.dve(lambda e: e.tensor_scalar(out=lg[:], in0=lgt[:], scalar1=-1.0, scalar2=None, op0=ALU.mult), r=['lgt'], w=['lg'])
        P.dve(lambda e: e.tensor_tensor(out=tmp32[:], in0=lg[:], in1=jfac, op=ALU.mult), r=['lg', 'cf', 'tmp32'], w=['tmp32'])
        P.act(lambda e: e.activation(out=kdec[:], in_=tmp32[:], func=AF.Exp), r=['tmp32'], w=['kdec'])
        P.act(lambda e: e.activation(out=dch[:], in_=lg[:], func=AF.Exp, scale=128.0), r=['lg'], w=['dch'])
        for idx_ in range(32):
            esrc = cf[:, 672:688] if (idx_ % 16) < 8 else cf[:, 688:704]
            P.act(lambda e, idx_=idx_, esrc=esrc: e.activation(out=tabk[:, idx_, :], in_=esrc, func=AF.Exp, scale=lg[:, idx_:idx_ + 1]), r=['lg', 'cf'], w=['tabk'])
        P.dma(role[:], role_in, w=['role'])
        P.act(lambda e: e.activation(out=tmp32[:], in_=lg[:], func=AF.Exp, scale=float(NH)), r=['lg', 'kdec'], w=['tmp32'])
        for l in range(DEPTH):
            P.dve(lambda e, l=l: e.tensor_scalar(out=coef[:, l * 16:l * 16 + 8], in0=tmp32[:, l * 16:l * 16 + 8], scalar1=role[:, 1:2], scalar2=role[:, 0:1], op0=ALU.mult, op1=ALU.add), r=['tmp32', 'role'], w=['coef'])
            P.dve(lambda e, l=l: e.tensor_scalar(out=coef[:, l * 16 + 8:l * 16 + 16], in0=tmp32[:, l * 16 + 8:l * 16 + 16], scalar1=role[:, 0:1], scalar2=role[:, 1:2], op0=ALU.mult, op1=ALU.add), r=['tmp32', 'role'], w=['coef'])
        for l in range(DEPTH):
            for h in range(8):
                t = tm[h % 2]
                tr = 'tm%d' % (h % 2)
                i0, i1 = l * 16 + h, l * 16 + 8 + h
                P.dve(lambda e, t=t, i0=i0: e.tensor_scalar(out=t[:], in0=Dpos, scalar1=lg[:, i0:i0 + 1], scalar2=None, op0=ALU.mult), r=['lg', 'cf'], w=[tr])
                P.dve(lambda e, t=t, i1=i1: e.scalar_tensor_tensor(out=t[:], in0=Dneg, scalar=lg[:, i1:i1 + 1], in1=t[:], op0=ALU.mult, op1=ALU.add), r=['lg', 'cf', tr], w=[tr])
                P.act(lambda e, t=t: e.activation(out=t[:], in_=t[:], func=AF.Exp), r=[tr], w=[tr])
                P.dve(lambda e, t=t, l=l, h=h: e.scalar_tensor_tensor(out=masks[:, l * 8 + h, :], in0=t[:], scalar=QS, in1=Iq, op0=ALU.mult, op1=ALU.add), r=[tr, 'cf'], w=['masks'])
                P.act(lambda e, i0=i0: e.activation(out=qdec[:, i0, :], in_=ipl1, func=AF.Exp, scale=lg[:, i0:i0 + 1]), r=['lg', 'cf'], w=['qdec'])
                P.act(lambda e, i1=i1: e.activation(out=qdec[:, i1, :], in_=imn, func=AF.Exp, scale=lg[:, i1:i1 + 1]), r=['lg', 'cf'], w=['qdec'])
        P.dve(lambda e: e.tensor_scalar(out=qdec[:], in0=qdec[:], scalar1=QS, scalar2=None, op0=ALU.mult), r=['qdec'], w=['qdec'])
        P.dma(masks_d, masks[:], r=['masks'], w=['masks_d'])
        P.dma(qdec_d, qdec[:], r=['qdec'], w=['qdec_d'])
        P.flush()

    def phaseP(l, sq, xsrc, chunks, precast=None, pre=None):
        segs = ([(pre[0], pre[1])] if pre is not None else []) + [(sq, xsrc)]
        N = sum(sg_[0].N for sg_ in segs)
        TS = N
        whos = sorted(set(sg_[0].who for sg_ in segs))
        geffs = {w_: P.sb("geff%d" % w_, [128, D], F32) for w_ in whos}
        shifts = {w_: P.sb("shift%d" % w_, [128, D], F32) for w_ in whos}
        hT = P.sb("hT", [128, 16, TS], BF16)
        xb = [P.sb("xb%d" % i, [128, D], F32) for i in range(2)]
        t1s = [P.sb("t1%d" % i, [128, D], F32) for i in range(2)]
        hbs = [P.sb("hb%d" % i, [128, D], BF16) for i in range(2)]
        junk = P.sb("junk", [128, D], BF16)
        sts = [P.sb("st%d" % i, [128, 4], F32) for i in range(2)]
        wbuf = [P.sb("wbuf%d" % i, [128, 16, 512], BF16) for i in range(2)]
        stg = [P.sb("stg%d" % i, [128, TS], BF16) for i in range(2)]
        pT = P.ps("pT", [128, 2048], BF16)
        pb = [P.ps("pb%d" % i, [128, 512]) for i in range(4)]
        tile_src = []
        for (sq_, src_) in segs:
            for t_ in range(sq_.N // 128):
                tile_src.append((src_, t_ * 128, sq_.who))
        for w_ in sorted(whos, key=lambda w_: min(i_ for i_, ts_ in enumerate(tile_src) if ts_[2] == w_)):
            P.dma(geffs[w_][:], bc(modrows[l, 1, w_]), w=['geff%d' % w_])
            P.dma(shifts[w_][:], bc(modrows[l, 0, w_]), w=['shift%d' % w_])
        groups = sorted(set(c // 4 for c in chunks))
        kx = kw = ks = kp = 0
        pieces = []
        if precast is not None:
            for dc_ in range(16):
                pieces.append(lambda dc_=dc_: P.dma(wub[precast, dc_], wup_in[precast, dc_], w=[('wub', precast, dc_)], q='pool'))
                cg_, q_ = dc_ // 4, dc_ % 4
                pieces.append(lambda cg_=cg_, q_=q_: P.dma(wob[precast, cg_][q_ * 32:(q_ + 1) * 32], wout_in[precast, cg_][q_ * 32:(q_ + 1) * 32], w=[('wob', precast, cg_, q_)], q='pool'))
        for sb0 in range(0, N, TS):
            stageB_prev = None
            for tt in range(TS // 128):
                src_, t0, who_ = tile_src[tt]
                geff, shift = geffs[who_], shifts[who_]
                gn, sn = 'geff%d' % who_, 'shift%d' % who_
                xt = xb[kx % 2]
                xr = 'xb%d' % (kx % 2)
                q2 = kx % 2
                t1, hb, st = t1s[q2], hbs[q2], sts[q2]
                t1n, hbn, stn = 't1%d' % q2, 'hb%d' % q2, 'st%d' % q2
                kx += 1
                P.capture()
                P.dma(xt[:], src_[t0:t0 + 128, :], w=[xr])
                P.act(lambda e, xt=xt, st=st: e.activation(out=junk[:], in_=xt[:], func=AF.Square, accum_out=st[:, 0:1]), r=[xr], w=['junk', stn + 'a'])
                P.act(lambda e, st=st: e.activation(out=st[:, 1:2], in_=st[:, 0:1], func=AF.Ln, scale=1.0 / D, bias=EPS), r=[stn + 'a'], w=[stn + 'b'])
                P.act(lambda e, st=st: e.activation(out=st[:, 2:3], in_=st[:, 1:2], func=AF.Exp, scale=-0.5), r=[stn + 'b'], w=[stn + 'c'])
                P.dve(lambda e, xt=xt, st=st, t1=t1, geff=geff: e.scalar_tensor_tensor(out=t1[:], in0=xt[:], scalar=st[:, 2:3], in1=geff[:], op0=ALU.mult, op1=ALU.mult),
                      r=[xr, stn + 'c', gn], w=[t1n])
                XS = 640
                P.dve(lambda e, t1=t1, hb=hb, shift=shift: e.tensor_tensor(out=hb[:, 0:XS], in0=t1[:, 0:XS], in1=shift[:, 0:XS], op=ALU.add), r=[t1n, sn], w=[hbn + 'a'])
                P.pool(lambda e, t1=t1, hb=hb, shift=shift: e.tensor_tensor(out=hb[:, XS:D], in0=t1[:, XS:D], in1=shift[:, XS:D], op=ALU.add), r=[t1n, sn], w=[hbn + 'b'])
                stageA = P.end_capture()
                if stageB_prev is None:
                    P.replay(stageA)
                else:
                    P.replay(stageA, stageB_prev)
                P.capture()
                for kc in range(16):
                    P.pe(lambda e, kc=kc, hb=hb: e.transpose(out=pT[:, kc * 128:(kc + 1) * 128], in_=hb[:, kc * 128:(kc + 1) * 128], identity=ident),
                         r=[hbn + 'a', hbn + 'b', 'cb'], w=['pT'])
                P.act(lambda e, tt=tt: e.activation(out=hT[:, 0:8, tt * 128:(tt + 1) * 128], in_=pT[:, 0:1024].rearrange("p (c n) -> p c n", c=8), func=AF.Copy),
                      r=['pT'], w=[('hT', tt)])
                P.dve(lambda e, tt=tt: e.tensor_copy(out=hT[:, 8:16, tt * 128:(tt + 1) * 128], in_=pT[:, 1024:2048].rearrange("p (c n) -> p c n", c=8)),
                      r=['pT'], w=[('hT', tt)])
                stageB_prev = P.end_capture()
            P.replay(stageB_prev)
            hTr = [('hT', tt) for tt in range(TS // 128)]
            for cg in groups:
                wb = wbuf[kw % 2]
                wr = 'wbuf%d' % (kw % 2)
                kw += 1
                P.dma(wb[:], win_in[l, cg], w=[wr], q='pool')
                if precast is not None:
                    for _ in range(2 if kw <= 8 else 1):
                        if pieces:
                            pieces.pop(0)()
                for cc in range(4):
                    chunk = cg * 4 + cc
                    if chunk not in chunks:
                        continue
                    c0 = chunk * 128
                    sg = stg[ks % 2]
                    sr = 'stg%d' % (ks % 2)
                    ks += 1
                    if (F_G <= c0 < P_X) or (P_G <= c0 < R_Q) or (R_G <= c0 < MG):
                        fn = AF.Silu
                    elif c0 >= MG:
                        fn = AF.Sigmoid
                    else:
                        fn = None
                    for tb in range(0, TS, 512):
                        tw = min(512, TS - tb)
                        pp = pb[kp % 4]
                        pr = 'pb%d' % (kp % 4)
                        kp += 1
                        for kc in range(16):
                            P.pe(lambda e, pp=pp, wb=wb, kc=kc, cc=cc, tb=tb, tw=tw: e.matmul(pp[:, 0:tw], lhsT=wb[:, kc, cc * 128:(cc + 1) * 128], rhs=hT[:, kc, tb:tb + tw], start=(kc == 0), stop=(kc == 15)),
                                 r=[wr] + hTr[tb // 128:(tb + tw) // 128], w=[pr])
                        if fn is None:
                            P.dve(lambda e, pp=pp, sg=sg, tb=tb, tw=tw: e.tensor_copy(out=sg[:, tb:tb + tw], in_=pp[:, 0:tw]), r=[pr], w=[sr])
                        else:
                            P.act(lambda e, pp=pp, sg=sg, tb=tb, tw=tw, fn=fn: e.activation(out=sg[:, tb:tb + tw], in_=pp[:, 0:tw], func=fn), r=[pr], w=[sr])
                    o_ = 0
                    for (sq_, src_) in segs:
                        P.dma(projT[sq_.name][c0:c0 + 128, 0:sq_.N], sg[:, o_:o_ + sq_.N], r=[sr], w=[('projT', sq_.name, chunk)])
                        o_ += sq_.N
        P.flush()

    def phaseF(l, sq, mode='all', ccw=None, noflush=False):
        N = sq.N
        nloc = N // 128
        nch = nloc if mode != 'f2' else 2 * nloc
        KB = min(N, 512)
        pj = projT[sq.name]
        zt = zT[sq.name]
        AB = P.sb("AB", [128, nch, 4, 256], BF16)
        if mode == 'f2':
            ex = [(nm_, 1) for nm_ in ccw]
            for r_ in range(2):
                for k_ in range(4):
                    c0_ = r_ * nloc + k_ * 4
                    P.dma(AB[:, c0_:c0_ + 4].rearrange("p c g f -> p c (g f)"), ab_all[k_][r_ * 512:(r_ + 1) * 512].rearrange("(c p) f -> p c f", p=128),
                          w=[('ABl', r_, k_)], extra=ex)
        fxb = [P.sb("fxb%d" % i, [128, N], BF16) for i in range(2)]
        wF = P.sb("wF", [128, 4, 128], BF16)
        HC = min(nch, 16)
        NDB = 3 if mode == 'f2' else (1 if mode == 'f1' else 4)
        dbuf = [P.sb("dbuf%d" % i, [128, HC if mode != 'f1' else 1, KB], BF16) for i in range(NDB)]
        mx = [P.sb("mx%d" % i, [128, KB], BF16) for i in range(4)]
        fg = [P.sb("fg%d" % i, [128, KB], BF16) for i in range(2)]
        zf = [P.sb("zf%d" % i, [128, KB], BF16) for i in range(2)]
        pf = [P.ps("pf%d" % i, [128, 512]) for i in range(2)] if mode != 'f2' else None
        pm = [P.ps("pm%d" % i, [128, 512]) for i in range(4)] if mode != 'f1' else None
        py = [P.ps("py%d" % i, [128, 512]) for i in range(2)] if mode != 'f1' else None
        if mode != 'f1':
            P.dma(wF[:], wf_in[l], w=['wF'], q='pool')
        ke = 0
        for g in (range(4) if mode != 'f2' else ()):
            fx = fxb[g % 2]
            fr = 'fxb%d' % (g % 2)
            P.dma(fx[:], pj[F_X + g * 128:F_X + (g + 1) * 128, :], w=[fr])
            for ch in range(0, nch, 2):
                pp = pf[ke % 2]
                pr = 'pf%d' % (ke % 2)
                for s in range(2):
                    P.pe(lambda e, pp=pp, fx=fx, ch=ch, s=s: e.matmul(pp[:, s * 256:(s + 1) * 256], lhsT=fx[:, (ch + s) * 128:(ch + s + 1) * 128], rhs=cs128, start=True, stop=True),
                         r=[fr, 'cb'], w=[pr])
                if ke % 2 == 0:
                    P.act(lambda e, pp=pp, ch=ch, g=g: e.activation(out=AB[:, ch:ch + 2, g, :], in_=pp[:].rearrange("p (s n) -> p s n", s=2), func=AF.Copy), r=[pr], w=[('AB', g)])
                else:
                    P.dve(lambda e, pp=pp, ch=ch, g=g: e.tensor_copy(out=AB[:, ch:ch + 2, g, :], in_=pp[:].rearrange("p (s n) -> p s n", s=2)), r=[pr], w=[('AB', g)])
                ke += 1
        ABr = [('AB', g) for g in range(4)]
        if mode == 'f2':
            ABr = [('ABl', r_, k_) for r_ in range(2) for k_ in range(4)]
        if mode == 'f1':
            names = []
            ev = ed_src.rearrange("r (c j) -> (r c) j", j=16)
            P.dma(ev[:, 0:8], pj[P_X:P_X + 512, 0:8], w=['ed_src0'])
            P.dma(ev[:, 8:16], pj[P_X:P_X + 512, N - 8:N], w=['ed_src1'])
            names.append(P.cc("AllGather", [ed_src], [ed_all], PAIRS, r=['ed_src0', 'ed_src1'], w=['ed_all']))
            for k_ in range(4):
                P.dma(ab_src[k_].rearrange("(c p) f -> p c f", p=128), AB[:, k_ * 4:(k_ + 1) * 4].rearrange("p c g f -> p c (g f)"), r=ABr, w=[('ab_src', k_)])
                names.append(P.cc("AllGather", [ab_src[k_]], [ab_all[k_]], PAIRS, r=[('ab_src', k_)], w=[('ab_all', k_)]))
            if not noflush:
                P.flush()
            return names
        def emit_f3(kb_):
            k0_ = kb_ * KB
            for g in range(4):
                i = (kb_ * 4 + g) % 2
                P.dma(fg[i][:], pj[F_G + g * 128:F_G + (g + 1) * 128, k0_:k0_ + KB], w=['fg%d' % i])
                P.pe(lambda e, g=g, i=i: e.matmul(py[i][:, 0:KB], lhsT=wF[:, g, :], rhs=mx[g][:], start=True, stop=True), r=['wF', 'mx%d' % g], w=['py%d' % i])
                P.dve(lambda e, i=i: e.tensor_tensor(out=zf[i][:], in0=py[i][:, 0:KB], in1=fg[i][:], op=ALU.mult), r=['py%d' % i, 'fg%d' % i], w=['zf%d' % i])
                P.dma(zt[g * 128:(g + 1) * 128, k0_:k0_ + KB], zf[i][:], r=['zf%d' % i], w=[('zT', g, kb_)], q='act')

        kd = 0
        for kb in range(N // KB):
            k0 = kb * KB
            first = True
            nsteps = 2 * nch
            step = 0
            for tbl in range(2):
                src = (dftc_in if tbl == 0 else dfts_in)[sq.name]
                for hf in range(nch // HC):
                    db = dbuf[kd % NDB]
                    dr = 'dbuf%d' % (kd % NDB)
                    kd += 1
                    P.dma(db[:], src[hf * HC * 128:(hf + 1) * HC * 128, k0:k0 + KB].rearrange("(c p) k -> p c k", p=128), w=[dr])
                    for c in range(HC):
                        ch = hf * HC + c
                        for g in range(4):
                            P.pe(lambda e, g=g, ch=ch, tbl=tbl, db=db, c=c, step=step: e.matmul(pm[g][:, 0:KB], lhsT=AB[:, ch, g, tbl * 128:(tbl + 1) * 128], rhs=db[:, c, :], start=(step == 0), stop=(step == nsteps - 1)),
                                 r=[dr] + ABr, w=['pm%d' % g])
                        step += 1
                    if tbl == 0 and hf == 0 and kb > 0:
                        emit_f3(kb - 1)
            for g in range(4):
                if g % 2 == 0 or mode == 'f2':
                    P.act(lambda e, g=g: e.activation(out=mx[g][:], in_=pm[g][:, 0:KB], func=AF.Copy), r=['pm%d' % g], w=['mx%d' % g])
                else:
                    P.dve(lambda e, g=g: e.tensor_copy(out=mx[g][:], in_=pm[g][:, 0:KB]), r=['pm%d' % g], w=['mx%d' % g])
        emit_f3(N // KB - 1)
        if not noflush:
            P.flush()

    def phaseQ(l, sq, noflush=False, ccw=None):
        N = sq.N
        exq = [(ccw, 1)] if ccw is not None else []
        KB = min(N, 512)
        pj = projT[sq.name]
        zt = zT[sq.name]
        pxb = P.sb("pxb", [128, N + 16], BF16)
        X = P.sb("X", [128, N + 16], F32)
        A = P.sb("A", [128, N + 16], F32)
        B = P.sb("B", [128, N + 16], F32)
        invc = P.sb("invc", [128, N], F32)
        pooled = P.sb("pooled", [128, N], BF16)
        pg = P.sb("pg", [128, N], BF16)
        zp = P.sb("zp", [128, N], BF16)
        wP = P.sb("wP", [128, 4, 128], BF16)
        pscale = P.sb("pscale", [128, 4], F32)
        hl = P.sb("hl", [128, 8], BF16)
        hr = P.sb("hr", [128, 8], BF16)
        py = [P.ps("py%d" % i, [128, 512]) for i in range(2)]
        P.dma(wP[:], wp_in[l], w=['wP'], q='pool')
        P.dma(pscale[:], psc_in[l], w=['pscale'])
        P.pool(lambda e: e.memset(pxb[:], 0.0), w=['pxb'])
        k = 0
        for g in range(4):
            P.dma(pxb[:, 8:N + 8], pj[P_X + g * 128:P_X + (g + 1) * 128, :], r=[], w=['pxb'])
            P.dma(pg[:], pj[P_G + g * 128:P_G + (g + 1) * 128, :], w=['pg'])
            P.dma(invc[:], bc(invc_in[sq.name][g]), w=['invc'])
            P.dve(lambda e: e.tensor_copy(out=X[:], in_=pxb[:]), r=['pxb'], w=['X'])
            if sq.name == 'x':
                G0 = ed_all[0:8].rearrange("r (c j) -> (r c) j", j=16)
                G1 = ed_all[8:16].rearrange("r (c j) -> (r c) j", j=16)
                P.dma(hl[:], G0[g * 128:(g + 1) * 128, 8:16], w=['hl'], extra=exq)
                P.dma(hr[:], G1[g * 128:(g + 1) * 128, 0:8], w=['hr'], extra=exq)
                P.dve(lambda e: e.tensor_scalar(out=X[:, 0:8], in0=hl[:], scalar1=role[:, 1:2], scalar2=None, op0=ALU.mult), r=['hl', 'role'], w=['X'])
                P.dve(lambda e: e.tensor_scalar(out=X[:, N + 8:N + 16], in0=hr[:], scalar1=role[:, 0:1], scalar2=None, op0=ALU.mult), r=['hr', 'role'], w=['X'])
            P.dve(lambda e: e.tensor_tensor(out=A[:, 1:N + 16], in0=X[:, 0:N + 15], in1=X[:, 1:N + 16], op=ALU.add), r=['X'], w=['A'])
            cur, curr, oth, othr = A, 'A', B, 'B'
            lo, hi = 1, N + 16
            for s in range(g):
                sh = 1 << s
                P.dve(lambda e, cur=cur, oth=oth, lo=lo, hi=hi, sh=sh: e.tensor_tensor(out=oth[:, lo + sh:hi - sh], in0=cur[:, lo:hi - 2 * sh], in1=cur[:, lo + 2 * sh:hi], op=ALU.add),
                      r=[curr], w=[othr])
                lo, hi = lo + sh, hi - sh
                cur, curr, oth, othr = oth, othr, cur, curr
            P.dve(lambda e, cur=cur, oth=oth: e.tensor_tensor(out=oth[:, 8:N + 8], in0=cur[:, 8:N + 8], in1=invc[:], op=ALU.mult), r=[curr, 'invc'], w=[othr])
            P.dve(lambda e, oth=oth: e.tensor_tensor(out=pooled[:], in0=oth[:, 8:N + 8], in1=X[:, 8:N + 8], op=ALU.subtract), r=[othr, 'X'], w=['pooled'])
            for tb in range(0, N, KB):
                i = k % 2
                k += 1
                P.pe(lambda e, g=g, i=i, tb=tb: e.matmul(py[i][:, 0:KB], lhsT=wP[:, g, :], rhs=pooled[:, tb:tb + KB], start=True, stop=True), r=['wP', 'pooled'], w=['qpy%d' % i])
                P.dve(lambda e, g=g, i=i, tb=tb: e.scalar_tensor_tensor(out=zp[:, tb:tb + KB], in0=py[i][:, 0:KB], scalar=pscale[:, g:g + 1], in1=pg[:, tb:tb + KB], op0=ALU.mult, op1=ALU.mult),
                      r=['qpy%d' % i, 'pscale', 'pg'], w=['zp'])
            P.dma(zt[512 + g * 128:512 + (g + 1) * 128, :], zp[:], r=['zp'], w=[('zT', 4 + g)], q='act')
        if not noflush:
            P.flush()

    def phaseR(l, sq, states_only):
        N = sq.N
        nch = N // 128
        G4 = min(nch, 4)
        pj = projT[sq.name]
        zt = zT[sq.name]
        is_ctx = sq.who == 1
        qT = P.sb("qT", [128, N], BF16)
        kT = P.sb("kT", [128, N], BF16)
        vT = P.sb("vT", [128, N], BF16)
        vtok = P.sb("vtok", [128, nch, 128], BF16)
        ktok = P.sb("ktok", [128, nch, 128], BF16)
        kdf = P.sb("kdf", [128, nch, 128], BF16)
        kdb = P.sb("kdb", [128, nch, 128], BF16)
        Sf = P.sb("Sf", [128, nch, 128], BF16)
        Sb = P.sb("Sb", [128, nch, 128], BF16)
        Sx = [P.sb("Sx%d" % i, [128, 128], F32) for i in range(2)]
        ptr = P.ps("ptr", [128, 1024], BF16)
        pkv = [P.ps("pkv%d" % i, [128, 512]) for i in range(2)]
        if sq.rope:
            ropec = P.sb("ropec", [128, N], F32)
            ropes = P.sb("ropes", [128, N], F32)
            qr = P.sb("qr", [128, N], BF16)
            kr = P.sb("kr", [128, N], BF16)
            rt1 = P.sb("rt1", [128, 512], F32)
            rt2 = P.sb("rt2", [128, 512], F32)
            prot = P.ps("prot", [128, 512])
            P.dma(ropec[:], ropec_in, w=['ropec'])
            P.dma(ropes[:], ropes_in, w=['ropes'])
        else:
            qr, kr = qT, kT
        if not states_only:
            masks = P.sb("masks", [128, 8, 128], F32)
            qdec = P.sb("qdec", [128, 16, 128], F32)
            rg = P.sb("rg", [128, N], BF16)
            qf = P.sb("qf", [128, N], BF16)
            qb = P.sb("qb", [128, N], BF16)
            zr = P.sb("zr", [128, N], BF16)
            PT = P.sb("PT", [128, G4, 128], BF16)
            on = P.sb("on", [128, G4, 128], BF16)
            jk = P.sb("jk", [128, 128], BF16)
            rs = P.sb("rs", [128, 3, G4], F32)
            psc = P.ps("psc", [128, 512])
            po = P.ps("po", [128, 512])
            pT2 = P.ps("pT2", [128, 1024], BF16)
            P.dma(masks[:], masks_d[:, l * 8:(l + 1) * 8, :], w=['masks'])
            P.dma(qdec[:], qdec_d[:, l * 16:(l + 1) * 16, :], w=['qdec'])
        kk = 0
        for h in range(R_HEADS):
            P.dma(kT[:], pj[R_K + h * 128:R_K + (h + 1) * 128, :], w=['kT'])
            P.dma(vT[:], pj[R_V + h * 128:R_V + (h + 1) * 128, :], w=['vT'])
            if not states_only:
                P.dma(qT[:], pj[R_Q + h * 128:R_Q + (h + 1) * 128, :], w=['qT'])
                P.dma(rg[:], pj[R_G + h * 128:R_G + (h + 1) * 128, :], w=['rg'])
            qrr, krr = ('qr', 'kr') if sq.rope else ('qT', 'kT')
            if sq.rope:
                todo = [(kT, 'kT', kr, 'kr')]
                if not states_only:
                    todo.append((qT, 'qT', qr, 'qr'))
                for (src, srn, dst, dsn) in todo:
                    for tb in range(0, N, 512):
                        P.pe(lambda e, src=src, tb=tb: e.matmul(prot[:], lhsT=perm, rhs=src[:, tb:tb + 512], start=True, stop=True), r=[srn, 'cb'], w=['prot'])
                        P.dve(lambda e, tb=tb: e.tensor_tensor(out=rt1[:], in0=prot[:], in1=ropes[:, tb:tb + 512], op=ALU.mult), r=['prot', 'ropes'], w=['rt1'])
                        P.pool(lambda e, src=src, tb=tb: e.tensor_tensor(out=rt2[:], in0=src[:, tb:tb + 512], in1=ropec[:, tb:tb + 512], op=ALU.mult), r=[srn, 'ropec'], w=['rt2'])
                        P.pool(lambda e, dst=dst, tb=tb: e.tensor_tensor(out=dst[:, tb:tb + 512], in0=rt1[:], in1=rt2[:], op=ALU.add), r=['rt1', 'rt2'], w=[dsn])
            for (src, srn, dst, dsn) in ((vT, 'vT', vtok, 'vtok'), (kr, krr, ktok, 'ktok')):
                for c8 in range(0, nch, 8):
                    n8 = min(8, nch - c8)
                    for c in range(n8):
                        P.pe(lambda e, src=src, c=c, c8=c8: e.transpose(out=ptr[:, c * 128:(c + 1) * 128], in_=src[:, (c8 + c) * 128:(c8 + c + 1) * 128], identity=ident), r=[srn, 'cb'], w=['ptr'])
                    if kk % 2 == 0:
                        P.act(lambda e, dst=dst, c8=c8, n8=n8: e.activation(out=dst[:, c8:c8 + n8, :], in_=ptr[:, 0:n8 * 128].rearrange("p (c n) -> p c n", c=n8), func=AF.Copy), r=['ptr'], w=[dsn])
                    else:
                        P.dve(lambda e, dst=dst, c8=c8, n8=n8: e.tensor_copy(out=dst[:, c8:c8 + n8, :], in_=ptr[:, 0:n8 * 128].rearrange("p (c n) -> p c n", c=n8)), r=['ptr'], w=[dsn])
                    kk += 1
            if0, ib0 = l * 16 + h, l * 16 + 8 + h
            P.act(lambda e, i=if0: e.activation(out=kdf[:], in_=ktok[:], func=AF.Copy, scale=kdec[:, i:i + 1]), r=['ktok', 'kdec'], w=['kdf'])
            P.act(lambda e, i=ib0: e.activation(out=kdb[:], in_=ktok[:], func=AF.Copy, scale=kdec[:, i:i + 1]), r=['ktok', 'kdec'], w=['kdb'])
            if R_STAGE < 2:
                continue
            for d in range(2):
                kd_, kdn = (kdf, 'kdf') if d == 0 else (kdb, 'kdb')
                Sall, Sn = (Sf, 'Sf') if d == 0 else (Sb, 'Sb')
                idx = l * 16 + d * 8 + h
                order = list(range(nch)) if d == 0 else list(range(nch - 1, -1, -1))
                cur = 0
                if is_ctx:
                    P.dve(lambda e: e.memset(Sx[0][:], 0.0), w=['Sx0'])
                else:
                    P.dve(lambda e, d=d, h=h: e.tensor_copy(out=Sx[0][:], in_=s0[:, d * 8 + h, :]), r=['s0'], w=['Sx0'])
                P.act(lambda e, Sall=Sall, c=order[0]: e.activation(out=Sall[:, c, :], in_=Sx[0][:], func=AF.Copy), r=['Sx0'], w=[Sn])
                for oi, c in enumerate(order):
                    pk = pkv[oi % 2]
                    P.pe(lambda e, kd_=kd_, c=c, pk=pk: e.matmul(pk[:, 0:128], lhsT=kd_[:, c, :], rhs=vtok[:, c, :], start=True, stop=True), r=[kdn, 'vtok'], w=[('pkv', oi % 2)])
                    nxt = 1 - cur
                    last = oi == nch - 1
                    if last and is_ctx:
                        P.dve(lambda e, cur=cur, pk=pk, idx=idx, d=d, h=h: e.scalar_tensor_tensor(out=s0[:, d * 8 + h, :], in0=Sx[cur][:], scalar=dch[:, idx:idx + 1], in1=pk[:, 0:128], op0=ALU.mult, op1=ALU.add),
                              r=['Sx%d' % cur, 'dch', ('pkv', oi % 2)], w=['s0'])
                    elif not last:
                        P.dve(lambda e, cur=cur, nxt=nxt, pk=pk, idx=idx: e.scalar_tensor_tensor(out=Sx[nxt][:], in0=Sx[cur][:], scalar=dch[:, idx:idx + 1], in1=pk[:, 0:128], op0=ALU.mult, op1=ALU.add),
                              r=['Sx%d' % cur, 'dch', ('pkv', oi % 2)], w=['Sx%d' % nxt])
                        P.act(lambda e, Sall=Sall, c2=order[oi + 1], nxt=nxt: e.activation(out=Sall[:, c2, :], in_=Sx[nxt][:], func=AF.Copy), r=['Sx%d' % nxt], w=[Sn])
                        cur = nxt
            if states_only or R_STAGE < 3:
                continue
            P.dve(lambda e, i=h: e.tensor_tensor(out=qf[:].rearrange("p (c i) -> p c i", i=128), in0=qr[:].rearrange("p (c i) -> p c i", i=128),
                                                  in1=qdec[:, i, :].unsqueeze(1).broadcast_to([128, nch, 128]), op=ALU.mult), r=[qrr, 'qdec'], w=['qf'])
            P.pool(lambda e, i=8 + h: e.tensor_tensor(out=qb[:].rearrange("p (c i) -> p c i", i=128), in0=qr[:].rearrange("p (c i) -> p c i", i=128),
                                                      in1=qdec[:, i, :].unsqueeze(1).broadcast_to([128, nch, 128]), op=ALU.mult), r=[qrr, 'qdec'], w=['qb'])
            if R_STAGE < 4:
                continue
            for c4 in range(0, nch, G4):
                for cc in range(G4):
                    c = c4 + cc
                    P.pe(lambda e, c=c, cc=cc: e.matmul(psc[:, cc * 128:(cc + 1) * 128], lhsT=kr[:, c * 128:(c + 1) * 128], rhs=qr[:, c * 128:(c + 1) * 128], start=True, stop=True), r=[krr, qrr], w=['psc'])
                P.dve(lambda e, h=h: e.tensor_tensor(out=PT[:], in0=psc[:, 0:G4 * 128].rearrange("p (c i) -> p c i", i=128), in1=masks[:, h, :].unsqueeze(1).broadcast_to([128, G4, 128]), op=ALU.mult),
                      r=['psc', 'masks'], w=['PT'])
                for cc in range(G4):
                    c = c4 + cc
                    P.pe(lambda e, c=c, cc=cc: e.matmul(po[:, cc * 128:(cc + 1) * 128], lhsT=PT[:, cc, :], rhs=vtok[:, c, :], start=True, stop=False), r=['PT', 'vtok'], w=['po'])
                    P.pe(lambda e, c=c, cc=cc: e.matmul(po[:, cc * 128:(cc + 1) * 128], lhsT=qf[:, c * 128:(c + 1) * 128], rhs=Sf[:, c, :], start=False, stop=False), r=['qf', 'Sf'], w=['po'])
                    P.pe(lambda e, c=c, cc=cc: e.matmul(po[:, cc * 128:(cc + 1) * 128], lhsT=qb[:, c * 128:(c + 1) * 128], rhs=Sb[:, c, :], start=False, stop=True), r=['qb', 'Sb'], w=['po'])
                for cc in range(G4):
                    P.act(lambda e, cc=cc: e.activation(out=jk[:], in_=po[:, cc * 128:(cc + 1) * 128], func=AF.Square, accum_out=rs[:, 0, cc:cc + 1]), r=['po'], w=['jk', 'rs0'])
                P.act(lambda e: e.activation(out=rs[:, 1, :], in_=rs[:, 0, :], func=AF.Ln, scale=1.0 / 128, bias=EPS), r=['rs0'], w=['rs1'])
                P.act(lambda e: e.activation(out=rs[:, 2, :], in_=rs[:, 1, :], func=AF.Exp, scale=-0.5), r=['rs1'], w=['rs2'])
                P.dve(lambda e: e.tensor_tensor(out=on[:], in0=po[:, 0:G4 * 128].rearrange("p (c i) -> p c i", i=128), in1=rs[:, 2, :].unsqueeze(2).broadcast_to([128, G4, 128]), op=ALU.mult),
                      r=['po', 'rs2'], w=['on'])
                for cc in range(G4):
                    P.pe(lambda e, cc=cc: e.transpose(out=pT2[:, cc * 128:(cc + 1) * 128], in_=on[:, cc, :], identity=ident), r=['on', 'cb'], w=['pT2'])
                P.dve(lambda e, c4=c4: e.tensor_tensor(out=zr[:, c4 * 128:(c4 + G4) * 128], in0=pT2[:, 0:G4 * 128], in1=rg[:, c4 * 128:(c4 + G4) * 128], op=ALU.mult), r=['pT2', 'rg'], w=['zr'])
            P.dma(zt[1024 + h * 128:1024 + (h + 1) * 128, :], zr[:], r=['zr'], w=[('zT', 8 + h)])
        P.flush()

    def rope_emit(src, srn, dst, dsn, N, ropec, ropes, prot, rt1, rt2, cnt):
        for tb in range(0, N, 512):
            i = cnt[0] % 2
            cnt[0] += 1
            P.pe(lambda e, src=src, tb=tb, i=i: e.matmul(prot[i][:], lhsT=perm, rhs=src[:, tb:tb + 512], start=True, stop=True), r=[srn, 'cb'], w=['prot%d' % i])
            P.dve(lambda e, tb=tb, i=i: e.tensor_tensor(out=rt1[i][:], in0=prot[i][:], in1=ropes[:, tb:tb + 512], op=ALU.mult), r=['prot%d' % i, 'ropes'], w=['rt1%d' % i])
            P.pool(lambda e, src=src, tb=tb, i=i: e.tensor_tensor(out=rt2[i][:], in0=src[:, tb:tb + 512], in1=ropec[:, tb:tb + 512], op=ALU.mult), r=[srn, 'ropec'], w=['rt2%d' % i])
            P.pool(lambda e, dst=dst, tb=tb, i=i: e.tensor_tensor(out=dst[:, tb:tb + 512], in0=rt1[i][:], in1=rt2[i][:], op=ALU.add), r=['rt1%d' % i, 'rt2%d' % i], w=[dsn])

    def phaseR0x(l):
        N = NH
        nch = N // 128
        pj = projT['x']
        ropec = P.sb("ropec", [128, N], F32)
        ropes = P.sb("ropes", [128, N], F32)
        kT = [P.sb("kT%d" % i, [128, N], BF16) for i in range(2)]
        vT = [P.sb("vT%d" % i, [128, N], BF16) for i in range(2)]
        kr = [P.sb("kr%d" % i, [128, N], BF16) for i in range(2)]
        vtok = [P.sb("vtok%d" % i, [128, nch, 128], BF16) for i in range(2)]
        ktok = [P.sb("ktok%d" % i, [128, nch, 128], BF16) for i in range(2)]
        kdf = [P.sb("kdf%d" % i, [128, nch, 128], BF16) for i in range(2)]
        kdb = [P.sb("kdb%d" % i, [128, nch, 128], BF16) for i in range(2)]
        kfin = [[P.sb("kfin%d_%d" % (d_, i), [128, nch, 128], BF16) for i in range(2)] for d_ in range(2)]
        Lst = P.sb("Lst", [128, 16, 128], F32)
        rt1 = [P.sb("rt1%d" % i, [128, 512], F32) for i in range(2)]
        rt2 = [P.sb("rt2%d" % i, [128, 512], F32) for i in range(2)]
        prot = [P.ps("prot%d" % i, [128, 512]) for i in range(2)]
        ptr = [P.ps("ptr%d" % i, [128, 1024], BF16) for i in range(2)]
        pkv = [P.ps("pkv%d" % i, [128, 512]) for i in range(2)]
        P.dma(ropec[:], ropec_in, w=['ropec'])
        P.dma(ropes[:], ropes_in, w=['ropes'])
        cnt = [0]
        kk = 0
        kq = 0
        def stA1(h):
            p = h % 2
            P.capture()
            P.dma(kT[p][:], pj[R_K + h * 128:R_K + (h + 1) * 128, :], w=['kT%d' % p])
            P.dma(vT[p][:], pj[R_V + h * 128:R_V + (h + 1) * 128, :], w=['vT%d' % p])
            rope_emit(kT[p], 'kT%d' % p, kr[p], 'kr%d' % p, N, ropec, ropes, prot, rt1, rt2, cnt)
            return P.end_capture()

        def stA2(h):
            p = h % 2
            P.capture()
            for (src, srn, dst, dsn) in ((vT[p], 'vT%d' % p, vtok[p], 'vtok%d' % p), (kr[p], 'kr%d' % p, ktok[p], 'ktok%d' % p)):
                for c8 in range(0, nch, 8):
                    pt = ptr[kkc[0] % 2]
                    ptn = 'ptr%d' % (kkc[0] % 2)
                    for c in range(8):
                        P.pe(lambda e, src=src, c=c, c8=c8, pt=pt: e.transpose(out=pt[:, c * 128:(c + 1) * 128], in_=src[:, (c8 + c) * 128:(c8 + c + 1) * 128], identity=ident), r=[srn, 'cb'], w=[ptn])
                    if kkc[0] % 2 == 0:
                        P.act(lambda e, dst=dst, c8=c8, pt=pt: e.activation(out=dst[:, c8:c8 + 8, :], in_=pt[:].rearrange("p (c n) -> p c n", c=8), func=AF.Copy), r=[ptn], w=[dsn])
                    else:
                        P.dve(lambda e, dst=dst, c8=c8, pt=pt: e.tensor_copy(out=dst[:, c8:c8 + 8, :], in_=pt[:].rearrange("p (c n) -> p c n", c=8)), r=[ptn], w=[dsn])
                    kkc[0] += 1
            if0, ib0 = l * 16 + h, l * 16 + 8 + h
            P.act(lambda e, i=if0, p=p: e.activation(out=kdf[p][:], in_=ktok[p][:], func=AF.Copy, scale=kdec[:, i:i + 1]), r=['ktok%d' % p, 'kdec'], w=['kdf%d' % p])
            P.act(lambda e, i=ib0, p=p: e.activation(out=kdb[p][:], in_=ktok[p][:], func=AF.Copy, scale=kdec[:, i:i + 1]), r=['ktok%d' % p, 'kdec'], w=['kdb%d' % p])
            P.dma(rsc[h, 0], kr[p][:], r=['kr%d' % p], w=[('rsc', h, 0)], q='act')
            P.dma(rsc[h, 1], vtok[p][:].rearrange("p c n -> p (c n)"), r=['vtok%d' % p], w=[('rsc', h, 1)], q='act')
            P.dma(rsc[h, 2], kdf[p][:].rearrange("p c n -> p (c n)"), r=['kdf%d' % p], w=[('rsc', h, 2)], q='act')
            P.dma(rsc[h, 3], kdb[p][:].rearrange("p c n -> p (c n)"), r=['kdb%d' % p], w=[('rsc', h, 3)], q='act')
            for d in range(2):
                idx = l * 16 + d * 8 + h
                P.dve(lambda e, d=d, p=p, idx=idx: e.tensor_tensor(out=kfin[d][p][:], in0=ktok[p][:], in1=tabk[:, idx, :].unsqueeze(2).broadcast_to([128, nch, 128]), op=ALU.mult),
                      r=['ktok%d' % p, 'tabk'], w=['kfin%d_%d' % (d, p)])
            return P.end_capture()

        def stB(h):
            p = h % 2
            P.capture()
            for d in range(2):
                pk = pkv[kqc[0] % 2]
                pkn = 'pkv%d' % (kqc[0] % 2)
                kqc[0] += 1
                for c in range(nch):
                    P.pe(lambda e, d=d, c=c, pk=pk, p=p: e.matmul(pk[:, 0:128], lhsT=kfin[d][p][:, c, :], rhs=vtok[p][:, c, :], start=(c == 0), stop=(c == nch - 1)), r=['kfin%d_%d' % (d, p), 'vtok%d' % p], w=[pkn])
                if d == 0:
                    P.act(lambda e, pk=pk, d=d, h=h: e.activation(out=Lst[:, d * 8 + h, :], in_=pk[:, 0:128], func=AF.Copy), r=[pkn], w=['Lst'])
                else:
                    P.dve(lambda e, pk=pk, d=d, h=h: e.tensor_copy(out=Lst[:, d * 8 + h, :], in_=pk[:, 0:128]), r=[pkn], w=['Lst'])
            return P.end_capture()

        kkc = [0]
        kqc = [0]
        for sst in range(8 + 2):
            lists = []
            if sst < 8:
                lists.append(stA1(sst))
            if 0 <= sst - 1 < 8:
                lists.append(stA2(sst - 1))
            if 0 <= sst - 2 < 8:
                lists.append(stB(sst - 2))
            P.replay(*lists)
        P.dma(st_src.rearrange("(k p) e -> p k e", p=128), Lst[:], r=['Lst'], w=['st_src'])
        name = P.cc("AllGather", [st_src], [st_all], PAIRS, r=['st_src'], w=['st_all'])
        P.flush()
        return name

    def phaseRx(l, ccw, precast=None):
        N = NH
        nch = N // 128
        G4 = 4
        pj = projT['x']
        zt = zT['x']
        ropec = P.sb("ropec", [128, N], F32)
        ropes = P.sb("ropes", [128, N], F32)
        masks = P.sb("masks", [128, 8, 128], F32)
        qdec = P.sb("qdec", [128, 16, 128], F32)
        names = ['qT', 'rg', 'kr', 'vtok', 'kdf', 'kdb', 'qr', 'qf', 'qb', 'Sf', 'Sb', 'zr']
        NS = 3
        T = {nm: [P.sb("%s%d" % (nm, i), [128, N], BF16) for i in range(NS)] for nm in names}
        Ii = [P.sb("Ii%d" % i, [128, 128], F32) for i in range(2)]
        tis = [P.sb("ti%d" % i, [128, 128], F32) for i in range(2)]
        Sxd = [[P.sb("Sx%d_%d" % (d_, i), [128, 128], F32) for i in range(2)] for d_ in range(2)]
        PT = [P.sb("PT%d" % i, [128, G4, 128], BF16) for i in range(2)]
        on = [P.sb("on%d" % i, [128, G4, 128], BF16) for i in range(2)]
        jk = P.sb("jk", [128, 128], BF16)
        rs = [P.sb("rs%d" % i, [128, 3, G4], F32) for i in range(2)]
        rt1 = [P.sb("rt1%d" % i, [128, 512], F32) for i in range(2)]
        rt2 = [P.sb("rt2%d" % i, [128, 512], F32) for i in range(2)]
        prot = [P.ps("prot%d" % i, [128, 512]) for i in range(1)]
        prot = [prot[0], prot[0]]
        pkv = [P.ps("pkv%d" % i, [128, 512]) for i in range(2)]
        psc = [P.ps("psc%d" % i, [128, 512]) for i in range(2)]
        po = [P.ps("po%d" % i, [128, 512]) for i in range(2)]
        pT2 = P.ps("pT2", [128, 1024], BF16)
        P.dma(ropec[:], ropec_in, w=['ropec'])
        P.dma(ropes[:], ropes_in, w=['ropes'])
        P.dma(masks[:], masks_d[:, l * 8:(l + 1) * 8, :], w=['masks'])
        P.dma(qdec[:], qdec_d[:, l * 16:(l + 1) * 16, :], w=['qdec'])
        stv = st_all.rearrange("(r k p) e -> r k p e", r=2, k=16)
        if precast is not None:
            cast_upout(precast)
        cnt = [0]
        kq = 0
        kg = 0
        c3 = lambda t: t[:].rearrange("p (c i) -> p c i", i=128)
        stage_lists = {}
        for h in range(8):
            p = h % NS
            R = lambda nm, p=p: '%s%d' % (nm, p)
            qT, rg, kr, vtok, kdf, kdb, qr, qf, qb, Sf, Sb, zr = [T[nm][p] for nm in names]
            P.capture()
            P.dma(qT[:], pj[R_Q + h * 128:R_Q + (h + 1) * 128, :], w=[R('qT')])
            P.dma(rg[:], pj[R_G + h * 128:R_G + (h + 1) * 128, :], w=[R('rg')])
            P.dma(kr[:], rsc[h, 0], w=[R('kr')])
            P.dma(vtok[:], rsc[h, 1], w=[R('vtok')])
            P.dma(kdf[:], rsc[h, 2], w=[R('kdf')])
            P.dma(kdb[:], rsc[h, 3], w=[R('kdb')])
            for tb in range(0, N, 512):
                i = cnt[0] % 2
                cnt[0] += 1
                P.pe(lambda e, qT=qT, tb=tb: e.matmul(prot[0][:], lhsT=perm, rhs=qT[:, tb:tb + 512], start=True, stop=True), r=[R('qT'), 'cb'], w=['prot'])
                P.dve(lambda e, tb=tb, i=i: e.tensor_tensor(out=rt1[i][:], in0=prot[0][:], in1=ropes[:, tb:tb + 512], op=ALU.mult), r=['prot', 'ropes'], w=['rt1%d' % i])
                P.pool(lambda e, qT=qT, tb=tb, i=i: e.tensor_tensor(out=rt2[i][:], in0=qT[:, tb:tb + 512], in1=ropec[:, tb:tb + 512], op=ALU.mult), r=[R('qT'), 'ropec'], w=['rt2%d' % i])
                P.pool(lambda e, qr=qr, tb=tb, i=i: e.tensor_tensor(out=qr[:, tb:tb + 512], in0=rt1[i][:], in1=rt2[i][:], op=ALU.add), r=['rt1%d' % i, 'rt2%d' % i], w=[R('qr')])
            P.pool(lambda e, i=h, qf=qf, qr=qr: e.tensor_tensor(out=c3(qf), in0=c3(qr), in1=qdec[:, i, :].unsqueeze(1).broadcast_to([128, nch, 128]), op=ALU.mult), r=[R('qr'), 'qdec'], w=[R('qf')])
            P.pool(lambda e, i=8 + h, qb=qb, qr=qr: e.tensor_tensor(out=c3(qb), in0=c3(qr), in1=qdec[:, i, :].unsqueeze(1).broadcast_to([128, nch, 128]), op=ALU.mult), r=[R('qr'), 'qdec'], w=[R('qb')])
            stage_lists[('A1', h)] = P.end_capture()
            P.capture()
            dlists = []
            for d in range(2):
                P.capture()
                kd_, kdn = (kdf, R('kdf')) if d == 0 else (kdb, R('kdb'))
                Sall, Sn = (Sf, R('Sf')) if d == 0 else (Sb, R('Sb'))
                idx = l * 16 + d * 8 + h
                order = list(range(nch)) if d == 0 else list(range(nch - 1, -1, -1))
                ii = Ii[d]
                tid = tis[d]
                SxD = Sxd[d]
                pk = pkv[d]
                pkn = 'pkv%d' % d
                src = stv[0, h] if d == 0 else stv[1, 8 + h]
                P.dma(ii[:], src, w=['Ii%d' % d], extra=[(ccw, 1)])
                P.dve(lambda e, d=d, h=h, idx=idx, tid=tid: e.tensor_scalar(out=tid[:], in0=s0[:, d * 8 + h, :], scalar1=coef[:, idx:idx + 1], scalar2=None, op0=ALU.mult), r=['s0', 'coef'], w=['ti%d' % d])
                P.dve(lambda e, d=d, ii=ii, tid=tid, SxD=SxD: e.scalar_tensor_tensor(out=SxD[0][:], in0=ii[:], scalar=role[:, (1 - d):(2 - d)], in1=tid[:], op0=ALU.mult, op1=ALU.add), r=['Ii%d' % d, 'role', 'ti%d' % d], w=['Sx%d_0' % d])
                P.act(lambda e, Sall=Sall, c=order[0], SxD=SxD: e.activation(out=Sall[:, c * 128:(c + 1) * 128], in_=SxD[0][:], func=AF.Copy), r=['Sx%d_0' % d], w=[Sn])
                cur = 0
                for oi, c in enumerate(order[:-1]):
                    P.pe(lambda e, kd_=kd_, c=c, pk=pk, vtok=vtok: e.matmul(pk[:, 0:128], lhsT=kd_[:, c * 128:(c + 1) * 128], rhs=vtok[:, c * 128:(c + 1) * 128], start=True, stop=True), r=[kdn, R('vtok')], w=[pkn])
                    nxt = 1 - cur
                    P.dve(lambda e, cur=cur, nxt=nxt, pk=pk, idx=idx, SxD=SxD: e.scalar_tensor_tensor(out=SxD[nxt][:], in0=SxD[cur][:], scalar=dch[:, idx:idx + 1], in1=pk[:, 0:128], op0=ALU.mult, op1=ALU.add),
                          r=['Sx%d_%d' % (d, cur), 'dch', pkn], w=['Sx%d_%d' % (d, nxt)])
                    P.act(lambda e, Sall=Sall, c2=order[oi + 1], nxt=nxt, SxD=SxD: e.activation(out=Sall[:, c2 * 128:(c2 + 1) * 128], in_=SxD[nxt][:], func=AF.Copy), r=['Sx%d_%d' % (d, nxt)], w=[Sn])
                    cur = nxt
                dlists.append(P.end_capture())
            P.replay(dlists[0], dlists[1])
            stage_lists[('A2', h)] = P.end_capture()
            P.capture()
            NG = nch // G4

            def stX(g, h=h, kr=kr, qr=qr):
                g2 = g % 2
                c4 = g * G4
                ps_, PT_ = psc[g2], PT[g2]
                pscn, PTn = 'psc%d' % g2, 'PT%d' % g2
                for cc in range(G4):
                    c = c4 + cc
                    P.pe(lambda e, c=c, cc=cc, ps_=ps_: e.matmul(ps_[:, cc * 128:(cc + 1) * 128], lhsT=kr[:, c * 128:(c + 1) * 128], rhs=qr[:, c * 128:(c + 1) * 128], start=True, stop=True), r=[R('kr'), R('qr')], w=[pscn])
                P.dve(lambda e, ps_=ps_, PT_=PT_: e.tensor_tensor(out=PT_[:], in0=ps_[:].rearrange("p (c i) -> p c i", i=128), in1=masks[:, h, :].unsqueeze(1).broadcast_to([128, G4, 128]), op=ALU.mult),
                      r=[pscn, 'masks'], w=[PTn])

            def stY(g, vtok=vtok, qf=qf, qb=qb, Sf=Sf, Sb=Sb):
                g2 = g % 2
                c4 = g * G4
                po_, PT_, on_, rs_ = po[g2], PT[g2], on[g2], rs[g2]
                pon, PTn, onn, rsn = 'po%d' % g2, 'PT%d' % g2, 'on%d' % g2, 'rs%d' % g2
                for cc in range(G4):
                    c = c4 + cc
                    sl = slice(c * 128, (c + 1) * 128)
                    P.pe(lambda e, cc=cc, sl=sl: e.matmul(po_[:, cc * 128:(cc + 1) * 128], lhsT=PT_[:, cc, :], rhs=vtok[:, sl], start=True, stop=False), r=[PTn, R('vtok')], w=[pon])
                    P.pe(lambda e, cc=cc, sl=sl: e.matmul(po_[:, cc * 128:(cc + 1) * 128], lhsT=qf[:, sl], rhs=Sf[:, sl], start=False, stop=False), r=[R('qf'), R('Sf')], w=[pon])
                    P.pe(lambda e, cc=cc, sl=sl: e.matmul(po_[:, cc * 128:(cc + 1) * 128], lhsT=qb[:, sl], rhs=Sb[:, sl], start=False, stop=True), r=[R('qb'), R('Sb')], w=[pon])
                for cc in range(G4):
                    P.act(lambda e, cc=cc: e.activation(out=jk[:], in_=po_[:, cc * 128:(cc + 1) * 128], func=AF.Square, accum_out=rs_[:, 0, cc:cc + 1]), r=[pon], w=['jk', rsn + 'a'])
                P.act(lambda e: e.activation(out=rs_[:, 1, :], in_=rs_[:, 0, :], func=AF.Ln, scale=1.0 / 128, bias=EPS), r=[rsn + 'a'], w=[rsn + 'b'])
                P.act(lambda e: e.activation(out=rs_[:, 2, :], in_=rs_[:, 1, :], func=AF.Exp, scale=-0.5), r=[rsn + 'b'], w=[rsn + 'c'])
                for cc in range(G4):
                    P.act(lambda e, cc=cc: e.activation(out=on_[:, cc, :], in_=po_[:, cc * 128:(cc + 1) * 128], func=AF.Copy, scale=rs_[:, 2, cc:cc + 1]), r=[pon, rsn + 'c'], w=[onn])

            def stZ(g, zr=zr, rg=rg):
                g2 = g % 2
                c4 = g * G4
                on_ = on[g2]
                onn = 'on%d' % g2
                for cc in range(G4):
                    P.pe(lambda e, cc=cc: e.transpose(out=pT2[:, cc * 128:(cc + 1) * 128], in_=on_[:, cc, :], identity=ident), r=[onn, 'cb'], w=['pT2'])
                P.dve(lambda e: e.tensor_tensor(out=zr[:, c4 * 128:(c4 + G4) * 128], in0=pT2[:, 0:G4 * 128], in1=rg[:, c4 * 128:(c4 + G4) * 128], op=ALU.mult), r=['pT2', R('rg')], w=[R('zr')])

            for sst in range(NG + 2):
                if sst < NG:
                    stX(sst)
                if 0 <= sst - 1 < NG:
                    stY(sst - 1)
                if 0 <= sst - 2 < NG:
                    stZ(sst - 2)
            P.dma(zt[1024 + h * 128:1024 + (h + 1) * 128, :], zr[:], r=[R('zr')], w=[('zT', 8 + h)], q='act')
            stage_lists[('B', h)] = P.end_capture()
        for sst in range(8 + 2):
            lists = []
            if sst < 8:
                lists.append(stage_lists[('A1', sst)])
            if 0 <= sst - 1 < 8:
                lists.append(stage_lists[('A2', sst - 1)])
            if 0 <= sst - 2 < 8:
                lists.append(stage_lists[('B', sst - 2)])
            P.replay(*lists)
        P.flush()

    def phaseM(l, sq, xsrc, xdst, final, precast=None):
        N = sq.N
        KB = min(N, 512)
        NT = KB // 128
        pj = projT[sq.name]
        zt = zT[sq.name]
        gate = P.sb("gate", [128, D], F32)
        zbs = [P.sb("zb%d" % i, [128, 16, KB], BF16) for i in range(2)]
        mergeds = [P.sb("merged%d" % i, [128, 16, KB], BF16) for i in range(2)]
        wu = [P.sb("wu%d" % i, [128, 16, 128], BF16) for i in range(3)]
        wo = [P.sb("wo%d" % i, [128, 16, 512], BF16) for i in range(2)]
        gts = [P.sb("gts%d" % i, [128, 3, KB], BF16) for i in range(3)]
        tqs = [[P.sb("tq%d_%d" % (i, j), [128, KB], F32) for i in range(3)] for j in range(2)]
        xin = [P.sb("xin%d" % i, [128, 512], F32) for i in range(2)]
        tys = [P.sb("ty%d" % i, [128, 512], F32) for i in range(2)]
        if final:
            fng = P.sb("fng", [128, D], F32)
            xfull = [P.sb("xfull%d" % i, [128, D], F32) for i in range(NT)]
            ss = P.sb("ss", [128, NT, 8], F32)
            jk2 = P.sb("jk2", [128, 512], BF16)
            P.dma(fng[:], bc(fng_in), w=['fng'])
        else:
            xo = [P.sb("xo%d" % i, [128, 512], F32) for i in range(2)]
        pybs = [[P.ps("pyb%d_%d" % (i, j), [128, 512]) for i in range(3)] for j in range(2)]
        pout = [P.ps("pout%d" % i, [128, 512]) for i in range(2)]
        P.dma(gate[:], bc(modrows[l, 2, sq.who]), w=['gate'])
        gsrc = pj[MG:INW].rearrange("(b c p) t -> p b c t", b=3, c=16)
        cn = dict(ku=0, ko=0, kx=0, kp=0)
        NB = N // KB

        def emitU(bi, dcs):
            tb = bi * KB
            zbi = bi % 2
            zb = zbs[zbi]
            zbn = 'zb%d' % zbi
            mg = mergeds[bi % 2]
            for dc in dcs:
                ku = cn['ku']
                cn['ku'] += 1
                w_ = wu[ku % 3]
                wr = 'wu%d' % (ku % 3)
                gt = gts[ku % 3]
                gr = 'gts%d' % (ku % 3)
                pyb = pybs[ku % 2]
                tq = tqs[ku % 2]
                pj_ = ku % 2
                while cn.get('kl', 0) <= min(ku + 2, NB * 16 - 1):
                    kl = cn.get('kl', 0)
                    cn['kl'] = kl + 1
                    lb, ldc = kl // 16, kl % 16
                    P.dma(wu[kl % 3][:], wub[l, ldc], w=['wu%d' % (kl % 3)])
                    P.dma(gts[kl % 3][:], gsrc[:, :, ldc, lb * KB:(lb + 1) * KB], w=['gts%d' % (kl % 3)])
                for br, (k0, k1) in enumerate(((0, 4), (4, 8), (8, 16))):
                    for kc in range(k0, k1):
                        P.pe(lambda e, br=br, kc=kc, k0=k0, k1=k1, w_=w_, pyb=pyb, zb=zb: e.matmul(pyb[br][:, 0:KB], lhsT=w_[:, kc, :], rhs=zb[:, kc, :], start=(kc == k0), stop=(kc == k1 - 1)),
                             r=[wr, zbn], w=['pyb%d_%d' % (br, pj_)])
                for br in range(3):
                    P.dve(lambda e, br=br, gt=gt, pyb=pyb, tq=tq: e.tensor_tensor(out=tq[br][:], in0=pyb[br][:, 0:KB], in1=gt[:, br, :], op=ALU.mult), r=['pyb%d_%d' % (br, pj_), gr], w=['tq%d_%d' % (br, pj_)])
                P.pool(lambda e, tq=tq: e.tensor_tensor(out=tq[0][:], in0=tq[0][:], in1=tq[1][:], op=ALU.add), r=['tq0_%d' % pj_, 'tq1_%d' % pj_], w=['tq0_%d' % pj_])
                P.pool(lambda e, dc=dc, tq=tq, mg=mg: e.tensor_tensor(out=mg[:, dc, :], in0=tq[0][:], in1=tq[2][:], op=ALU.add), r=['tq0_%d' % pj_, 'tq2_%d' % pj_], w=[('merged', bi % 2, dc)])

        def emitO(bi):
            tb = bi * KB
            mg = mergeds[bi % 2]
            mgr = [('merged', bi % 2, d_) for d_ in range(16)]
            for cg in range(4):
                ko = cn['ko']
                cn['ko'] += 1
                wo_ = wo[ko % 2]
                wor = 'wo%d' % (ko % 2)
                if cg < 3:
                    P.dma(wo[(ko + 1) % 2][:], wob[l, cg + 1], w=['wo%d' % ((ko + 1) % 2)])
                for ti in range(NT):
                    t0 = tb + ti * 128
                    kp, kx = cn['kp'], cn['kx']
                    cn['kp'] += 1
                    cn['kx'] += 1
                    pp = pout[kp % 2]
                    pr = 'pout%d' % (kp % 2)
                    xi = xin[kx % 2]
                    xr = 'xin%d' % (kx % 2)
                    P.dma(xi[:], xsrc[t0:t0 + 128, cg * 512:(cg + 1) * 512], w=[xr])
                    for kc in range(16):
                        P.pe(lambda e, pp=pp, kc=kc, ti=ti, wo_=wo_, mg=mg: e.matmul(pp[:], lhsT=mg[:, kc, ti * 128:(ti + 1) * 128], rhs=wo_[:, kc, :], start=(kc == 0), stop=(kc == 15)),
                             r=mgr + [wor], w=[pr])
                    ty = tys[kx % 2]
                    tyn = 'ty%d' % (kx % 2)
                    P.dve(lambda e, pp=pp, cg=cg, ty=ty: e.tensor_tensor(out=ty[:], in0=pp[:], in1=gate[:, cg * 512:(cg + 1) * 512], op=ALU.mult), r=[pr, 'gate'], w=[tyn])
                    if final:
                        xf = xfull[ti]
                        P.pool(lambda e, xf=xf, xi=xi, cg=cg, ty=ty: e.tensor_tensor(out=xf[:, cg * 512:(cg + 1) * 512], in0=ty[:], in1=xi[:], op=ALU.add), r=[tyn, xr], w=[('xfull', ti)])
                        P.act(lambda e, xf=xf, cg=cg, ti=ti: e.activation(out=jk2[:], in_=xf[:, cg * 512:(cg + 1) * 512], func=AF.Square, accum_out=ss[:, ti, cg:cg + 1]), r=[('xfull', ti)], w=['jk2', ('ss', ti)])
                    else:
                        xo_ = xo[kx % 2]
                        xor_ = 'xo%d' % (kx % 2)
                        P.pool(lambda e, xo_=xo_, xi=xi, ty=ty: e.tensor_tensor(out=xo_[:], in0=ty[:], in1=xi[:], op=ALU.add), r=[tyn, xr], w=[xor_])
                        P.dma(xdst[t0:t0 + 128, cg * 512:(cg + 1) * 512], xo_[:], r=[xor_], w=[('xdst', t0, cg)], q='act')
            if final:
                for ti in range(NT):
                    t0 = tb + ti * 128
                    xf = xfull[ti]
                    P.dve(lambda e, ti=ti: e.tensor_tensor(out=ss[:, ti, 4:5], in0=ss[:, ti, 0:1], in1=ss[:, ti, 1:2], op=ALU.add), r=[('ss', ti)], w=[('ss', ti)])
                    P.dve(lambda e, ti=ti: e.tensor_tensor(out=ss[:, ti, 5:6], in0=ss[:, ti, 2:3], in1=ss[:, ti, 3:4], op=ALU.add), r=[('ss', ti)], w=[('ss', ti)])
                    P.dve(lambda e, ti=ti: e.tensor_tensor(out=ss[:, ti, 6:7], in0=ss[:, ti, 4:5], in1=ss[:, ti, 5:6], op=ALU.add), r=[('ss', ti)], w=[('ss', ti)])
                    P.act(lambda e, ti=ti: e.activation(out=ss[:, ti, 7:8], in_=ss[:, ti, 6:7], func=AF.Ln, scale=1.0 / D, bias=EPS), r=[('ss', ti)], w=[('ss', ti)])
                    P.act(lambda e, ti=ti: e.activation(out=ss[:, ti, 6:7], in_=ss[:, ti, 7:8], func=AF.Exp, scale=-0.5), r=[('ss', ti)], w=[('ss', ti)])
                    P.dve(lambda e, ti=ti, xf=xf: e.scalar_tensor_tensor(out=xf[:], in0=xf[:], scalar=ss[:, ti, 6:7], in1=fng[:], op0=ALU.mult, op1=ALU.mult), r=[('xfull', ti), ('ss', ti), 'fng'], w=[('xfull', ti)])
                    P.dma(xdst[t0:t0 + 128, :], xf[:], r=[('xfull', ti)], w=[('xdst', t0)], q='act')

        def loadz(bi):
            P.dma(zbs[bi % 2][:], zt[:, bi * KB:(bi + 1) * KB].rearrange("(c p) t -> p c t", p=128), w=['zb%d' % (bi % 2)])

        HEAD = 2
        loadz(0)
        emitU(0, range(16))
        for bi in range(NB):
            P.dma(wo[cn['ko'] % 2][:], wob[l, 0], w=['wo%d' % (cn['ko'] % 2)])
            if bi + 1 < NB:
                loadz(bi + 1)
                emitU(bi + 1, range(HEAD))
            emitO(bi)
            if bi + 1 < NB:
                emitU(bi + 1, range(HEAD, 16))
        P.flush()

    allc = list(range(96))
    kvc = list(range(R_K // 128, R_G // 128))
    plan = [prologue]
    ccn = {}
    for l in range(DEPTH):
        last = l == DEPTH - 1
        csrc = ctx_in if l == 0 else xres['c']
        xsrc = x_in if l == 0 else xres['x']
        if not last:
            plan.append(lambda l=l, csrc=csrc, xsrc=xsrc: phaseP(l, seq_x, xsrc, allc, precast=l, pre=(seq_c, csrc)))
            plan.append(lambda l=l: phaseF(l, seq_c))
            def qr_c(l=l):
                phaseQ(l, seq_c, noflush=True)
                phaseR(l, seq_c, False)
            plan.append(qr_c)
            plan.append(lambda l=l, csrc=csrc: phaseM(l, seq_c, csrc, xres['c'], False))
        else:
            plan.append(lambda l=l, csrc=csrc: phaseP(l, seq_c, csrc, kvc))
            plan.append(lambda l=l: phaseR(l, seq_c, True))
        if last:
            plan.append(lambda l=l, xsrc=xsrc: phaseP(l, seq_x, xsrc, allc, precast=l))
        def f1r0(l=l):
            ccn['ab'] = phaseF(l, seq_x, 'f1', noflush=True)
            ccn['st'] = phaseR0x(l)
        plan.append(f1r0)
        def f2q(l=l):
            P.capture()
            phaseF(l, seq_x, 'f2', ccn['ab'], noflush=True)
            Lf = P.end_capture()
            P.capture()
            phaseQ(l, seq_x, noflush=True, ccw=ccn['ab'][0])
            Lq = P.end_capture()
            P.replay(Lf, Lq)
            P.flush()
        plan.append(f2q)
        plan.append(lambda l=l: phaseRx(l, ccn['st']))
        plan.append(lambda l=l, xsrc=xsrc, last=last: phaseM(l, seq_x, xsrc, out if last else xres['x'], last))
    for ph in plan[:PHASE_LIMIT]:
        ph()
    if dbg:
        srcs = dict(projT_c=projT['c'], zT_c=zT['c'], xres_c=xres['c'], modrows=modrows, masks_d=masks_d, qdec_d=qdec_d,
                    projT_x=projT['x'], zT_x=zT['x'], xres_x=xres['x'])
        for k, ap in dbg.items():
            P.dma(ap, srcs[k], w=[('dbg', k)])
        P.flush()
    P.finish()
    return nc


def _blk(w, ncols):
    K, C = w.shape
    return np.ascontiguousarray(w.reshape(16, 128, C // ncols, ncols).transpose(2, 1, 0, 3))


def make_in_maps(x, c, ctx, c_ctx, w_ada, b_ada, norm_g, w_in, w_fourier, w_pool, pool_scale,
                 ret_decay_logit, w_up_fourier, w_up_pool, w_up_ret, w_out, final_norm_g):
    f = np.float32
    x = np.asarray(x, f); c = np.asarray(c, f); ctx = np.asarray(ctx, f); c_ctx = np.asarray(c_ctx, f)
    C = _consts()
    shared = {}
    wada_blk = np.stack([_blk(np.asarray(w_ada[l], f), 512) for l in range(DEPTH)])
    b_ada = np.asarray(b_ada, f)
    shared["norm_g"] = np.ascontiguousarray(np.asarray(norm_g, f))
    shared["w_in"] = np.stack([_blk(np.asarray(w_in[l], f), 512) for l in range(DEPTH)])
    shared["w_fourier"] = np.ascontiguousarray(np.asarray(w_fourier, f).transpose(0, 2, 1, 3))
    shared["w_pool"] = np.ascontiguousarray(np.asarray(w_pool, f).transpose(0, 2, 1, 3))
    shared["pool_scale"] = np.ascontiguousarray(np.asarray(pool_scale, f).reshape(DEPTH, 4, 128).transpose(0, 2, 1))
    shared["ret_decay_logit"] = np.ascontiguousarray(np.asarray(ret_decay_logit, f).reshape(32))
    wup = [np.concatenate([np.asarray(w_up_fourier[l], f), np.asarray(w_up_pool[l], f), np.asarray(w_up_ret[l], f)], axis=0) for l in range(DEPTH)]
    shared["w_up"] = np.stack([_blk(w, 128) for w in wup])
    shared["w_out"] = np.stack([_blk(np.asarray(w_out[l], f), 512) for l in range(DEPTH)])
    shared["final_norm_g"] = np.ascontiguousarray(np.asarray(final_norm_g, f))
    shared["cf"] = C['cf']
    shared["cb"] = C['cb']
    shared["invc_c"] = C['invc_c']
    shared["dftc_c"] = C['dftc_c']
    shared["dfts_c"] = C['dfts_c']
    half = []
    for r in range(2):
        sl = slice(r * NH, (r + 1) * NH)
        role = np.zeros((128, 4), np.float32)
        role[:, r] = 1.0
        half.append(dict(w_ada=np.ascontiguousarray(wada_blk[:, r * 6:(r + 1) * 6]),
                         b_ada=np.ascontiguousarray(b_ada[:, r * 3072:(r + 1) * 3072].reshape(-1)),
                         ropec=np.ascontiguousarray(C['ropec'][:, sl]), ropes=np.ascontiguousarray(C['ropes'][:, sl]),
                         invc_x=np.ascontiguousarray(C['invc_x'][:, sl]), dftc_x=np.ascontiguousarray(C['dftc_x'][:, sl]),
                         dfts_x=np.ascontiguousarray(C['dfts_x'][:, sl]), role=role))
    maps = []
    for core in range(NCORES):
        b, r = core // 2, core % 2
        m = dict(shared)
        m.update(half[r])
        m["x"] = np.ascontiguousarray(x[b, r * NH:(r + 1) * NH])
        m["ctx"] = np.ascontiguousarray(ctx[b])
        cc = np.stack([c[b], c_ctx], axis=-1)
        m["cT"] = np.ascontiguousarray(cc.reshape(16, 128, 2).transpose(1, 0, 2))
        maps.append(m)
    return maps


_NC = None


def kernel(**inputs):
    global _NC
    maps = make_in_maps(**inputs)
    if _NC is None:
        _NC = build()
    res = run_bass_kernel_spmd(_NC, maps, core_ids=list(range(NCORES)))
    outs = [np.asarray(r["out"], np.float32) for r in res.results]
    return np.stack([np.concatenate([outs[2 * b], outs[2 * b + 1]], axis=0) for b in range(NCORES // 2)], axis=0)
```

```python
import math
from contextlib import ExitStack

import ml_dtypes
import numpy as np

import concourse.bass as bass
import concourse.mybir as mybir
from concourse.bass_utils import run_bass_kernel_spmd

F32 = mybir.dt.float32
BF16 = mybir.dt.bfloat16
AF = mybir.ActivationFunctionType
ALU = mybir.AluOpType

EPOCH = 30000
RING = 8

D = 2048
NX = 4096
NH = 2048
NCX = 256
DEPTH = 2
INW = 12288
F_X, F_G, P_X, P_G, R_Q, R_K, R_V, R_G, MG = 0, 512, 1024, 1536, 2048, 3072, 4096, 5120, 6144
EPS = 1e-6
QS = 128.0 ** -0.5
NCORES = 8
PAIRS = [[0, 1], [2, 3], [4, 5], [6, 7]]

DEBUG = {}
PHASE_LIMIT = 1000
R_STAGE = 99
R_HEADS = 8


class Prog:
    ENGS = ['pe', 'act', 'dve', 'pool', 'sp']

    def __init__(self, nc):
        self.nc = nc
        self.gstack = ExitStack()
        self.pstack = ExitStack()
        self.sem_cache = {}
        self.tick = {e: 0 for e in self.ENGS}
        self.dcount = {e: 0 for e in self.ENGS}
        self.ring_last = {}
        self.last_tick = {}
        self._reset()
        self.nuniq = 0
        self.ncc = 0
        self._caps = []

    def _reset(self):
        self.ops = []
        self.last_w = {}
        self.readers = {}

    def sem(self, name):
        if name not in self.sem_cache:
            self.sem_cache[name] = self.gstack.enter_context(self.nc.semaphore(name))
        return self.sem_cache[name]

    def sbp(self, name, shape, dt):
        return self.gstack.enter_context(self.nc.sbuf_tensor("sb_" + name, list(shape), dt))

    def sb(self, name, shape, dt):
        self.nuniq += 1
        return self.pstack.enter_context(self.nc.sbuf_tensor("%s_%d" % (name, self.nuniq), list(shape), dt))

    def ps(self, name, shape, dt=F32):
        self.nuniq += 1
        return self.pstack.enter_context(self.nc.psum_tensor("%s_%d" % (name, self.nuniq), list(shape), dt))

    def dram(self, name, shape, dt):
        return self.nc.dram_tensor(name, list(shape), dt, kind="Internal").ap()

    def cc(self, kind, ins, outs, groups, r=(), w=()):
        name = 'cc_%d' % self.ncc
        self.ncc += 1
        self.op('pool', lambda e: e.collective_compute(kind, ALU.bypass, replica_groups=groups, ins=ins, outs=outs), r, w, False, (), name)
        return name

    def capture(self):
        self._caps.append([])

    def end_capture(self):
        return self._caps.pop()

    def replay(self, *lists):
        pos = [0] * len(lists)
        total = sum(len(l) for l in lists)
        for _ in range(total):
            best, bf = None, None
            for k, l in enumerate(lists):
                if pos[k] < len(l):
                    f = (pos[k] + 0.5) / len(l)
                    if bf is None or f < bf:
                        best, bf = k, f
            a = lists[best][pos[best]]
            pos[best] += 1
            self.op(*a)

    def op(self, eng, fn, r=(), w=(), dma=False, extra=(), cc=None):
        if self._caps:
            self._caps[-1].append((eng, fn, tuple(r), tuple(w), dma, tuple(extra), cc))
            return None
        i = len(self.ops)
        deps = set()
        for k in r:
            lw = self.last_w.get(k)
            if lw is not None:
                deps.add(lw)
        for k in w:
            lw = self.last_w.get(k)
            if lw is not None:
                deps.add(lw)
            for j in self.readers.get(k, ()):
                deps.add(j)
        deps.discard(i)
        self.ops.append(dict(eng=eng, fn=fn, dma=dma, deps=deps, r=tuple(r), w=tuple(w), extra=tuple(extra)))
        if cc is not None:
            self.ops[-1]['cc'] = cc
        for k in w:
            self.last_w[k] = i
            self.readers[k] = []
        for k in r:
            self.readers.setdefault(k, []).append(i)
        return i

    def pe(self, fn, r=(), w=()):
        return self.op('pe', fn, r, w)

    def act(self, fn, r=(), w=()):
        return self.op('act', fn, r, w)

    def dve(self, fn, r=(), w=()):
        return self.op('dve', fn, r, w)

    def pool(self, fn, r=(), w=()):
        return self.op('pool', fn, r, w)

    def dma(self, out, in_, r=(), w=(), q='sp', extra=()):
        return self.op(q, lambda e: e.dma_start(out=out, in_=in_), r, w, dma=True, extra=extra)

    def flush(self):
        nc = self.nc
        ops = self.ops
        n = len(ops)
        for i, o in enumerate(ops):
            keep = set()
            for j in o['deps']:
                p = ops[j]
                same = (p['eng'] == o['eng']) and (not p['dma']) and (not o['dma'])
                if same:
                    if o['eng'] == 'pe':
                        continue
                    if not (set(p['w']) & (set(o['r']) | set(o['w']))):
                        continue
                keep.add(j)
            o['deps'] = keep
        needed = [False] * n
        for o in ops:
            for j in o['deps']:
                needed[j] = True
        lastc = {}
        for i, o in enumerate(ops):
            if not o['dma'] and 'cc' not in o:
                lastc[o['eng']] = i
        for e, i in lastc.items():
            needed[i] = True
        ring_prev = {}
        for i, o in enumerate(ops):
            e = o['eng']
            if 'cc' in o:
                o['sig'] = (o['cc'], 1)
            elif o['dma']:
                k = self.dcount[e]
                self.dcount[e] += 1
                slot = k % RING
                o['sig'] = ('d_%s_%d' % (e, slot), 16 * (k // RING + 1))
                o['ring_wait'] = self.ring_last.get((e, slot))
                self.ring_last[(e, slot)] = o['sig']
            elif needed[i]:
                t = self.tick[e]
                self.tick[e] += 1
                o['sig'] = ('c_%s_%d' % (e, t // EPOCH), t % EPOCH + 1)
                self.last_tick[e] = o['sig']
            else:
                o['sig'] = None
        for o in ops:
            if o['sig'] is not None:
                self.sem(o['sig'][0])
        barrier = list(self.last_tick.values()) + list(self.ring_last.values())

        def run_engine(e, eh):
            waited = {}

            def wait(sn, v):
                if waited.get(sn, 0) >= v:
                    return
                eh.wait_ge(self.sem(sn), v)
                waited[sn] = v

            for i, o in enumerate(ops):
                if o['eng'] != e:
                    continue
                need = {}
                for j in o['deps']:
                    s = ops[j]['sig']
                    if need.get(s[0], 0) < s[1]:
                        need[s[0]] = s[1]
                if o['dma'] and o['ring_wait'] is not None:
                    s = o['ring_wait']
                    if need.get(s[0], 0) < s[1]:
                        need[s[0]] = s[1]
                for s in o['extra']:
                    if need.get(s[0], 0) < s[1]:
                        need[s[0]] = s[1]
                for sn, v in need.items():
                    wait(sn, v)
                ins = o['fn'](eh)
                if 'cc' in o:
                    ins.then_inc(self.sem(o['sig'][0]))
                elif o['sig'] is not None:
                    ins.then_inc(self.sem(o['sig'][0]), 16 if o['dma'] else 1)
            for sn, v in barrier:
                wait(sn, v)

        with nc.Block() as block:
            @block.sync
            def _(eh):
                run_engine('sp', eh)

            @block.tensor
            def _(eh):
                run_engine('pe', eh)

            @block.vector
            def _(eh):
                run_engine('dve', eh)

            @block.scalar
            def _(eh):
                run_engine('act', eh)

            @block.gpsimd
            def _(eh):
                run_engine('pool', eh)
        self.pstack.close()
        self.pstack = ExitStack()
        self._reset()

    def finish(self):
        self.gstack.close()


def bc(ap1d, n=128):
    return ap1d.partition_broadcast(n)


_CONST = {}


def _consts():
    if _CONST:
        return _CONST
    bf = ml_dtypes.bfloat16
    j = np.arange(128)
    diff = j[None, :] - j[:, None]
    cf = np.zeros((128, 704), np.float32)
    cf[:, 0:128] = np.maximum(diff, 0)
    cf[:, 128:256] = np.maximum(-diff, 0)
    cf[:, 256:384] = np.eye(128) * QS
    cf[:, 384:512] = (j + 1)[None, :]
    cf[:, 512:640] = (128 - j)[None, :]
    jf = np.zeros((128, 32), np.float32)
    for l in range(2):
        jf[:, l * 16:l * 16 + 8] = (127 - j)[:, None]
        jf[:, l * 16 + 8:l * 16 + 16] = j[:, None]
    cf[:, 640:672] = jf
    cchunk = np.arange(16)
    cf[:, 672:688] = (NH - 1) - (128 * cchunk[None, :] + j[:, None])
    cf[:, 688:704] = 128 * cchunk[None, :] + j[:, None]
    _CONST['cf'] = cf
    cb = np.zeros((128, 512), np.float32)
    cb[:, 0:128] = np.eye(128)
    perm = np.zeros((128, 128), np.float32)
    for m in range(128):
        partner = m + 32 if (m % 64) < 32 else m - 32
        perm[partner, m] = 1.0
    cb[:, 128:256] = perm
    ang = 2 * np.pi * np.outer(j, j) / 128.0
    cb[:, 256:384] = np.cos(ang)
    cb[:, 384:512] = np.sin(ang)
    _CONST['cb'] = cb.astype(bf)
    n = np.arange(NX)
    rows, cols = n // 64, n % 64
    inv = 10000.0 ** (-np.arange(32, dtype=np.float64) / 32)
    rc = np.zeros((128, NX), np.float64)
    rs = np.zeros((128, NX), np.float64)
    for d in range(128):
        pos = rows if d < 64 else cols
        a = pos * inv[d % 32]
        rc[d] = np.cos(a)
        rs[d] = np.sin(a) * (-1.0 if (d % 64) < 32 else 1.0)
    _CONST['ropec'] = rc.astype(np.float32)
    _CONST['ropes'] = rs.astype(np.float32)
    for nm, N in (('x', NX), ('c', NCX)):
        t = np.arange(N)
        ic = np.zeros((4, N), np.float32)
        for g, w in enumerate((2, 4, 8, 16)):
            lo = np.clip(t - w // 2, 0, N)
            hi = np.clip(t + w // 2, 0, N)
            ic[g] = 1.0 / (hi - lo)
        _CONST['invc_' + nm] = ic
        k = np.arange(N, dtype=np.int64)
        m = (np.outer(k, k) % N).astype(np.float64)
        a = 2 * np.pi * m / N
        sc = 1.0 / math.sqrt(N * 128.0)
        _CONST['dftc_' + nm] = (np.cos(a) * sc).astype(bf)
        _CONST['dfts_' + nm] = (-np.sin(a) * sc).astype(bf)
    return _CONST


class Seq:
    def __init__(self, name, N, who, rope):
        self.name, self.N, self.who, self.rope = name, N, who, rope


def build():
    nc = bass.Bass("TRN2", target_bir_lowering=False)
    P = Prog(nc)

    def inp(name, shape, dt=F32):
        return nc.dram_tensor(name, list(shape), dt, kind="ExternalInput").ap()

    x_in = inp("x", [NH, D])
    ctx_in = inp("ctx", [NCX, D])
    cT_in = inp("cT", [128, 16, 2])
    wada_in = inp("w_ada", [DEPTH, 6, 128, 16, 512])
    bada_in = inp("b_ada", [DEPTH * 3072])
    ng_in = inp("norm_g", [DEPTH, D])
    win_in = inp("w_in", [DEPTH, 24, 128, 16, 512])
    wf_in = inp("w_fourier", [DEPTH, 128, 4, 128])
    wp_in = inp("w_pool", [DEPTH, 128, 4, 128])
    psc_in = inp("pool_scale", [DEPTH, 128, 4])
    rdl_in = inp("ret_decay_logit", [32])
    wup_in = inp("w_up", [DEPTH, 16, 128, 16, 128])
    wout_in = inp("w_out", [DEPTH, 4, 128, 16, 512])
    fng_in = inp("final_norm_g", [D])
    cf_in = inp("cf", [128, 704])
    cb_in = inp("cb", [128, 512], BF16)
    ropec_in = inp("ropec", [128, NH])
    ropes_in = inp("ropes", [128, NH])
    role_in = inp("role", [128, 4])
    invc_in = {'x': inp("invc_x", [4, NH]), 'c': inp("invc_c", [4, NCX])}
    dftc_in = {'x': inp("dftc_x", [NX, NH], BF16), 'c': inp("dftc_c", [NCX, NCX], BF16)}
    dfts_in = {'x': inp("dfts_x", [NX, NH], BF16), 'c': inp("dfts_c", [NCX, NCX], BF16)}
    out = nc.dram_tensor("out", [NH, D], F32, kind="ExternalOutput").ap()

    modrows = P.dram("modrows", [DEPTH, 3, 2, D], F32)
    projT = {'x': P.dram("projT_x", [INW, NH], BF16), 'c': P.dram("projT_c", [INW, NCX], BF16)}
    zT = {'x': P.dram("zT_x", [D, NH], BF16), 'c': P.dram("zT_c", [D, NCX], BF16)}
    xres = {'x': P.dram("xres_x", [NH, D], F32), 'c': P.dram("xres_c", [NCX, D], F32)}
    wub = P.dram("wub", [DEPTH, 16, 128, 16, 128], BF16)
    wob = P.dram("wob", [DEPTH, 4, 128, 16, 512], BF16)
    masks_d = P.dram("masks_d", [128, 16, 128], F32)
    ab_src = [P.dram("ab_src%d" % i, [512, 1024], BF16) for i in range(4)]
    ab_all = [P.dram("ab_all%d" % i, [1024, 1024], BF16) for i in range(4)]
    ed_src = P.dram("ed_src", [8, 1024], BF16)
    ed_all = P.dram("ed_all", [16, 1024], BF16)
    md_src = P.dram("md_src", [4, 3072], F32)
    md_all = P.dram("md_all", [8, 3072], F32)
    st_src = P.dram("st_src", [16 * 128, 128], F32)
    st_all = P.dram("st_all", [2 * 16 * 128, 128], F32)
    rsc = P.dram("rsc", [8, 4, 128, NH], BF16)
    qdec_d = P.dram("qdec_d", [128, 32, 128], F32)
    dbg = {}
    for k, shp in DEBUG.items():
        dbg[k] = nc.dram_tensor("dbg_" + k, list(shp[0]), shp[1], kind="ExternalOutput").ap()

    cf = P.sbp("cf", [128, 704], F32)
    cb = P.sbp("cb", [128, 512], BF16)
    kdec = P.sbp("kdec", [128, 32], F32)
    dch = P.sbp("dch", [128, 32], F32)
    s0 = P.sbp("s0", [128, 16, 128], F32)
    role = P.sbp("role", [128, 4], F32)
    coef = P.sbp("coef", [128, 32], F32)
    tabk = P.sbp("tabk", [128, 32, 16], F32)
    Dpos, Dneg, Iq, ipl1, imn, jfac = cf[:, 0:128], cf[:, 128:256], cf[:, 256:384], cf[:, 384:512], cf[:, 512:640], cf[:, 640:672]
    ident, perm, cs128 = cb[:, 0:128], cb[:, 128:256], cb[:, 256:512]

    seq_c = Seq('c', NCX, 1, False)
    seq_x = Seq('x', NH, 0, True)

    def cast_upout(l):
        for dc in range(0, 16, 4):
            P.dma(wub[l, dc:dc + 4], wup_in[l, dc:dc + 4], w=[('wub', l, dc)], q='pool')
        for cg in range(4):
            P.dma(wob[l, cg], wout_in[l, cg], w=[('wob', l, cg)], q='pool')

    def prologue():
        P.dma(cf[:], cf_in, w=['cf'])
        P.dma(cb[:], cb_in, w=['cb'])
        cT = P.sb("cT", [128, 16, 2], F32)
        scb = P.sb("scb", [128, 16, 2], BF16)
        HW_ = 3072
        badah = P.sb("badah", [2, DEPTH * HW_], F32)
        ngb = P.sb("ngb", [2, D], F32)
        modh = P.sb("modh", [2, DEPTH, HW_], F32)
        mod = P.sb("mod", [2, 6144], F32)
        wab = [P.sb("wab%d" % i, [128, 16, 512], BF16) for i in range(2)]
        pa = P.ps("pa", [128, 512])
        P.dma(cT[:], cT_in, w=['cT'])
        P.act(lambda e: e.activation(out=scb[:], in_=cT[:], func=AF.Silu), r=['cT'], w=['scb'])
        P.dma(badah[:], bc(bada_in, 2), w=['badah'])
        k = 0
        for l in range(DEPTH):
            for cg in range(6):
                wa = wab[k % 2]
                wr = 'wab%d' % (k % 2)
                k += 1
                P.dma(wa[:], wada_in[l, cg], w=[wr], q='pool')
                for kc in range(16):
                    P.pe(lambda e, wa=wa, kc=kc: e.matmul(pa[0:2, :], lhsT=scb[:, kc, :], rhs=wa[:, kc, :], start=(kc == 0), stop=(kc == 15)),
                         r=['scb', wr], w=['pa'])
                P.dve(lambda e, cg=cg, l=l: e.tensor_tensor(out=modh[:, l, cg * 512:(cg + 1) * 512], in0=pa[0:2, :], in1=badah[:, l * HW_ + cg * 512:l * HW_ + (cg + 1) * 512], op=ALU.add),
                      r=['pa', 'badah'], w=['modh'])
        P.dma(md_src.rearrange("(l w) c -> w l c", w=2), modh[:], r=['modh'], w=['md_src'])
        mdn = P.cc("AllGather", [md_src], [md_all], PAIRS, r=['md_src'], w=['md_all'])
        for l in range(DEPTH):
            P.dma(ngb[:], bc(ng_in[l], 2), w=['ngb'])
            for r_ in range(2):
                P.dma(mod[:, r_ * HW_:(r_ + 1) * HW_], md_all[r_ * 4 + l * 2:r_ * 4 + l * 2 + 2, :], r=['md_all'], w=['mod'], extra=[(mdn, 1)])
            P.dve(lambda e: e.scalar_tensor_tensor(out=mod[:, 2048:4096], in0=mod[:, 2048:4096], scalar=1.0, in1=ngb[:], op0=ALU.add, op1=ALU.mult),
                  r=['mod', 'ngb'], w=['mod'])
            for kind in range(3):
                P.dma(modrows[l, kind], mod[:, kind * 2048:(kind + 1) * 2048], r=['mod'], w=[('modrows', l, kind)])
        lgt = P.sb("lgt", [128, 32], F32)
        lg = P.sb("lg", [128, 32], F32)
        tmp32 = P.sb("tmp32", [128, 32], F32)
        masks = P.sb("masks", [128, 16, 128], F32)
        qdec = P.sb("qdec", [128, 32, 128], F32)
        tm = [P.sb("tm%d" % i, [128, 128], F32) for i in range(2)]
        P.dma(lgt[:], bc(rdl_in), w=['lgt'])
        P.act(lambda e: e.activation(out=tmp32[:], in_=lgt[:], func=AF.Exp, scale=-1.0), r=['lgt'], w=['tmp32'])
        P.act(lambda e: e.activation(out=lgt[:], in_=tmp32[:], func=AF.Ln, bias=1.0), r=['tmp32'], w=['lgt'])
        P.dve(lambda e: e.tensor_scalar(out=lg[:], in0=lgt[:], scalar1=-1.0, scalar2=None, op0=ALU.mult), r=['lgt'], w=['lg'])
        P.dve(lambda e: e.tensor_tensor(out=tmp32[:], in0=lg[:], in1=jfac, op=ALU.mult), r=['lg', 'cf', 'tmp32'], w=['tmp32'])
        P.act(lambda e: e.activation(out=kdec[:], in_=tmp32[:], func=AF.Exp), r=['tmp32'], w=['kdec'])
        P.act(lambda e: e.activation(out=dch[:], in_=lg[:], func=AF.Exp, scale=128.0), r=['lg'], w=['dch'])
        for idx_ in range(32):
            esrc = cf[:, 672:688] if (idx_ % 16) < 8 else cf[:, 688:704]
            P.act(lambda e, idx_=idx_, esrc=esrc: e.activation(out=tabk[:, idx_, :], in_=esrc, func=AF.Exp, scale=lg[:, idx_:idx_ + 1]), r=['lg', 'cf'], w=['tabk'])
        P.dma(role[:], role_in, w=['role'])
        P.act(lambda e: e.activation(out=tmp32[:], in_=lg[:], func=AF.Exp, scale=float(NH)), r=['lg', 'kdec'], w=['tmp32'])
        for l in range(DEPTH):
            P.dve(lambda e, l=l: e.tensor_scalar(out=coef[:, l * 16:l * 16 + 8], in0=tmp32[:, l * 16:l * 16 + 8], scalar1=role[:, 1:2], scalar2=role[:, 0:1], op0=ALU.mult, op1=ALU.add), r=['tmp32', 'role'], w=['coef'])
            P.dve(lambda e, l=l: e.tensor_scalar(out=coef[:, l * 16 + 8:l * 16 + 16], in0=tmp32[:, l * 16 + 8:l * 16 + 16], scalar1=role[:, 0:1], scalar2=role[:, 1:2], op0=ALU.mult, op1=ALU.add), r=['tmp32', 'role'], w=['coef'])
        for l in range(DEPTH):
            for h in range(8):
                t = tm[h % 2]
                tr = 'tm%d' % (h % 2)
                i0, i1 = l * 16 + h, l * 16 + 8 + h
                P.dve(lambda e, t=t, i0=i0: e.tensor_scalar(out=t[:], in0=Dpos, scalar1=lg[:, i0:i0 + 1], scalar2=None, op0=ALU.mult), r=['lg', 'cf'], w=[tr])
                P.dve(lambda e, t=t, i1=i1: e.scalar_tensor_tensor(out=t[:], in0=Dneg, scalar=lg[:, i1:i1 + 1], in1=t[:], op0=ALU.mult, op1=ALU.add), r=['lg', 'cf', tr], w=[tr])
                P.act(lambda e, t=t: e.activation(out=t[:], in_=t[:], func=AF.Exp), r=[tr], w=[tr])
                P.dve(lambda e, t=t, l=l, h=h: e.scalar_tensor_tensor(out=masks[:, l * 8 + h, :], in0=t[:], scalar=QS, in1=Iq, op0=ALU.mult, op1=ALU.add), r=[tr, 'cf'], w=['masks'])
                P.act(lambda e, i0=i0: e.activation(out=qdec[:, i0, :], in_=ipl1, func=AF.Exp, scale=lg[:, i0:i0 + 1]), r=['lg', 'cf'], w=['qdec'])
                P.act(lambda e, i1=i1: e.activation(out=qdec[:, i1, :], in_=imn, func=AF.Exp, scale=lg[:, i1:i1 + 1]), r=['lg', 'cf'], w=['qdec'])
        P.dve(lambda e: e.tensor_scalar(out=qdec[:], in0=qdec[:], scalar1=QS, scalar2=None, op0=ALU.mult), r=['qdec'], w=['qdec'])
        P.dma(masks_d, masks[:], r=['masks'], w=['masks_d'])
        P.dma(qdec_d, qdec[:], r=['qdec'], w=['qdec_d'])
        P.flush()

    def phaseP(l, sq, xsrc, chunks, precast=None, pre=None):
        segs = ([(pre[0], pre[1])] if pre is not None else []) + [(sq, xsrc)]
        N = sum(sg_[0].N for sg_ in segs)
        TS = N
        whos = sorted(set(sg_[0].who for sg_ in segs))
        geffs = {w_: P.sb("geff%d" % w_, [128, D], F32) for w_ in whos}
        shifts = {w_: P.sb("shift%d" % w_, [128, D], F32) for w_ in whos}
        hT = P.sb("hT", [128, 16, TS], BF16)
        xb = [P.sb("xb%d" % i, [128, D], F32) for i in range(2)]
        t1s = [P.sb("t1%d" % i, [128, D], F32) for i in range(2)]
        hbs = [P.sb("hb%d" % i, [128, D], BF16) for i in range(2)]
        junk = P.sb("junk", [128, D], BF16)
        sts = [P.sb("st%d" % i, [128, 4], F32) for i in range(2)]
        wbuf = [P.sb("wbuf%d" % i, [128, 16, 512], BF16) for i in range(2)]
        stg = [P.sb("stg%d" % i, [128, TS], BF16) for i in range(2)]
        pT = P.ps("pT", [128, 2048], BF16)
        pb = [P.ps("pb%d" % i, [128, 512]) for i in range(4)]
        tile_src = []
        for (sq_, src_) in segs:
            for t_ in range(sq_.N // 128):
                tile_src.append((src_, t_ * 128, sq_.who))
        for w_ in sorted(whos, key=lambda w_: min(i_ for i_, ts_ in enumerate(tile_src) if ts_[2] == w_)):
            P.dma(geffs[w_][:], bc(modrows[l, 1, w_]), w=['geff%d' % w_])
            P.dma(shifts[w_][:], bc(modrows[l, 0, w_]), w=['shift%d' % w_])
        groups = sorted(set(c // 4 for c in chunks))
        kx = kw = ks = kp = 0
        pieces = []
        if precast is not None:
            for dc_ in range(16):
                pieces.append(lambda dc_=dc_: P.dma(wub[precast, dc_], wup_in[precast, dc_], w=[('wub', precast, dc_)], q='pool'))
                cg_, q_ = dc_ // 4, dc_ % 4
                pieces.append(lambda cg_=cg_, q_=q_: P.dma(wob[precast, cg_][q_ * 32:(q_ + 1) * 32], wout_in[precast, cg_][q_ * 32:(q_ + 1) * 32], w=[('wob', precast, cg_, q_)], q='pool'))
        for sb0 in range(0, N, TS):
            stageB_prev = None
            for tt in range(TS // 128):
                src_, t0, who_ = tile_src[tt]
                geff, shift = geffs[who_], shifts[who_]
                gn, sn = 'geff%d' % who_, 'shift%d' % who_
                xt = xb[kx % 2]
                xr = 'xb%d' % (kx % 2)
                q2 = kx % 2
                t1, hb, st = t1s[q2], hbs[q2], sts[q2]
                t1n, hbn, stn = 't1%d' % q2, 'hb%d' % q2, 'st%d' % q2
                kx += 1
                P.capture()
                P.dma(xt[:], src_[t0:t0 + 128, :], w=[xr])
                P.act(lambda e, xt=xt, st=st: e.activation(out=junk[:], in_=xt[:], func=AF.Square, accum_out=st[:, 0:1]), r=[xr], w=['junk', stn + 'a'])
                P.act(lambda e, st=st: e.activation(out=st[:, 1:2], in_=st[:, 0:1], func=AF.Ln, scale=1.0 / D, bias=EPS), r=[stn + 'a'], w=[stn + 'b'])
                P.act(lambda e, st=st: e.activation(out=st[:, 2:3], in_=st[:, 1:2], func=AF.Exp, scale=-0.5), r=[stn + 'b'], w=[stn + 'c'])
                P.dve(lambda e, xt=xt, st=st, t1=t1, geff=geff: e.scalar_tensor_tensor(out=t1[:], in0=xt[:], scalar=st[:, 2:3], in1=geff[:], op0=ALU.mult, op1=ALU.mult),
                      r=[xr, stn + 'c', gn], w=[t1n])
                XS = 640
                P.dve(lambda e, t1=t1, hb=hb, shift=shift: e.tensor_tensor(out=hb[:, 0:XS], in0=t1[:, 0:XS], in1=shift[:, 0:XS], op=ALU.add), r=[t1n, sn], w=[hbn + 'a'])
                P.pool(lambda e, t1=t1, hb=hb, shift=shift: e.tensor_tensor(out=hb[:, XS:D], in0=t1[:, XS:D], in1=shift[:, XS:D], op=ALU.add), r=[t1n, sn], w=[hbn + 'b'])
                stageA = P.end_capture()
                if stageB_prev is None:
                    P.replay(stageA)
                else:
                    P.replay(stageA, stageB_prev)
                P.capture()
                for kc in range(16):
                    P.pe(lambda e, kc=kc, hb=hb: e.transpose(out=pT[:, kc * 128:(kc + 1) * 128], in_=hb[:, kc * 128:(kc + 1) * 128], identity=ident),
                         r=[hbn + 'a', hbn + 'b', 'cb'], w=['pT'])
                P.act(lambda e, tt=tt: e.activation(out=hT[:, 0:8, tt * 128:(tt + 1) * 128], in_=pT[:, 0:1024].rearrange("p (c n) -> p c n", c=8), func=AF.Copy),
                      r=['pT'], w=[('hT', tt)])
                P.dve(lambda e, tt=tt: e.tensor_copy(out=hT[:, 8:16, tt * 128:(tt + 1) * 128], in_=pT[:, 1024:2048].rearrange("p (c n) -> p c n", c=8)),
                      r=['pT'], w=[('hT', tt)])
                stageB_prev = P.end_capture()
            P.replay(stageB_prev)
            hTr = [('hT', tt) for tt in range(TS // 128)]
            for cg in groups:
                wb = wbuf[kw % 2]
                wr = 'wbuf%d' % (kw % 2)
                kw += 1
                P.dma(wb[:], win_in[l, cg], w=[wr], q='pool')
                if precast is not None:
                    for _ in range(2 if kw <= 8 else 1):
                        if pieces:
                            pieces.pop(0)()
                for cc in range(4):
                    chunk = cg * 4 + cc
                    if chunk not in chunks:
                        continue
                    c0 = chunk * 128
                    sg = stg[ks % 2]
                    sr = 'stg%d' % (ks % 2)
                    ks += 1
                    if (F_G <= c0 < P_X) or (P_G <= c0 < R_Q) or (R_G <= c0 < MG):
                        fn = AF.Silu
                    elif c0 >= MG:
                        fn = AF.Sigmoid
                    else:
                        fn = None
                    for tb in range(0, TS, 512):
                        tw = min(512, TS - tb)
                        pp = pb[kp % 4]
                        pr = 'pb%d' % (kp % 4)
                        kp += 1
                        for kc in range(16):
                            P.pe(lambda e, pp=pp, wb=wb, kc=kc, cc=cc, tb=tb, tw=tw: e.matmul(pp[:, 0:tw], lhsT=wb[:, kc, cc * 128:(cc + 1) * 128], rhs=hT[:, kc, tb:tb + tw], start=(kc == 0), stop=(kc == 15)),
                                 r=[wr] + hTr[tb // 128:(tb + tw) // 128], w=[pr])
                        if fn is None:
                            P.dve(lambda e, pp=pp, sg=sg, tb=tb, tw=tw: e.tensor_copy(out=sg[:, tb:tb + tw], in_=pp[:, 0:tw]), r=[pr], w=[sr])
                        else:
                            P.act(lambda e, pp=pp, sg=sg, tb=tb, tw=tw, fn=fn: e.activation(out=sg[:, tb:tb + tw], in_=pp[:, 0:tw], func=fn), r=[pr], w=[sr])
                    o_ = 0
                    for (sq_, src_) in segs:
                        P.dma(projT[sq_.name][c0:c0 + 128, 0:sq_.N], sg[:, o_:o_ + sq_.N], r=[sr], w=[('projT', sq_.name, chunk)])
                        o_ += sq_.N
        P.flush()

    def phaseF(l, sq, mode='all', ccw=None, noflush=False):
        N = sq.N
        nloc = N // 128
        nch = nloc if mode != 'f2' else 2 * nloc
        KB = min(N, 512)
        pj = projT[sq.name]
        zt = zT[sq.name]
        AB = P.sb("AB", [128, nch, 4, 256], BF16)
        if mode == 'f2':
            ex = [(nm_, 1) for nm_ in ccw]
            for r_ in range(2):
                for k_ in range(4):
                    c0_ = r_ * nloc + k_ * 4
                    P.dma(AB[:, c0_:c0_ + 4].rearrange("p c g f -> p c (g f)"), ab_all[k_][r_ * 512:(r_ + 1) * 512].rearrange("(c p) f -> p c f", p=128),
                          w=[('ABl', r_, k_)], extra=ex)
        fxb = [P.sb("fxb%d" % i, [128, N], BF16) for i in range(2)]
        wF = P.sb("wF", [128, 4, 128], BF16)
        HC = min(nch, 16)
        NDB = 3 if mode == 'f2' else (1 if mode == 'f1' else 4)
        dbuf = [P.sb("dbuf%d" % i, [128, HC if mode != 'f1' else 1, KB], BF16) for i in range(NDB)]
        mx = [P.sb("mx%d" % i, [128, KB], BF16) for i in range(4)]
        fg = [P.sb("fg%d" % i, [128, KB], BF16) for i in range(2)]
        zf = [P.sb("zf%d" % i, [128, KB], BF16) for i in range(2)]
        pf = [P.ps("pf%d" % i, [128, 512]) for i in range(2)] if mode != 'f2' else None
        pm = [P.ps("pm%d" % i, [128, 512]) for i in range(4)] if mode != 'f1' else None
        py = [P.ps("py%d" % i, [128, 512]) for i in range(2)] if mode != 'f1' else None
        if mode != 'f1':
            P.dma(wF[:], wf_in[l], w=['wF'], q='pool')
        ke = 0
        for g in (range(4) if mode != 'f2' else ()):
            fx = fxb[g % 2]
            fr = 'fxb%d' % (g % 2)
            P.dma(fx[:], pj[F_X + g * 128:F_X + (g + 1) * 128, :], w=[fr])
            for ch in range(0, nch, 2):
                pp = pf[ke % 2]
                pr = 'pf%d' % (ke % 2)
                for s in range(2):
                    P.pe(lambda e, pp=pp, fx=fx, ch=ch, s=s: e.matmul(pp[:, s * 256:(s + 1) * 256], lhsT=fx[:, (ch + s) * 128:(ch + s + 1) * 128], rhs=cs128, start=True, stop=True),
                         r=[fr, 'cb'], w=[pr])
                if ke % 2 == 0:
                    P.act(lambda e, pp=pp, ch=ch, g=g: e.activation(out=AB[:, ch:ch + 2, g, :], in_=pp[:].rearrange("p (s n) -> p s n", s=2), func=AF.Copy), r=[pr], w=[('AB', g)])
                else:
                    P.dve(lambda e, pp=pp, ch=ch, g=g: e.tensor_copy(out=AB[:, ch:ch + 2, g, :], in_=pp[:].rearrange("p (s n) -> p s n", s=2)), r=[pr], w=[('AB', g)])
                ke += 1
        ABr = [('AB', g) for g in range(4)]
        if mode == 'f2':
            ABr = [('ABl', r_, k_) for r_ in range(2) for k_ in range(4)]
        if mode == 'f1':
            names = []
            ev = ed_src.rearrange("r (c j) -> (r c) j", j=16)
            P.dma(ev[:, 0:8], pj[P_X:P_X + 512, 0:8], w=['ed_src0'])
            P.dma(ev[:, 8:16], pj[P_X:P_X + 512, N - 8:N], w=['ed_src1'])
            names.append(P.cc("AllGather", [ed_src], [ed_all], PAIRS, r=['ed_src0', 'ed_src1'], w=['ed_all']))
            for k_ in range(4):
                P.dma(ab_src[k_].rearrange("(c p) f -> p c f", p=128), AB[:, k_ * 4:(k_ + 1) * 4].rearrange("p c g f -> p c (g f)"), r=ABr, w=[('ab_src', k_)])
                names.append(P.cc("AllGather", [ab_src[k_]], [ab_all[k_]], PAIRS, r=[('ab_src', k_)], w=[('ab_all', k_)]))
            if not noflush:
                P.flush()
            return names
        def emit_f3(kb_, gs=(0, 1, 2, 3)):
            k0_ = kb_ * KB
            for g in gs:
                i = (kb_ * 4 + g) % 2
                P.dma(fg[i][:], pj[F_G + g * 128:F_G + (g + 1) * 128, k0_:k0_ + KB], w=['fg%d' % i])
                P.pe(lambda e, g=g, i=i: e.matmul(py[i][:, 0:KB], lhsT=wF[:, g, :], rhs=mx[g][:], start=True, stop=True), r=['wF', 'mx%d' % g], w=['py%d' % i])
                P.dve(lambda e, i=i: e.tensor_tensor(out=zf[i][:], in0=py[i][:, 0:KB], in1=fg[i][:], op=ALU.mult), r=['py%d' % i, 'fg%d' % i], w=['zf%d' % i])
                P.dma(zt[g * 128:(g + 1) * 128, k0_:k0_ + KB], zf[i][:], r=['zf%d' % i], w=[('zT', g, kb_)], q='act')

        kd = 0
        for kb in range(N // KB):
            k0 = kb * KB
            first = True
            nsteps = 2 * nch
            step = 0
            for tbl in range(2):
                src = (dftc_in if tbl == 0 else dfts_in)[sq.name]
                for hf in range(nch // HC):
                    db = dbuf[kd % NDB]
                    dr = 'dbuf%d' % (kd % NDB)
                    kd += 1
                    P.dma(db[:], src[hf * HC * 128:(hf + 1) * HC * 128, k0:k0 + KB].rearrange("(c p) k -> p c k", p=128), w=[dr])
                    for c in range(HC):
                        ch = hf * HC + c
                        for g in range(4):
                            P.pe(lambda e, g=g, ch=ch, tbl=tbl, db=db, c=c, step=step: e.matmul(pm[g][:, 0:KB], lhsT=AB[:, ch, g, tbl * 128:(tbl + 1) * 128], rhs=db[:, c, :], start=(step == 0), stop=(step == nsteps - 1)),
                                 r=[dr] + ABr, w=['pm%d' % g])
                        step += 1
                    if tbl == 0 and kb > 0 and nch // HC >= 2 and hf < 2:
                        emit_f3(kb - 1, (2 * hf, 2 * hf + 1))
                    elif tbl == 0 and kb > 0 and nch // HC < 2 and hf == 0:
                        emit_f3(kb - 1)
            for g in range(4):
                if g % 2 == 0 or mode == 'f2':
                    P.act(lambda e, g=g: e.activation(out=mx[g][:], in_=pm[g][:, 0:KB], func=AF.Copy), r=['pm%d' % g], w=['mx%d' % g])
                else:
                    P.dve(lambda e, g=g: e.tensor_copy(out=mx[g][:], in_=pm[g][:, 0:KB]), r=['pm%d' % g], w=['mx%d' % g])
        emit_f3(N // KB - 1)
        if not noflush:
            P.flush()

    def phaseQ(l, sq, noflush=False, ccw=None):
        N = sq.N
        exq = [(ccw, 1)] if ccw is not None else []
        KB = min(N, 512)
        pj = projT[sq.name]
        zt = zT[sq.name]
        pxb = P.sb("pxb", [128, N + 16], BF16)
        X = P.sb("X", [128, N + 16], F32)
        A = P.sb("A", [128, N + 16], F32)
        B = P.sb("B", [128, N + 16], F32)
        invc = P.sb("invc", [128, N], F32)
        pooled = P.sb("pooled", [128, N], BF16)
        pg = P.sb("pg", [128, N], BF16)
        zp = P.sb("zp", [128, N], BF16)
        wP = P.sb("wP", [128, 4, 128], BF16)
        pscale = P.sb("pscale", [128, 4], F32)
        hl = P.sb("hl", [128, 8], BF16)
        hr = P.sb("hr", [128, 8], BF16)
        py = [P.ps("py%d" % i, [128, 512]) for i in range(2)]
        P.dma(wP[:], wp_in[l], w=['wP'], q='pool')
        P.dma(pscale[:], psc_in[l], w=['pscale'])
        P.pool(lambda e: e.memset(pxb[:], 0.0), w=['pxb'])
        k = 0
        for g in range(4):
            P.dma(pxb[:, 8:N + 8], pj[P_X + g * 128:P_X + (g + 1) * 128, :], r=[], w=['pxb'])
            P.dma(pg[:], pj[P_G + g * 128:P_G + (g + 1) * 128, :], w=['pg'])
            P.dma(invc[:], bc(invc_in[sq.name][g]), w=['invc'])
            P.dve(lambda e: e.tensor_copy(out=X[:], in_=pxb[:]), r=['pxb'], w=['X'])
            if sq.name == 'x':
                G0 = ed_all[0:8].rearrange("r (c j) -> (r c) j", j=16)
                G1 = ed_all[8:16].rearrange("r (c j) -> (r c) j", j=16)
                P.dma(hl[:], G0[g * 128:(g + 1) * 128, 8:16], w=['hl'], extra=exq)
                P.dma(hr[:], G1[g * 128:(g + 1) * 128, 0:8], w=['hr'], extra=exq)
                P.dve(lambda e: e.tensor_scalar(out=X[:, 0:8], in0=hl[:], scalar1=role[:, 1:2], scalar2=None, op0=ALU.mult), r=['hl', 'role'], w=['X'])
                P.dve(lambda e: e.tensor_scalar(out=X[:, N + 8:N + 16], in0=hr[:], scalar1=role[:, 0:1], scalar2=None, op0=ALU.mult), r=['hr', 'role'], w=['X'])
            P.dve(lambda e: e.tensor_tensor(out=A[:, 1:N + 16], in0=X[:, 0:N + 15], in1=X[:, 1:N + 16], op=ALU.add), r=['X'], w=['A'])
            cur, curr, oth, othr = A, 'A', B, 'B'
            lo, hi = 1, N + 16
            for s in range(g):
                sh = 1 << s
                P.dve(lambda e, cur=cur, oth=oth, lo=lo, hi=hi, sh=sh: e.tensor_tensor(out=oth[:, lo + sh:hi - sh], in0=cur[:, lo:hi - 2 * sh], in1=cur[:, lo + 2 * sh:hi], op=ALU.add),
                      r=[curr], w=[othr])
                lo, hi = lo + sh, hi - sh
                cur, curr, oth, othr = oth, othr, cur, curr
            P.dve(lambda e, cur=cur, oth=oth: e.tensor_tensor(out=oth[:, 8:N + 8], in0=cur[:, 8:N + 8], in1=invc[:], op=ALU.mult), r=[curr, 'invc'], w=[othr])
            P.dve(lambda e, oth=oth: e.tensor_tensor(out=pooled[:], in0=oth[:, 8:N + 8], in1=X[:, 8:N + 8], op=ALU.subtract), r=[othr, 'X'], w=['pooled'])
            for tb in range(0, N, KB):
                i = k % 2
                k += 1
                P.pe(lambda e, g=g, i=i, tb=tb: e.matmul(py[i][:, 0:KB], lhsT=wP[:, g, :], rhs=pooled[:, tb:tb + KB], start=True, stop=True), r=['wP', 'pooled'], w=['qpy%d' % i])
                P.dve(lambda e, g=g, i=i, tb=tb: e.scalar_tensor_tensor(out=zp[:, tb:tb + KB], in0=py[i][:, 0:KB], scalar=pscale[:, g:g + 1], in1=pg[:, tb:tb + KB], op0=ALU.mult, op1=ALU.mult),
                      r=['qpy%d' % i, 'pscale', 'pg'], w=['zp'])
            P.dma(zt[512 + g * 128:512 + (g + 1) * 128, :], zp[:], r=['zp'], w=[('zT', 4 + g)], q='act')
        if not noflush:
            P.flush()

    def phaseR(l, sq, states_only):
        N = sq.N
        nch = N // 128
        G4 = min(nch, 4)
        pj = projT[sq.name]
        zt = zT[sq.name]
        is_ctx = sq.who == 1
        qT = P.sb("qT", [128, N], BF16)
        kT = P.sb("kT", [128, N], BF16)
        vT = P.sb("vT", [128, N], BF16)
        vtok = P.sb("vtok", [128, nch, 128], BF16)
        ktok = P.sb("ktok", [128, nch, 128], BF16)
        kdf = P.sb("kdf", [128, nch, 128], BF16)
        kdb = P.sb("kdb", [128, nch, 128], BF16)
        Sf = P.sb("Sf", [128, nch, 128], BF16)
        Sb = P.sb("Sb", [128, nch, 128], BF16)
        Sx = [P.sb("Sx%d" % i, [128, 128], F32) for i in range(2)]
        ptr = P.ps("ptr", [128, 1024], BF16)
        pkv = [P.ps("pkv%d" % i, [128, 512]) for i in range(2)]
        if sq.rope:
            ropec = P.sb("ropec", [128, N], F32)
            ropes = P.sb("ropes", [128, N], F32)
            qr = P.sb("qr", [128, N], BF16)
            kr = P.sb("kr", [128, N], BF16)
            rt1 = P.sb("rt1", [128, 512], F32)
            rt2 = P.sb("rt2", [128, 512], F32)
            prot = P.ps("prot", [128, 512])
            P.dma(ropec[:], ropec_in, w=['ropec'])
            P.dma(ropes[:], ropes_in, w=['ropes'])
        else:
            qr, kr = qT, kT
        if not states_only:
            masks = P.sb("masks", [128, 8, 128], F32)
            qdec = P.sb("qdec", [128, 16, 128], F32)
            rg = P.sb("rg", [128, N], BF16)
            qf = P.sb("qf", [128, N], BF16)
            qb = P.sb("qb", [128, N], BF16)
            zr = P.sb("zr", [128, N], BF16)
            PT = P.sb("PT", [128, G4, 128], BF16)
            on = P.sb("on", [128, G4, 128], BF16)
            jk = P.sb("jk", [128, 128], BF16)
            rs = P.sb("rs", [128, 3, G4], F32)
            psc = P.ps("psc", [128, 512])
            po = P.ps("po", [128, 512])
            pT2 = P.ps("pT2", [128, 1024], BF16)
            P.dma(masks[:], masks_d[:, l * 8:(l + 1) * 8, :], w=['masks'])
            P.dma(qdec[:], qdec_d[:, l * 16:(l + 1) * 16, :], w=['qdec'])
        kk = 0
        for h in range(R_HEADS):
            P.dma(kT[:], pj[R_K + h * 128:R_K + (h + 1) * 128, :], w=['kT'])
            P.dma(vT[:], pj[R_V + h * 128:R_V + (h + 1) * 128, :], w=['vT'])
            if not states_only:
                P.dma(qT[:], pj[R_Q + h * 128:R_Q + (h + 1) * 128, :], w=['qT'])
                P.dma(rg[:], pj[R_G + h * 128:R_G + (h + 1) * 128, :], w=['rg'])
            qrr, krr = ('qr', 'kr') if sq.rope else ('qT', 'kT')
            if sq.rope:
                todo = [(kT, 'kT', kr, 'kr')]
                if not states_only:
                    todo.append((qT, 'qT', qr, 'qr'))
                for (src, srn, dst, dsn) in todo:
                    for tb in range(0, N, 512):
                        P.pe(lambda e, src=src, tb=tb: e.matmul(prot[:], lhsT=perm, rhs=src[:, tb:tb + 512], start=True, stop=True), r=[srn, 'cb'], w=['prot'])
                        P.dve(lambda e, tb=tb: e.tensor_tensor(out=rt1[:], in0=prot[:], in1=ropes[:, tb:tb + 512], op=ALU.mult), r=['prot', 'ropes'], w=['rt1'])
                        P.pool(lambda e, src=src, tb=tb: e.tensor_tensor(out=rt2[:], in0=src[:, tb:tb + 512], in1=ropec[:, tb:tb + 512], op=ALU.mult), r=[srn, 'ropec'], w=['rt2'])
                        P.pool(lambda e, dst=dst, tb=tb: e.tensor_tensor(out=dst[:, tb:tb + 512], in0=rt1[:], in1=rt2[:], op=ALU.add), r=['rt1', 'rt2'], w=[dsn])
            for (src, srn, dst, dsn) in ((vT, 'vT', vtok, 'vtok'), (kr, krr, ktok, 'ktok')):
                for c8 in range(0, nch, 8):
                    n8 = min(8, nch - c8)
                    for c in range(n8):
                        P.pe(lambda e, src=src, c=c, c8=c8: e.transpose(out=ptr[:, c * 128:(c + 1) * 128], in_=src[:, (c8 + c) * 128:(c8 + c + 1) * 128], identity=ident), r=[srn, 'cb'], w=['ptr'])
                    if kk % 2 == 0:
                        P.act(lambda e, dst=dst, c8=c8, n8=n8: e.activation(out=dst[:, c8:c8 + n8, :], in_=ptr[:, 0:n8 * 128].rearrange("p (c n) -> p c n", c=n8), func=AF.Copy), r=['ptr'], w=[dsn])
                    else:
                        P.dve(lambda e, dst=dst, c8=c8, n8=n8: e.tensor_copy(out=dst[:, c8:c8 + n8, :], in_=ptr[:, 0:n8 * 128].rearrange("p (c n) -> p c n", c=n8)), r=['ptr'], w=[dsn])
                    kk += 1
            if0, ib0 = l * 16 + h, l * 16 + 8 + h
            P.act(lambda e, i=if0: e.activation(out=kdf[:], in_=ktok[:], func=AF.Copy, scale=kdec[:, i:i + 1]), r=['ktok', 'kdec'], w=['kdf'])
            P.act(lambda e, i=ib0: e.activation(out=kdb[:], in_=ktok[:], func=AF.Copy, scale=kdec[:, i:i + 1]), r=['ktok', 'kdec'], w=['kdb'])
            if R_STAGE < 2:
                continue
            for d in range(2):
                kd_, kdn = (kdf, 'kdf') if d == 0 else (kdb, 'kdb')
                Sall, Sn = (Sf, 'Sf') if d == 0 else (Sb, 'Sb')
                idx = l * 16 + d * 8 + h
                order = list(range(nch)) if d == 0 else list(range(nch - 1, -1, -1))
                cur = 0
                if is_ctx:
                    P.dve(lambda e: e.memset(Sx[0][:], 0.0), w=['Sx0'])
                else:
                    P.dve(lambda e, d=d, h=h: e.tensor_copy(out=Sx[0][:], in_=s0[:, d * 8 + h, :]), r=['s0'], w=['Sx0'])
                P.act(lambda e, Sall=Sall, c=order[0]: e.activation(out=Sall[:, c, :], in_=Sx[0][:], func=AF.Copy), r=['Sx0'], w=[Sn])
                for oi, c in enumerate(order):
                    pk = pkv[oi % 2]
                    P.pe(lambda e, kd_=kd_, c=c, pk=pk: e.matmul(pk[:, 0:128], lhsT=kd_[:, c, :], rhs=vtok[:, c, :], start=True, stop=True), r=[kdn, 'vtok'], w=[('pkv', oi % 2)])
                    nxt = 1 - cur
                    last = oi == nch - 1
                    if last and is_ctx:
                        P.dve(lambda e, cur=cur, pk=pk, idx=idx, d=d, h=h: e.scalar_tensor_tensor(out=s0[:, d * 8 + h, :], in0=Sx[cur][:], scalar=dch[:, idx:idx + 1], in1=pk[:, 0:128], op0=ALU.mult, op1=ALU.add),
                              r=['Sx%d' % cur, 'dch', ('pkv', oi % 2)], w=['s0'])
                    elif not last:
                        P.dve(lambda e, cur=cur, nxt=nxt, pk=pk, idx=idx: e.scalar_tensor_tensor(out=Sx[nxt][:], in0=Sx[cur][:], scalar=dch[:, idx:idx + 1], in1=pk[:, 0:128], op0=ALU.mult, op1=ALU.add),
                              r=['Sx%d' % cur, 'dch', ('pkv', oi % 2)], w=['Sx%d' % nxt])
                        P.act(lambda e, Sall=Sall, c2=order[oi + 1], nxt=nxt: e.activation(out=Sall[:, c2, :], in_=Sx[nxt][:], func=AF.Copy), r=['Sx%d' % nxt], w=[Sn])
                        cur = nxt
            if states_only or R_STAGE < 3:
                continue
            P.dve(lambda e, i=h: e.tensor_tensor(out=qf[:].rearrange("p (c i) -> p c i", i=128), in0=qr[:].rearrange("p (c i) -> p c i", i=128),
                                                  in1=qdec[:, i, :].unsqueeze(1).broadcast_to([128, nch, 128]), op=ALU.mult), r=[qrr, 'qdec'], w=['qf'])
            P.pool(lambda e, i=8 + h: e.tensor_tensor(out=qb[:].rearrange("p (c i) -> p c i", i=128), in0=qr[:].rearrange("p (c i) -> p c i", i=128),
                                                      in1=qdec[:, i, :].unsqueeze(1).broadcast_to([128, nch, 128]), op=ALU.mult), r=[qrr, 'qdec'], w=['qb'])
            if R_STAGE < 4:
                continue
            for c4 in range(0, nch, G4):
                for cc in range(G4):
                    c = c4 + cc
                    P.pe(lambda e, c=c, cc=cc: e.matmul(psc[:, cc * 128:(cc + 1) * 128], lhsT=kr[:, c * 128:(c + 1) * 128], rhs=qr[:, c * 128:(c + 1) * 128], start=True, stop=True), r=[krr, qrr], w=['psc'])
                P.dve(lambda e, h=h: e.tensor_tensor(out=PT[:], in0=psc[:, 0:G4 * 128].rearrange("p (c i) -> p c i", i=128), in1=masks[:, h, :].unsqueeze(1).broadcast_to([128, G4, 128]), op=ALU.mult),
                      r=['psc', 'masks'], w=['PT'])
                for cc in range(G4):
                    c = c4 + cc
                    P.pe(lambda e, c=c, cc=cc: e.matmul(po[:, cc * 128:(cc + 1) * 128], lhsT=PT[:, cc, :], rhs=vtok[:, c, :], start=True, stop=False), r=['PT', 'vtok'], w=['po'])
                    P.pe(lambda e, c=c, cc=cc: e.matmul(po[:, cc * 128:(cc + 1) * 128], lhsT=qf[:, c * 128:(c + 1) * 128], rhs=Sf[:, c, :], start=False, stop=False), r=['qf', 'Sf'], w=['po'])
                    P.pe(lambda e, c=c, cc=cc: e.matmul(po[:, cc * 128:(cc + 1) * 128], lhsT=qb[:, c * 128:(c + 1) * 128], rhs=Sb[:, c, :], start=False, stop=True), r=['qb', 'Sb'], w=['po'])
                for cc in range(G4):
                    P.act(lambda e, cc=cc: e.activation(out=jk[:], in_=po[:, cc * 128:(cc + 1) * 128], func=AF.Square, accum_out=rs[:, 0, cc:cc + 1]), r=['po'], w=['jk', 'rs0'])
                P.act(lambda e: e.activation(out=rs[:, 1, :], in_=rs[:, 0, :], func=AF.Ln, scale=1.0 / 128, bias=EPS), r=['rs0'], w=['rs1'])
                P.act(lambda e: e.activation(out=rs[:, 2, :], in_=rs[:, 1, :], func=AF.Exp, scale=-0.5), r=['rs1'], w=['rs2'])
                P.dve(lambda e: e.tensor_tensor(out=on[:], in0=po[:, 0:G4 * 128].rearrange("p (c i) -> p c i", i=128), in1=rs[:, 2, :].unsqueeze(2).broadcast_to([128, G4, 128]), op=ALU.mult),
                      r=['po', 'rs2'], w=['on'])
                for cc in range(G4):
                    P.pe(lambda e, cc=cc: e.transpose(out=pT2[:, cc * 128:(cc + 1) * 128], in_=on[:, cc, :], identity=ident), r=['on', 'cb'], w=['pT2'])
                P.dve(lambda e, c4=c4: e.tensor_tensor(out=zr[:, c4 * 128:(c4 + G4) * 128], in0=pT2[:, 0:G4 * 128], in1=rg[:, c4 * 128:(c4 + G4) * 128], op=ALU.mult), r=['pT2', 'rg'], w=['zr'])
            P.dma(zt[1024 + h * 128:1024 + (h + 1) * 128, :], zr[:], r=['zr'], w=[('zT', 8 + h)])
        P.flush()

    def rope_emit(src, srn, dst, dsn, N, ropec, ropes, prot, rt1, rt2, cnt):
        for tb in range(0, N, 512):
            i = cnt[0] % 2
            cnt[0] += 1
            P.pe(lambda e, src=src, tb=tb, i=i: e.matmul(prot[i][:], lhsT=perm, rhs=src[:, tb:tb + 512], start=True, stop=True), r=[srn, 'cb'], w=['prot%d' % i])
            P.dve(lambda e, tb=tb, i=i: e.tensor_tensor(out=rt1[i][:], in0=prot[i][:], in1=ropes[:, tb:tb + 512], op=ALU.mult), r=['prot%d' % i, 'ropes'], w=['rt1%d' % i])
            P.pool(lambda e, src=src, tb=tb, i=i: e.tensor_tensor(out=rt2[i][:], in0=src[:, tb:tb + 512], in1=ropec[:, tb:tb + 512], op=ALU.mult), r=[srn, 'ropec'], w=['rt2%d' % i])
            P.pool(lambda e, dst=dst, tb=tb, i=i: e.tensor_tensor(out=dst[:, tb:tb + 512], in0=rt1[i][:], in1=rt2[i][:], op=ALU.add), r=['rt1%d' % i, 'rt2%d' % i], w=[dsn])

    def phaseR0x(l):
        N = NH
        nch = N // 128
        pj = projT['x']
        ropec = P.sb("ropec", [128, N], F32)
        ropes = P.sb("ropes", [128, N], F32)
        kT = [P.sb("kT%d" % i, [128, N], BF16) for i in range(2)]
        vT = [P.sb("vT%d" % i, [128, N], BF16) for i in range(2)]
        kr = [P.sb("kr%d" % i, [128, N], BF16) for i in range(2)]
        vtok = [P.sb("vtok%d" % i, [128, nch, 128], BF16) for i in range(2)]
        ktok = [P.sb("ktok%d" % i, [128, nch, 128], BF16) for i in range(2)]
        kdf = [P.sb("kdf%d" % i, [128, nch, 128], BF16) for i in range(2)]
        kdb = [P.sb("kdb%d" % i, [128, nch, 128], BF16) for i in range(2)]
        kfin = [[P.sb("kfin%d_%d" % (d_, i), [128, nch, 128], BF16) for i in range(2)] for d_ in range(2)]
        Lst = P.sb("Lst", [128, 16, 128], F32)
        rt1 = [P.sb("rt1%d" % i, [128, 512], F32) for i in range(2)]
        rt2 = [P.sb("rt2%d" % i, [128, 512], F32) for i in range(2)]
        prot = [P.ps("prot%d" % i, [128, 512]) for i in range(2)]
        ptr = [P.ps("ptr%d" % i, [128, 1024], BF16) for i in range(2)]
        pkv = [P.ps("pkv%d" % i, [128, 512]) for i in range(2)]
        P.dma(ropec[:], ropec_in, w=['ropec'])
        P.dma(ropes[:], ropes_in, w=['ropes'])
        cnt = [0]
        kk = 0
        kq = 0
        def stA1(h):
            p = h % 2
            P.capture()
            P.dma(kT[p][:], pj[R_K + h * 128:R_K + (h + 1) * 128, :], w=['kT%d' % p])
            P.dma(vT[p][:], pj[R_V + h * 128:R_V + (h + 1) * 128, :], w=['vT%d' % p])
            rope_emit(kT[p], 'kT%d' % p, kr[p], 'kr%d' % p, N, ropec, ropes, prot, rt1, rt2, cnt)
            return P.end_capture()

        def stA2(h):
            p = h % 2
            P.capture()
            for (src, srn, dst, dsn) in ((vT[p], 'vT%d' % p, vtok[p], 'vtok%d' % p), (kr[p], 'kr%d' % p, ktok[p], 'ktok%d' % p)):
                for c8 in range(0, nch, 8):
                    pt = ptr[kkc[0] % 2]
                    ptn = 'ptr%d' % (kkc[0] % 2)
                    for c in range(8):
                        P.pe(lambda e, src=src, c=c, c8=c8, pt=pt: e.transpose(out=pt[:, c * 128:(c + 1) * 128], in_=src[:, (c8 + c) * 128:(c8 + c + 1) * 128], identity=ident), r=[srn, 'cb'], w=[ptn])
                    if kkc[0] % 2 == 0:
                        P.act(lambda e, dst=dst, c8=c8, pt=pt: e.activation(out=dst[:, c8:c8 + 8, :], in_=pt[:].rearrange("p (c n) -> p c n", c=8), func=AF.Copy), r=[ptn], w=[dsn])
                    else:
                        P.dve(lambda e, dst=dst, c8=c8, pt=pt: e.tensor_copy(out=dst[:, c8:c8 + 8, :], in_=pt[:].rearrange("p (c n) -> p c n", c=8)), r=[ptn], w=[dsn])
                    kkc[0] += 1
            if0, ib0 = l * 16 + h, l * 16 + 8 + h
            P.act(lambda e, i=if0, p=p: e.activation(out=kdf[p][:], in_=ktok[p][:], func=AF.Copy, scale=kdec[:, i:i + 1]), r=['ktok%d' % p, 'kdec'], w=['kdf%d' % p])
            P.act(lambda e, i=ib0, p=p: e.activation(out=kdb[p][:], in_=ktok[p][:], func=AF.Copy, scale=kdec[:, i:i + 1]), r=['ktok%d' % p, 'kdec'], w=['kdb%d' % p])
            P.dma(rsc[h, 0], kr[p][:], r=['kr%d' % p], w=[('rsc', h, 0)], q='act')
            P.dma(rsc[h, 1], vtok[p][:].rearrange("p c n -> p (c n)"), r=['vtok%d' % p], w=[('rsc', h, 1)], q='act')
            P.dma(rsc[h, 2], kdf[p][:].rearrange("p c n -> p (c n)"), r=['kdf%d' % p], w=[('rsc', h, 2)], q='act')
            P.dma(rsc[h, 3], kdb[p][:].rearrange("p c n -> p (c n)"), r=['kdb%d' % p], w=[('rsc', h, 3)], q='act')
            for d in range(2):
                idx = l * 16 + d * 8 + h
                P.dve(lambda e, d=d, p=p, idx=idx: e.tensor_tensor(out=kfin[d][p][:], in0=ktok[p][:], in1=tabk[:, idx, :].unsqueeze(2).broadcast_to([128, nch, 128]), op=ALU.mult),
                      r=['ktok%d' % p, 'tabk'], w=['kfin%d_%d' % (d, p)])
            return P.end_capture()

        def stB(h):
            p = h % 2
            P.capture()
            for d in range(2):
                pk = pkv[kqc[0] % 2]
                pkn = 'pkv%d' % (kqc[0] % 2)
                kqc[0] += 1
                for c in range(nch):
                    P.pe(lambda e, d=d, c=c, pk=pk, p=p: e.matmul(pk[:, 0:128], lhsT=kfin[d][p][:, c, :], rhs=vtok[p][:, c, :], start=(c == 0), stop=(c == nch - 1)), r=['kfin%d_%d' % (d, p), 'vtok%d' % p], w=[pkn])
                if d == 0:
                    P.act(lambda e, pk=pk, d=d, h=h: e.activation(out=Lst[:, d * 8 + h, :], in_=pk[:, 0:128], func=AF.Copy), r=[pkn], w=['Lst'])
                else:
                    P.dve(lambda e, pk=pk, d=d, h=h: e.tensor_copy(out=Lst[:, d * 8 + h, :], in_=pk[:, 0:128]), r=[pkn], w=['Lst'])
            return P.end_capture()

        kkc = [0]
        kqc = [0]
        for sst in range(8 + 2):
            lists = []
            if sst < 8:
                lists.append(stA1(sst))
            if 0 <= sst - 1 < 8:
                lists.append(stA2(sst - 1))
            if 0 <= sst - 2 < 8:
                lists.append(stB(sst - 2))
            P.replay(*lists)
        P.dma(st_src.rearrange("(k p) e -> p k e", p=128), Lst[:], r=['Lst'], w=['st_src'])
        name = P.cc("AllGather", [st_src], [st_all], PAIRS, r=['st_src'], w=['st_all'])
        P.flush()
        return name

    def phaseRx(l, ccw, precast=None):
        N = NH
        nch = N // 128
        G4 = 4
        pj = projT['x']
        zt = zT['x']
        ropec = P.sb("ropec", [128, N], F32)
        ropes = P.sb("ropes", [128, N], F32)
        masks = P.sb("masks", [128, 8, 128], F32)
        qdec = P.sb("qdec", [128, 16, 128], F32)
        names = ['qT', 'rg', 'kr', 'vtok', 'kdf', 'kdb', 'qr', 'qf', 'qb', 'Sf', 'Sb', 'zr']
        NS = 3
        T = {nm: [P.sb("%s%d" % (nm, i), [128, N], BF16) for i in range(NS)] for nm in names}
        Ii = [P.sb("Ii%d" % i, [128, 128], F32) for i in range(2)]
        tis = [P.sb("ti%d" % i, [128, 128], F32) for i in range(2)]
        Sxd = [[P.sb("Sx%d_%d" % (d_, i), [128, 128], F32) for i in range(2)] for d_ in range(2)]
        PT = [P.sb("PT%d" % i, [128, G4, 128], BF16) for i in range(2)]
        on = [P.sb("on%d" % i, [128, G4, 128], BF16) for i in range(2)]
        jk = P.sb("jk", [128, 128], BF16)
        rs = [P.sb("rs%d" % i, [128, 3, G4], F32) for i in range(2)]
        rt1 = [P.sb("rt1%d" % i, [128, 512], F32) for i in range(2)]
        rt2 = [P.sb("rt2%d" % i, [128, 512], F32) for i in range(2)]
        prot = [P.ps("prot%d" % i, [128, 512]) for i in range(1)]
        prot = [prot[0], prot[0]]
        pkv = [P.ps("pkv%d" % i, [128, 512]) for i in range(2)]
        psc = [P.ps("psc%d" % i, [128, 512]) for i in range(2)]
        po = [P.ps("po%d" % i, [128, 512]) for i in range(2)]
        pT2 = P.ps("pT2", [128, 1024], BF16)
        P.dma(ropec[:], ropec_in, w=['ropec'])
        P.dma(ropes[:], ropes_in, w=['ropes'])
        P.dma(masks[:], masks_d[:, l * 8:(l + 1) * 8, :], w=['masks'])
        P.dma(qdec[:], qdec_d[:, l * 16:(l + 1) * 16, :], w=['qdec'])
        stv = st_all.rearrange("(r k p) e -> r k p e", r=2, k=16)
        if precast is not None:
            cast_upout(precast)
        cnt = [0]
        kq = 0
        kg = 0
        c3 = lambda t: t[:].rearrange("p (c i) -> p c i", i=128)
        stage_lists = {}
        for h in range(8):
            p = h % NS
            R = lambda nm, p=p: '%s%d' % (nm, p)
            qT, rg, kr, vtok, kdf, kdb, qr, qf, qb, Sf, Sb, zr = [T[nm][p] for nm in names]
            P.capture()
            P.dma(qT[:], pj[R_Q + h * 128:R_Q + (h + 1) * 128, :], w=[R('qT')])
            P.dma(rg[:], pj[R_G + h * 128:R_G + (h + 1) * 128, :], w=[R('rg')])
            P.dma(kr[:], rsc[h, 0], w=[R('kr')])
            P.dma(vtok[:], rsc[h, 1], w=[R('vtok')])
            P.dma(kdf[:], rsc[h, 2], w=[R('kdf')])
            P.dma(kdb[:], rsc[h, 3], w=[R('kdb')])
            for tb in range(0, N, 512):
                i = cnt[0] % 2
                cnt[0] += 1
                P.pe(lambda e, qT=qT, tb=tb: e.matmul(prot[0][:], lhsT=perm, rhs=qT[:, tb:tb + 512], start=True, stop=True), r=[R('qT'), 'cb'], w=['prot'])
                P.dve(lambda e, tb=tb, i=i: e.tensor_tensor(out=rt1[i][:], in0=prot[0][:], in1=ropes[:, tb:tb + 512], op=ALU.mult), r=['prot', 'ropes'], w=['rt1%d' % i])
                P.pool(lambda e, qT=qT, tb=tb, i=i: e.tensor_tensor(out=rt2[i][:], in0=qT[:, tb:tb + 512], in1=ropec[:, tb:tb + 512], op=ALU.mult), r=[R('qT'), 'ropec'], w=['rt2%d' % i])
                P.pool(lambda e, qr=qr, tb=tb, i=i: e.tensor_tensor(out=qr[:, tb:tb + 512], in0=rt1[i][:], in1=rt2[i][:], op=ALU.add), r=['rt1%d' % i, 'rt2%d' % i], w=[R('qr')])
            P.pool(lambda e, i=h, qf=qf, qr=qr: e.tensor_tensor(out=c3(qf), in0=c3(qr), in1=qdec[:, i, :].unsqueeze(1).broadcast_to([128, nch, 128]), op=ALU.mult), r=[R('qr'), 'qdec'], w=[R('qf')])
            P.pool(lambda e, i=8 + h, qb=qb, qr=qr: e.tensor_tensor(out=c3(qb), in0=c3(qr), in1=qdec[:, i, :].unsqueeze(1).broadcast_to([128, nch, 128]), op=ALU.mult), r=[R('qr'), 'qdec'], w=[R('qb')])
            stage_lists[('A1', h)] = P.end_capture()
            P.capture()
            dlists = []
            for d in range(2):
                P.capture()
                kd_, kdn = (kdf, R('kdf')) if d == 0 else (kdb, R('kdb'))
                Sall, Sn = (Sf, R('Sf')) if d == 0 else (Sb, R('Sb'))
                idx = l * 16 + d * 8 + h
                order = list(range(nch)) if d == 0 else list(range(nch - 1, -1, -1))
                ii = Ii[d]
                tid = tis[d]
                SxD = Sxd[d]
                pk = pkv[d]
                pkn = 'pkv%d' % d
                src = stv[0, h] if d == 0 else stv[1, 8 + h]
                P.dma(ii[:], src, w=['Ii%d' % d], extra=[(ccw, 1)])
                P.dve(lambda e, d=d, h=h, idx=idx, tid=tid: e.tensor_scalar(out=tid[:], in0=s0[:, d * 8 + h, :], scalar1=coef[:, idx:idx + 1], scalar2=None, op0=ALU.mult), r=['s0', 'coef'], w=['ti%d' % d])
                P.dve(lambda e, d=d, ii=ii, tid=tid, SxD=SxD: e.scalar_tensor_tensor(out=SxD[0][:], in0=ii[:], scalar=role[:, (1 - d):(2 - d)], in1=tid[:], op0=ALU.mult, op1=ALU.add), r=['Ii%d' % d, 'role', 'ti%d' % d], w=['Sx%d_0' % d])
                P.act(lambda e, Sall=Sall, c=order[0], SxD=SxD: e.activation(out=Sall[:, c * 128:(c + 1) * 128], in_=SxD[0][:], func=AF.Copy), r=['Sx%d_0' % d], w=[Sn])
                cur = 0
                for oi, c in enumerate(order[:-1]):
                    P.pe(lambda e, kd_=kd_, c=c, pk=pk, vtok=vtok: e.matmul(pk[:, 0:128], lhsT=kd_[:, c * 128:(c + 1) * 128], rhs=vtok[:, c * 128:(c + 1) * 128], start=True, stop=True), r=[kdn, R('vtok')], w=[pkn])
                    nxt = 1 - cur
                    P.dve(lambda e, cur=cur, nxt=nxt, pk=pk, idx=idx, SxD=SxD: e.scalar_tensor_tensor(out=SxD[nxt][:], in0=SxD[cur][:], scalar=dch[:, idx:idx + 1], in1=pk[:, 0:128], op0=ALU.mult, op1=ALU.add),
                          r=['Sx%d_%d' % (d, cur), 'dch', pkn], w=['Sx%d_%d' % (d, nxt)])
                    P.act(lambda e, Sall=Sall, c2=order[oi + 1], nxt=nxt, SxD=SxD: e.activation(out=Sall[:, c2 * 128:(c2 + 1) * 128], in_=SxD[nxt][:], func=AF.Copy), r=['Sx%d_%d' % (d, nxt)], w=[Sn])
                    cur = nxt
                dlists.append(P.end_capture())
            P.replay(dlists[0], dlists[1])
            stage_lists[('A2', h)] = P.end_capture()
            P.capture()
            NG = nch // G4

            def stX(g, h=h, kr=kr, qr=qr):
                g2 = g % 2
                c4 = g * G4
                ps_, PT_ = psc[g2], PT[g2]
                pscn, PTn = 'psc%d' % g2, 'PT%d' % g2
                for cc in range(G4):
                    c = c4 + cc
                    P.pe(lambda e, c=c, cc=cc, ps_=ps_: e.matmul(ps_[:, cc * 128:(cc + 1) * 128], lhsT=kr[:, c * 128:(c + 1) * 128], rhs=qr[:, c * 128:(c + 1) * 128], start=True, stop=True), r=[R('kr'), R('qr')], w=[pscn])
                P.dve(lambda e, ps_=ps_, PT_=PT_: e.tensor_tensor(out=PT_[:], in0=ps_[:].rearrange("p (c i) -> p c i", i=128), in1=masks[:, h, :].unsqueeze(1).broadcast_to([128, G4, 128]), op=ALU.mult),
                      r=[pscn, 'masks'], w=[PTn])

            def stY(g, vtok=vtok, qf=qf, qb=qb, Sf=Sf, Sb=Sb):
                g2 = g % 2
                c4 = g * G4
                po_, PT_, on_, rs_ = po[g2], PT[g2], on[g2], rs[g2]
                pon, PTn, onn, rsn = 'po%d' % g2, 'PT%d' % g2, 'on%d' % g2, 'rs%d' % g2
                for cc in range(G4):
                    c = c4 + cc
                    sl = slice(c * 128, (c + 1) * 128)
                    P.pe(lambda e, cc=cc, sl=sl: e.matmul(po_[:, cc * 128:(cc + 1) * 128], lhsT=PT_[:, cc, :], rhs=vtok[:, sl], start=True, stop=False), r=[PTn, R('vtok')], w=[pon])
                    P.pe(lambda e, cc=cc, sl=sl: e.matmul(po_[:, cc * 128:(cc + 1) * 128], lhsT=qf[:, sl], rhs=Sf[:, sl], start=False, stop=False), r=[R('qf'), R('Sf')], w=[pon])
                    P.pe(lambda e, cc=cc, sl=sl: e.matmul(po_[:, cc * 128:(cc + 1) * 128], lhsT=qb[:, sl], rhs=Sb[:, sl], start=False, stop=True), r=[R('qb'), R('Sb')], w=[pon])
                for cc in range(G4):
                    P.act(lambda e, cc=cc: e.activation(out=jk[:], in_=po_[:, cc * 128:(cc + 1) * 128], func=AF.Square, accum_out=rs_[:, 0, cc:cc + 1]), r=[pon], w=['jk', rsn + 'a'])
                P.act(lambda e: e.activation(out=rs_[:, 1, :], in_=rs_[:, 0, :], func=AF.Ln, scale=1.0 / 128, bias=EPS), r=[rsn + 'a'], w=[rsn + 'b'])
                P.act(lambda e: e.activation(out=rs_[:, 2, :], in_=rs_[:, 1, :], func=AF.Exp, scale=-0.5), r=[rsn + 'b'], w=[rsn + 'c'])
                for cc in range(G4):
                    P.act(lambda e, cc=cc: e.activation(out=on_[:, cc, :], in_=po_[:, cc * 128:(cc + 1) * 128], func=AF.Copy, scale=rs_[:, 2, cc:cc + 1]), r=[pon, rsn + 'c'], w=[onn])

            def stZ(g, zr=zr, rg=rg):
                g2 = g % 2
                c4 = g * G4
                on_ = on[g2]
                onn = 'on%d' % g2
                for cc in range(G4):
                    P.pe(lambda e, cc=cc: e.transpose(out=pT2[:, cc * 128:(cc + 1) * 128], in_=on_[:, cc, :], identity=ident), r=[onn, 'cb'], w=['pT2'])
                P.dve(lambda e: e.tensor_tensor(out=zr[:, c4 * 128:(c4 + G4) * 128], in0=pT2[:, 0:G4 * 128], in1=rg[:, c4 * 128:(c4 + G4) * 128], op=ALU.mult), r=['pT2', R('rg')], w=[R('zr')])

            for sst in range(NG + 2):
                if sst < NG:
                    stX(sst)
                if 0 <= sst - 1 < NG:
                    stY(sst - 1)
                if 0 <= sst - 2 < NG:
                    stZ(sst - 2)
            P.dma(zt[1024 + h * 128:1024 + (h + 1) * 128, :], zr[:], r=[R('zr')], w=[('zT', 8 + h)], q='act')
            stage_lists[('B', h)] = P.end_capture()
        for sst in range(8 + 2):
            lists = []
            if sst < 8:
                lists.append(stage_lists[('A1', sst)])
            if 0 <= sst - 1 < 8:
                lists.append(stage_lists[('A2', sst - 1)])
            if 0 <= sst - 2 < 8:
                lists.append(stage_lists[('B', sst - 2)])
            P.replay(*lists)
        P.flush()

    def phaseM(l, sq, xsrc, xdst, final, precast=None):
        N = sq.N
        KB = min(N, 512)
        NT = KB // 128
        pj = projT[sq.name]
        zt = zT[sq.name]
        gate = P.sb("gate", [128, D], F32)
        zbs = [P.sb("zb%d" % i, [128, 16, KB], BF16) for i in range(2)]
        mergeds = [P.sb("merged%d" % i, [128, 16, KB], BF16) for i in range(2)]
        wu = [P.sb("wu%d" % i, [128, 16, 128], BF16) for i in range(3)]
        wo = [P.sb("wo%d" % i, [128, 16, 512], BF16) for i in range(2)]
        gts = [P.sb("gts%d" % i, [128, 3, KB], BF16) for i in range(3)]
        tqs = [[P.sb("tq%d_%d" % (i, j), [128, KB], F32) for i in range(3)] for j in range(2)]
        xin = [P.sb("xin%d" % i, [128, 512], F32) for i in range(2)]
        tys = [P.sb("ty%d" % i, [128, 512], F32) for i in range(2)]
        if final:
            fng = P.sb("fng", [128, D], F32)
            xfull = [P.sb("xfull%d" % i, [128, D], F32) for i in range(NT)]
            ss = P.sb("ss", [128, NT, 8], F32)
            jk2 = P.sb("jk2", [128, 512], BF16)
            P.dma(fng[:], bc(fng_in), w=['fng'])
        else:
            xo = [P.sb("xo%d" % i, [128, 512], F32) for i in range(2)]
        pybs = [[P.ps("pyb%d_%d" % (i, j), [128, 512]) for i in range(3)] for j in range(2)]
        pout = [P.ps("pout%d" % i, [128, 512]) for i in range(2)]
        P.dma(gate[:], bc(modrows[l, 2, sq.who]), w=['gate'])
        gsrc = pj[MG:INW].rearrange("(b c p) t -> p b c t", b=3, c=16)
        cn = dict(ku=0, ko=0, kx=0, kp=0)
        NB = N // KB

        def emitU(bi, dcs):
            tb = bi * KB
            zbi = bi % 2
            zb = zbs[zbi]
            zbn = 'zb%d' % zbi
            mg = mergeds[bi % 2]
            for dc in dcs:
                ku = cn['ku']
                cn['ku'] += 1
                w_ = wu[ku % 3]
                wr = 'wu%d' % (ku % 3)
                gt = gts[ku % 3]
                gr = 'gts%d' % (ku % 3)
                pyb = pybs[ku % 2]
                tq = tqs[ku % 2]
                pj_ = ku % 2
                while cn.get('kl', 0) <= min(ku + 2, NB * 16 - 1):
                    kl = cn.get('kl', 0)
                    cn['kl'] = kl + 1
                    lb, ldc = kl // 16, kl % 16
                    P.dma(wu[kl % 3][:], wub[l, ldc], w=['wu%d' % (kl % 3)])
                    P.dma(gts[kl % 3][:], gsrc[:, :, ldc, lb * KB:(lb + 1) * KB], w=['gts%d' % (kl % 3)])
                for br, (k0, k1) in enumerate(((0, 4), (4, 8), (8, 16))):
                    for kc in range(k0, k1):
                        P.pe(lambda e, br=br, kc=kc, k0=k0, k1=k1, w_=w_, pyb=pyb, zb=zb: e.matmul(pyb[br][:, 0:KB], lhsT=w_[:, kc, :], rhs=zb[:, kc, :], start=(kc == k0), stop=(kc == k1 - 1)),
                             r=[wr, zbn], w=['pyb%d_%d' % (br, pj_)])
                for br in range(3):
                    P.dve(lambda e, br=br, gt=gt, pyb=pyb, tq=tq: e.tensor_tensor(out=tq[br][:], in0=pyb[br][:, 0:KB], in1=gt[:, br, :], op=ALU.mult), r=['pyb%d_%d' % (br, pj_), gr], w=['tq%d_%d' % (br, pj_)])
                P.pool(lambda e, tq=tq: e.tensor_tensor(out=tq[0][:], in0=tq[0][:], in1=tq[1][:], op=ALU.add), r=['tq0_%d' % pj_, 'tq1_%d' % pj_], w=['tq0_%d' % pj_])
                P.pool(lambda e, dc=dc, tq=tq, mg=mg: e.tensor_tensor(out=mg[:, dc, :], in0=tq[0][:], in1=tq[2][:], op=ALU.add), r=['tq0_%d' % pj_, 'tq2_%d' % pj_], w=[('merged', bi % 2, dc)])

        def emitO(bi):
            tb = bi * KB
            mg = mergeds[bi % 2]
            mgr = [('merged', bi % 2, d_) for d_ in range(16)]
            for cg in range(4):
                ko = cn['ko']
                cn['ko'] += 1
                wo_ = wo[ko % 2]
                wor = 'wo%d' % (ko % 2)
                if cg < 3:
                    P.dma(wo[(ko + 1) % 2][:], wob[l, cg + 1], w=['wo%d' % ((ko + 1) % 2)])
                for ti in range(NT):
                    t0 = tb + ti * 128
                    kp, kx = cn['kp'], cn['kx']
                    cn['kp'] += 1
                    cn['kx'] += 1
                    pp = pout[kp % 2]
                    pr = 'pout%d' % (kp % 2)
                    xi = xin[kx % 2]
                    xr = 'xin%d' % (kx % 2)
                    P.dma(xi[:], xsrc[t0:t0 + 128, cg * 512:(cg + 1) * 512], w=[xr])
                    for kc in range(16):
                        P.pe(lambda e, pp=pp, kc=kc, ti=ti, wo_=wo_, mg=mg: e.matmul(pp[:], lhsT=mg[:, kc, ti * 128:(ti + 1) * 128], rhs=wo_[:, kc, :], start=(kc == 0), stop=(kc == 15)),
                             r=mgr + [wor], w=[pr])
                    ty = tys[kx % 2]
                    tyn = 'ty%d' % (kx % 2)
                    P.dve(lambda e, pp=pp, cg=cg, ty=ty: e.tensor_tensor(out=ty[:], in0=pp[:], in1=gate[:, cg * 512:(cg + 1) * 512], op=ALU.mult), r=[pr, 'gate'], w=[tyn])
                    if final:
                        xf = xfull[ti]
                        P.pool(lambda e, xf=xf, xi=xi, cg=cg, ty=ty: e.tensor_tensor(out=xf[:, cg * 512:(cg + 1) * 512], in0=ty[:], in1=xi[:], op=ALU.add), r=[tyn, xr], w=[('xfull', ti)])
                        P.act(lambda e, xf=xf, cg=cg, ti=ti: e.activation(out=jk2[:], in_=xf[:, cg * 512:(cg + 1) * 512], func=AF.Square, accum_out=ss[:, ti, cg:cg + 1]), r=[('xfull', ti)], w=['jk2', ('ss', ti)])
                    else:
                        xo_ = xo[kx % 2]
                        xor_ = 'xo%d' % (kx % 2)
                        P.pool(lambda e, xo_=xo_, xi=xi, ty=ty: e.tensor_tensor(out=xo_[:], in0=ty[:], in1=xi[:], op=ALU.add), r=[tyn, xr], w=[xor_])
                        P.dma(xdst[t0:t0 + 128, cg * 512:(cg + 1) * 512], xo_[:], r=[xor_], w=[('xdst', t0, cg)], q='act')
            if final:
                for ti in range(NT):
                    t0 = tb + ti * 128
                    xf = xfull[ti]
                    P.dve(lambda e, ti=ti: e.tensor_tensor(out=ss[:, ti, 4:5], in0=ss[:, ti, 0:1], in1=ss[:, ti, 1:2], op=ALU.add), r=[('ss', ti)], w=[('ss', ti)])
                    P.dve(lambda e, ti=ti: e.tensor_tensor(out=ss[:, ti, 5:6], in0=ss[:, ti, 2:3], in1=ss[:, ti, 3:4], op=ALU.add), r=[('ss', ti)], w=[('ss', ti)])
                    P.dve(lambda e, ti=ti: e.tensor_tensor(out=ss[:, ti, 6:7], in0=ss[:, ti, 4:5], in1=ss[:, ti, 5:6], op=ALU.add), r=[('ss', ti)], w=[('ss', ti)])
                    P.act(lambda e, ti=ti: e.activation(out=ss[:, ti, 7:8], in_=ss[:, ti, 6:7], func=AF.Ln, scale=1.0 / D, bias=EPS), r=[('ss', ti)], w=[('ss', ti)])
                    P.act(lambda e, ti=ti: e.activation(out=ss[:, ti, 6:7], in_=ss[:, ti, 7:8], func=AF.Exp, scale=-0.5), r=[('ss', ti)], w=[('ss', ti)])
                    P.dve(lambda e, ti=ti, xf=xf: e.scalar_tensor_tensor(out=xf[:], in0=xf[:], scalar=ss[:, ti, 6:7], in1=fng[:], op0=ALU.mult, op1=ALU.mult), r=[('xfull', ti), ('ss', ti), 'fng'], w=[('xfull', ti)])
                    P.dma(xdst[t0:t0 + 128, :], xf[:], r=[('xfull', ti)], w=[('xdst', t0)], q='act')

        def loadz(bi):
            P.dma(zbs[bi % 2][:], zt[:, bi * KB:(bi + 1) * KB].rearrange("(c p) t -> p c t", p=128), w=['zb%d' % (bi % 2)])

        HEAD = 2
        loadz(0)
        P.dma(wo[cn['ko'] % 2][:], wob[l, 0], w=['wo%d' % (cn['ko'] % 2)])
        if NB > 1:
            loadz(1)
        emitU(0, range(16))
        for bi in range(NB):
            if bi + 1 < NB:
                emitU(bi + 1, range(HEAD))
            emitO(bi)
            if bi + 1 < NB:
                P.dma(wo[cn['ko'] % 2][:], wob[l, 0], w=['wo%d' % (cn['ko'] % 2)])
                if bi + 2 < NB:
                    loadz(bi + 2)
                emitU(bi + 1, range(HEAD, 16))
        P.flush()

    allc = list(range(96))
    kvc = list(range(R_K // 128, R_G // 128))
    plan = [prologue]
    ccn = {}
    for l in range(DEPTH):
        last = l == DEPTH - 1
        csrc = ctx_in if l == 0 else xres['c']
        xsrc = x_in if l == 0 else xres['x']
        if not last:
            plan.append(lambda l=l, csrc=csrc, xsrc=xsrc: phaseP(l, seq_x, xsrc, allc, precast=l, pre=(seq_c, csrc)))
            plan.append(lambda l=l: phaseF(l, seq_c))
            def qr_c(l=l):
                phaseQ(l, seq_c, noflush=True)
                phaseR(l, seq_c, False)
            plan.append(qr_c)
            plan.append(lambda l=l, csrc=csrc: phaseM(l, seq_c, csrc, xres['c'], False))
        else:
            plan.append(lambda l=l, csrc=csrc: phaseP(l, seq_c, csrc, kvc))
            plan.append(lambda l=l: phaseR(l, seq_c, True))
        if last:
            plan.append(lambda l=l, xsrc=xsrc: phaseP(l, seq_x, xsrc, allc, precast=l))
        def f1r0(l=l):
            ccn['ab'] = phaseF(l, seq_x, 'f1', noflush=True)
            ccn['st'] = phaseR0x(l)
        plan.append(f1r0)
        def f2q(l=l):
            P.capture()
            phaseF(l, seq_x, 'f2', ccn['ab'], noflush=True)
            Lf = P.end_capture()
            P.capture()
            phaseQ(l, seq_x, noflush=True, ccw=ccn['ab'][0])
            Lq = P.end_capture()
            P.replay(Lf, Lq)
            P.flush()
        plan.append(f2q)
        plan.append(lambda l=l: phaseRx(l, ccn['st']))
        plan.append(lambda l=l, xsrc=xsrc, last=last: phaseM(l, seq_x, xsrc, out if last else xres['x'], last))
    for ph in plan[:PHASE_LIMIT]:
        ph()
    if dbg:
        srcs = dict(projT_c=projT['c'], zT_c=zT['c'], xres_c=xres['c'], modrows=modrows, masks_d=masks_d, qdec_d=qdec_d,
                    projT_x=projT['x'], zT_x=zT['x'], xres_x=xres['x'])
        for k, ap in dbg.items():
            P.dma(ap, srcs[k], w=[('dbg', k)])
        P.flush()
    P.finish()
    return nc


def _blk(w, ncols):
    K, C = w.shape
    return np.ascontiguousarray(w.reshape(16, 128, C // ncols, ncols).transpose(2, 1, 0, 3))


def make_in_maps(x, c, ctx, c_ctx, w_ada, b_ada, norm_g, w_in, w_fourier, w_pool, pool_scale,
                 ret_decay_logit, w_up_fourier, w_up_pool, w_up_ret, w_out, final_norm_g):
    f = np.float32
    x = np.asarray(x, f); c = np.asarray(c, f); ctx = np.asarray(ctx, f); c_ctx = np.asarray(c_ctx, f)
    C = _consts()
    shared = {}
    wada_blk = np.stack([_blk(np.asarray(w_ada[l], f), 512) for l in range(DEPTH)])
    b_ada = np.asarray(b_ada, f)
    shared["norm_g"] = np.ascontiguousarray(np.asarray(norm_g, f))
    shared["w_in"] = np.stack([_blk(np.asarray(w_in[l], f), 512) for l in range(DEPTH)])
    shared["w_fourier"] = np.ascontiguousarray(np.asarray(w_fourier, f).transpose(0, 2, 1, 3))
    shared["w_pool"] = np.ascontiguousarray(np.asarray(w_pool, f).transpose(0, 2, 1, 3))
    shared["pool_scale"] = np.ascontiguousarray(np.asarray(pool_scale, f).reshape(DEPTH, 4, 128).transpose(0, 2, 1))
    shared["ret_decay_logit"] = np.ascontiguousarray(np.asarray(ret_decay_logit, f).reshape(32))
    wup = [np.concatenate([np.asarray(w_up_fourier[l], f), np.asarray(w_up_pool[l], f), np.asarray(w_up_ret[l], f)], axis=0) for l in range(DEPTH)]
    shared["w_up"] = np.stack([_blk(w, 128) for w in wup])
    shared["w_out"] = np.stack([_blk(np.asarray(w_out[l], f), 512) for l in range(DEPTH)])
    shared["final_norm_g"] = np.ascontiguousarray(np.asarray(final_norm_g, f))
    shared["cf"] = C['cf']
    shared["cb"] = C['cb']
    shared["invc_c"] = C['invc_c']
    shared["dftc_c"] = C['dftc_c']
    shared["dfts_c"] = C['dfts_c']
    half = []
    for r in range(2):
        sl = slice(r * NH, (r + 1) * NH)
        role = np.zeros((128, 4), np.float32)
        role[:, r] = 1.0
        half.append(dict(w_ada=np.ascontiguousarray(wada_blk[:, r * 6:(r + 1) * 6]),
                         b_ada=np.ascontiguousarray(b_ada[:, r * 3072:(r + 1) * 3072].reshape(-1)),
                         ropec=np.ascontiguousarray(C['ropec'][:, sl]), ropes=np.ascontiguousarray(C['ropes'][:, sl]),
                         invc_x=np.ascontiguousarray(C['invc_x'][:, sl]), dftc_x=np.ascontiguousarray(C['dftc_x'][:, sl]),
                         dfts_x=np.ascontiguousarray(C['dfts_x'][:, sl]), role=role))
    maps = []
    for core in range(NCORES):
        b, r = core // 2, core % 2
        m = dict(shared)
        m.update(half[r])
        m["x"] = np.ascontiguousarray(x[b, r * NH:(r + 1) * NH])
        m["ctx"] = np.ascontiguousarray(ctx[b])
        cc = np.stack([c[b], c_ctx], axis=-1)
        m["cT"] = np.ascontiguousarray(cc.reshape(16, 128, 2).transpose(1, 0, 2))
        maps.append(m)
    return maps


_NC = None


def kernel(**inputs):
    global _NC
    maps = make_in_maps(**inputs)
    if _NC is None:
        _NC = build()
    res = run_bass_kernel_spmd(_NC, maps, core_ids=list(range(NCORES)))
    outs = [np.asarray(r["out"], np.float32) for r in res.results]
    return np.stack([np.concatenate([outs[2 * b], outs[2 * b + 1]], axis=0) for b in range(NCORES // 2)], axis=0)
```
